# Optimizing a Trainium2 kernel written in Bass

```python
import math
import jax, jax.numpy as jnp
from jax import lax
import numpy as np

D_MODEL = 1024
BATCH = 32
SEQ = 256
DEPTH = 2
DEC_BATCH = 8
DEC_SEQ = 2048
PAST_LEN = 256

GRID_W = 64
N_HEADS = 8
N_KV_HEADS = 2
HEAD_DIM = 64
GROUP = N_HEADS // N_KV_HEADS
ATTN_W = N_HEADS * HEAD_DIM
KV_W = N_KV_HEADS * HEAD_DIM
WINDOW = 128
BLOCK = 128
CONV_CH = D_MODEL // 4
CONV_K = 31
FNET_GROUPS = 4
FNET_CH = D_MODEL // 4
FNET_GC = FNET_CH // FNET_GROUPS
N_BRANCH = 3
N_IN = ATTN_W + 2 * KV_W + 2 * CONV_CH + FNET_CH + N_BRANCH * D_MODEL
D_FF = int(math.ceil(8 * D_MODEL / 3 / 128)) * 128
FFN_K = 3
ROPE_THETA = 10000.0
ALPHA = (2 * DEPTH) ** 0.25
BETA = (8 * DEPTH) ** -0.25
NEG_INF = -1e30
LN_EPS = 1e-6
SPLITS = (ATTN_W, ATTN_W + KV_W, ATTN_W + 2 * KV_W, ATTN_W + 2 * KV_W + 2 * CONV_CH,
          ATTN_W + 2 * KV_W + 2 * CONV_CH + FNET_CH)

kernel_name = "hybrid_diffusion_prefix_step"


def layer_norm(x, g=None, b=None):
    xf = x.astype(jnp.float32)
    mu = jnp.mean(xf, axis=-1, keepdims=True)
    var = jnp.mean(jnp.square(xf - mu), axis=-1, keepdims=True)
    y = (xf - mu) * lax.rsqrt(var + LN_EPS)
    if g is not None:
        y = y * g.astype(jnp.float32) + b.astype(jnp.float32)
    return y.astype(x.dtype)


def dwconv(x, w, b):
    k = w.shape[0]
    pad = k // 2
    y = lax.conv_general_dilated(x, w[:, None, :].astype(x.dtype), window_strides=(1,),
                                 padding=[(pad, pad)], dimension_numbers=("NWC", "WIO", "NWC"),
                                 feature_group_count=x.shape[-1])
    return y + b


def axial_rope(length):
    n_rows = length // GRID_W
    row = jnp.repeat(jnp.arange(n_rows, dtype=jnp.float32), GRID_W)
    col = jnp.tile(jnp.arange(GRID_W, dtype=jnp.float32), n_rows)
    quarter = HEAD_DIM // 4
    inv = ROPE_THETA ** (-jnp.arange(quarter, dtype=jnp.float32) / quarter)
    ang = jnp.concatenate([row[:, None] * inv, col[:, None] * inv], axis=-1)
    return jnp.cos(ang), jnp.sin(ang)


def apply_rope(x, cos, sin):
    c = cos[None, :, None, :].astype(x.dtype)
    s = sin[None, :, None, :].astype(x.dtype)
    x1, x2 = jnp.split(x, 2, axis=-1)
    return jnp.concatenate([x1 * c - x2 * s, x2 * c + x1 * s], axis=-1)


def window_attn(q, k, v, kc, vc, sink):
    bsz, length = q.shape[:2]
    nb = length // BLOCK
    lc = kc.shape[1]
    qb = q.reshape(bsz, nb, BLOCK, N_KV_HEADS, GROUP, HEAD_DIM) * (HEAD_DIM ** -0.5)
    padw = ((0, 0), (BLOCK, BLOCK), (0, 0), (0, 0))
    kp = jnp.pad(k, padw).reshape(bsz, nb + 2, BLOCK, N_KV_HEADS, HEAD_DIM)
    vp = jnp.pad(v, padw).reshape(bsz, nb + 2, BLOCK, N_KV_HEADS, HEAD_DIM)
    kwin = jnp.concatenate([kp[:, :-2], kp[:, 1:-1], kp[:, 2:]], axis=2)
    vwin = jnp.concatenate([vp[:, :-2], vp[:, 1:-1], vp[:, 2:]], axis=2)
    s_loc = jnp.einsum("bnqkgd,bnskd->bnkgqs", qb, kwin).astype(jnp.float32)
    blk = jnp.arange(nb)[:, None, None] * BLOCK
    qpos = blk + jnp.arange(BLOCK)[None, :, None]
    kpos = blk - BLOCK + jnp.arange(3 * BLOCK)[None, None, :]
    valid = (jnp.abs(qpos - kpos) <= WINDOW) & (kpos >= 0) & (kpos < length)
    s_loc = jnp.where(valid[None, :, None, None], s_loc, NEG_INF)
    s_ctx = jnp.einsum("bnqkgd,bskd->bnkgqs", qb, kc).astype(jnp.float32)
    s_sink = jnp.broadcast_to(sink.astype(jnp.float32).reshape(1, 1, N_KV_HEADS, GROUP, 1, 1),
                              s_loc.shape[:-1] + (1,))
    p = jax.nn.softmax(jnp.concatenate([s_loc, s_ctx, s_sink], axis=-1), axis=-1)
    p_loc = p[..., :3 * BLOCK].astype(v.dtype)
    p_ctx = p[..., 3 * BLOCK:3 * BLOCK + lc].astype(vc.dtype)
    o = (jnp.einsum("bnkgqs,bnskd->bnqkgd", p_loc, vwin)
         + jnp.einsum("bnkgqs,bskd->bnqkgd", p_ctx, vc))
    return o.reshape(bsz, length, ATTN_W)


def ctx_attn(q, kc, vc, sink):
    bsz, lc = q.shape[:2]
    nb = lc // BLOCK
    qb = jnp.moveaxis(q.reshape(bsz, nb, BLOCK, N_KV_HEADS, GROUP, HEAD_DIM), 1, 0)
    sink_f = sink.astype(jnp.float32).reshape(1, N_KV_HEADS, GROUP, 1, 1)

    def one_block(qi):
        s = jnp.einsum("bqkgd,bskd->bkgqs", qi * (HEAD_DIM ** -0.5), kc).astype(jnp.float32)
        s_sink = jnp.broadcast_to(sink_f, s.shape[:-1] + (1,))
        p = jax.nn.softmax(jnp.concatenate([s, s_sink], axis=-1), axis=-1)
        return jnp.einsum("bkgqs,bskd->bqkgd", p[..., :lc].astype(vc.dtype), vc)

    o = lax.map(one_block, qb)
    return jnp.moveaxis(o, 0, 1).reshape(bsz, lc, ATTN_W)


def fourier_mix(f):
    bsz, length = f.shape[:2]
    fg = f.reshape(bsz, length, FNET_GROUPS, FNET_GC).astype(jnp.float32)
    y = jnp.real(jnp.fft.fft2(fg, axes=(1, 3), norm="ortho"))
    return y.reshape(bsz, length, FNET_CH).astype(f.dtype)


def layer(x, mod, p, latent, kc, vc):
    sh1, sc1, g1, sh2, sc2, g2 = [m[:, None, :] for m in jnp.split(mod, 6, axis=-1)]
    bsz, length = x.shape[:2]
    h = layer_norm(x) * (1 + sc1) + sh1
    z = h @ p["w_in"] + p["b_in"]
    q, k, v, conv_in, f_in, gates = jnp.split(z, SPLITS, axis=-1)
    q = q.reshape(bsz, length, N_HEADS, HEAD_DIM)
    k = k.reshape(bsz, length, N_KV_HEADS, HEAD_DIM)
    v = v.reshape(bsz, length, N_KV_HEADS, HEAD_DIM)
    if latent:
        cos, sin = axial_rope(length)
        attn = window_attn(apply_rope(q, cos, sin), apply_rope(k, cos, sin), v, kc, vc, p["sink"])
    else:
        attn = ctx_attn(q, k, v, p["sink"])
    a_attn = attn @ p["w_attn_o"]
    ua, ub = jnp.split(conv_in, 2, axis=-1)
    u = dwconv(ua * jax.nn.sigmoid(ub), p["conv_w"], p["conv_b"])
    u = jax.nn.silu(layer_norm(u, p["conv_ln_g"], p["conv_ln_b"]))
    a_conv = u @ p["w_conv_o"]
    a_f = fourier_mix(f_in) @ p["w_fnet"] + p["b_fnet"]
    ga, gc, gf = jnp.split(jax.nn.sigmoid(gates), N_BRANCH, axis=-1)
    merged = ga * a_attn + gc * a_conv + gf * a_f
    x = layer_norm(ALPHA * x + g1 * (merged @ p["w_o"]), p["ln1_g"], p["ln1_b"])
    h = layer_norm(x) * (1 + sc2) + sh2
    u = dwconv(h @ p["w_up"], p["ffn_conv_w"], p["ffn_conv_b"])
    ug, uv = jnp.split(u, 2, axis=-1)
    x = layer_norm(ALPHA * x + g2 * ((jax.nn.silu(ug) * uv) @ p["w_down"]), p["ln2_g"], p["ln2_b"])
    return x, k, v


def setup_inputs(seed: int = 0) -> dict:
    key = jax.random.key(seed)
    ks = jax.random.split(key, 32)

    def nrm(k, shape, scale):
        return jax.random.normal(k, shape, jnp.float32) * scale

    L = DEPTH
    return {
        "x_prompt": nrm(ks[0], (BATCH, SEQ, D_MODEL), 1.0),
        "x_sample": nrm(ks[1], (DEC_BATCH, DEC_SEQ, D_MODEL), 1.0),
        "cache_k": nrm(ks[2], (DEC_BATCH, DEPTH, PAST_LEN, N_KV_HEADS, HEAD_DIM), 1.0),
        "cache_v": nrm(ks[3], (DEC_BATCH, DEPTH, PAST_LEN, N_KV_HEADS, HEAD_DIM), 1.0),
        "c": nrm(ks[4], (DEC_BATCH, D_MODEL), 1.0),
        "c_ctx": nrm(ks[5], (D_MODEL,), 1.0),
        "w_mod": nrm(ks[6], (L, D_MODEL, 6 * D_MODEL), 0.5 * D_MODEL ** -0.5),
        "b_mod": nrm(ks[7], (L, 6 * D_MODEL), 0.02),
        "w_in": nrm(ks[8], (L, D_MODEL, N_IN), D_MODEL ** -0.5),
        "b_in": nrm(ks[9], (L, N_IN), 0.02),
        "sink": nrm(ks[10], (L, N_HEADS), 0.5),
        "w_attn_o": nrm(ks[11], (L, ATTN_W, D_MODEL), ATTN_W ** -0.5),
        "conv_w": nrm(ks[12], (L, CONV_K, CONV_CH), CONV_K ** -0.5),
        "conv_b": nrm(ks[13], (L, CONV_CH), 0.02),
        "conv_ln_g": 1.0 + nrm(ks[14], (L, CONV_CH), 0.02),
        "conv_ln_b": nrm(ks[15], (L, CONV_CH), 0.02),
        "w_conv_o": nrm(ks[16], (L, CONV_CH, D_MODEL), CONV_CH ** -0.5),
        "w_fnet": nrm(ks[17], (L, FNET_CH, D_MODEL), FNET_CH ** -0.5),
        "b_fnet": nrm(ks[18], (L, D_MODEL), 0.02),
        "w_o": nrm(ks[19], (L, D_MODEL, D_MODEL), BETA * D_MODEL ** -0.5),
        "ln1_g": 1.0 + nrm(ks[20], (L, D_MODEL), 0.02),
        "ln1_b": nrm(ks[21], (L, D_MODEL), 0.02),
        "w_up": nrm(ks[22], (L, D_MODEL, 2 * D_FF), D_MODEL ** -0.5),
        "ffn_conv_w": nrm(ks[23], (L, FFN_K, 2 * D_FF), FFN_K ** -0.5),
        "ffn_conv_b": nrm(ks[24], (L, 2 * D_FF), 0.02),
        "w_down": nrm(ks[25], (L, D_FF, D_MODEL), BETA * D_FF ** -0.5),
        "ln2_g": 1.0 + nrm(ks[26], (L, D_MODEL), 0.02),
        "ln2_b": nrm(ks[27], (L, D_MODEL), 0.02),
    }


def reference(x_prompt, x_sample, cache_k, cache_v, c, c_ctx, w_mod, b_mod, w_in, b_in, sink,
              w_attn_o, conv_w, conv_b, conv_ln_g, conv_ln_b, w_conv_o, w_fnet, b_fnet, w_o,
              ln1_g, ln1_b, w_up, ffn_conv_w, ffn_conv_b, w_down, ln2_g, ln2_b):
    xp = x_prompt
    xs = x_sample
    new_k = []
    new_v = []
    for l in range(DEPTH):
        p = {
            "w_in": w_in[l], "b_in": b_in[l], "sink": sink[l], "w_attn_o": w_attn_o[l],
            "conv_w": conv_w[l], "conv_b": conv_b[l], "conv_ln_g": conv_ln_g[l],
            "conv_ln_b": conv_ln_b[l], "w_conv_o": w_conv_o[l], "w_fnet": w_fnet[l],
            "b_fnet": b_fnet[l], "w_o": w_o[l], "ln1_g": ln1_g[l], "ln1_b": ln1_b[l],
            "w_up": w_up[l], "ffn_conv_w": ffn_conv_w[l], "ffn_conv_b": ffn_conv_b[l],
            "w_down": w_down[l], "ln2_g": ln2_g[l], "ln2_b": ln2_b[l],
        }
        mod_ctx = (jax.nn.silu(c_ctx) @ w_mod[l] + b_mod[l])[None, :]
        mod_lat = jax.nn.silu(c) @ w_mod[l] + b_mod[l]
        xp, kc_new, vc_new = layer(xp, mod_ctx, p, False, None, None)
        new_k.append(kc_new)
        new_v.append(vc_new)
        xs, _, _ = layer(xs, mod_lat, p, True, cache_k[:, l], cache_v[:, l])
    new_cache_k = jnp.stack(new_k, axis=1)
    new_cache_v = jnp.stack(new_v, axis=1)
    return (xp, xs, new_cache_k, new_cache_v)
```

```python
import numpy as np
import ml_dtypes
import concourse.bass as bass
import concourse.mybir as mybir
from concourse.bass_utils import run_bass_kernel_spmd

F32 = mybir.dt.float32
BF16 = mybir.dt.bfloat16
AF = mybir.ActivationFunctionType
ALU = mybir.AluOpType

D = 1024
DEPTH = 2
NIN = 4608
DFF = 2816
NFF = 22
ALPHA = float((2 * DEPTH) ** 0.25)
EPS = 1e-6
FFN_GROUPS = [(0, 6), (6, 6), (12, 5), (17, 5)]
NEG = -30000.0
import os
P0_LEVEL = int(os.environ.get('P0_LEVEL', '9'))
EVAC_MODE = int(os.environ.get('EVAC_MODE', '0'))


class Eng:
    def __init__(self, name, e, sem):
        self.name, self.e, self.sem = name, e, sem
        self.count = 0
        self.seen = {}


class Cell:
    __slots__ = ("w", "r")

    def __init__(self):
        self.w = None
        self.r = {}


def cells(*dims):
    if len(dims) == 1:
        return [Cell() for _ in range(dims[0])]
    return [cells(*dims[1:]) for _ in range(dims[0])]


class KB:
    def __init__(self, nc):
        self.nc = nc
        self.engs = {}
        self.sems = {}
        for name, e in (("pe", nc.tensor), ("act", nc.scalar), ("dve", nc.vector),
                        ("pool", nc.gpsimd), ("sp", nc.sync)):
            sem = nc.semaphore("s_" + name).__enter__()
            self.engs[name] = Eng(name, e, sem)
            self.sems[name] = sem
        self.dcount = {}
        for key in ["const", "stg", "stg1", "stg2", "w0", "w1", "w2", "g0", "g1", "dbg"] + [f"x{i}" for i in range(8)]:
            self.sems[key] = nc.semaphore("d_" + key).__enter__()
            self.dcount[key] = 0
        self.banks = [nc.psum_tensor(f"ps{i}", [128, 512], F32).__enter__() for i in range(8)]
        self.bcells = cells(8)
        self.bi = 0
        self.reserved = set()
        self.out_stamps = []

    def bank(self):
        while self.bi in self.reserved:
            self.bi = (self.bi + 1) % 8
        i = self.bi
        self.bi = (self.bi + 1) % 8
        return self.banks[i], self.bcells[i]

    def reserve(self):
        b, c = self.bank()
        i = self.banks.index(b)
        self.reserved.add(i)
        return b, c, i

    def _wait(self, eng, key, val, war=False):
        if key == eng.name and (war or eng.name in ("pe", "sp")):
            return
        if eng.seen.get(key, 0) >= val:
            return
        have = self.engs[key].count if key in self.engs else self.dcount[key]
        assert val <= have, ("waiting on un-emitted signal", eng.name, key, val, have)
        eng.e.wait_ge(self.sems[key], val)
        eng.seen[key] = val

    def _deps(self, eng, reads, writes):
        for c in reads:
            if c.w is not None:
                self._wait(eng, *c.w)
        for c in writes:
            if c.w is not None:
                self._wait(eng, *c.w)
            for k, v in c.r.items():
                self._wait(eng, k, v, war=True)

    def _stamp(self, stamp, reads, writes):
        for c in reads:
            if c.r.get(stamp[0], 0) < stamp[1]:
                c.r[stamp[0]] = stamp[1]
        for c in writes:
            c.w = stamp
            c.r = {}

    def op(self, en, fn, reads=(), writes=(), signal=True):
        eng = self.engs[en]
        self._deps(eng, reads, writes)
        ins = fn(eng.e)
        if signal:
            eng.count += 1
            ins.then_inc(eng.sem, 1)
            stamp = (en, eng.count)
        else:
            stamp = (en, eng.count + 1)
        self._stamp(stamp, reads, writes)
        return ins

    def dma(self, q, out, in_, key, reads=(), writes=(), **kw):
        eng = self.engs[q]
        assert key in self.sems, key
        self._deps(eng, reads, writes)
        ins = eng.e.dma_start(out=out, in_=in_, **kw)
        self.dcount[key] += 16
        ins.then_inc(self.sems[key], 16)
        stamp = (key, self.dcount[key])
        self._stamp(stamp, reads, writes)
        return stamp

    def barrier(self):
        for en, eng in self.engs.items():
            for fn, f in self.engs.items():
                if f.count > 0:
                    self._wait(eng, fn, f.count)
            for key, cnt in self.dcount.items():
                if cnt > 0:
                    self._wait(eng, key, cnt)


def carve(R, off, shape, dt=BF16):
    n = int(np.prod(shape[1:]))
    nb = n * 2 if dt == F32 else n
    assert off % 2 == 0 and off + nb <= R.shape[1], (off, nb, R.shape)
    v = R[:, off:off + nb]
    if dt == F32:
        v = v.bitcast(F32)
    if len(shape) == 3:
        v = v.rearrange("p (a b) -> p a b", a=shape[1], b=shape[2])
    elif len(shape) == 4:
        v = v.rearrange("p (a b c) -> p a b c", a=shape[1], b=shape[2], c=shape[3])
    return v


class Seg:
    def __init__(self, name, T, nseq, latent, g, xin, yout):
        self.name, self.T, self.nseq, self.latent, self.g = name, T, nseq, latent, g
        self.L = T // nseq
        self.NT = T // 128
        self.NN = T // 512
        self.xin, self.yout = xin, yout
        self.gpad = self.L + 30
        self.upad = self.L + 2


class StopBuild(Exception):
    pass


def build(debug=None, stop=None, segs="PS"):
    nc = bass.Bass("TRN2", target_bir_lowering=False)
    dt_in = lambda name, shape, dt=F32: nc.dram_tensor(name, list(shape), dt, kind="ExternalInput").ap()
    dt_out = lambda name, shape, dt=F32: nc.dram_tensor(name, list(shape), dt, kind="ExternalOutput").ap()
    xp = dt_in("xp", [1024, D]); xs = dt_in("xs", [2048, D])
    ck = dt_in("ck", [DEPTH, 256, 128]); cv = dt_in("cv", [DEPTH, 256, 128])
    cvec = dt_in("cvec", [2, D])
    w_mod = dt_in("w_mod", [DEPTH, D, 6 * D]); b_mod = dt_in("b_mod", [DEPTH, 6 * D])
    w_in = dt_in("w_in", [DEPTH, D, NIN]); b_in = dt_in("b_in", [DEPTH, NIN])
    sink = dt_in("sink", [DEPTH, 8])
    w_attn_o = dt_in("w_attn_o", [DEPTH, 512, D])
    conv_w = dt_in("conv_w", [DEPTH, 31, 256]); conv_b = dt_in("conv_b", [DEPTH, 256])
    conv_ln_g = dt_in("conv_ln_g", [DEPTH, 256]); conv_ln_b = dt_in("conv_ln_b", [DEPTH, 256])
    w_conv_o = dt_in("w_conv_o", [DEPTH, 256, D]); w_fnet = dt_in("w_fnet", [DEPTH, 256, D])
    b_fnet = dt_in("b_fnet", [DEPTH, D]); w_o = dt_in("w_o", [DEPTH, D, D])
    ln1_g = dt_in("ln1_g", [DEPTH, D]); ln1_b = dt_in("ln1_b", [DEPTH, D])
    w_up = dt_in("w_up", [DEPTH, D, 2 * DFF]); ffn_conv_w = dt_in("ffn_conv_w", [DEPTH, 3, 2 * DFF])
    ffn_conv_b = dt_in("ffn_conv_b", [DEPTH, 2 * DFF]); w_down = dt_in("w_down", [DEPTH, DFF, D])
    ln2_g = dt_in("ln2_g", [DEPTH, D]); ln2_b = dt_in("ln2_b", [DEPTH, D])
    c_identb = dt_in("c_identb", [128, 128], BF16); c_identf = dt_in("c_identf", [128, 128])
    c_mask = dt_in("c_mask", [2, 128, 128], BF16)
    c_mask01 = dt_in("c_mask01", [128, 384], BF16)
    c_rope = dt_in("c_rope", [2, 128, 16, 32])
    c_dft256 = dt_in("c_dft256", [2, 256, 256], BF16)
    c_dft2k = dt_in("c_dft2k", [4, 4, 128, 2 * 4 * 512], BF16)
    c_bcs = dt_in("c_bcs", [4, 128, 128], BF16)
    yp = dt_out("yp", [1024, D]); ys = dt_out("ys", [2048, D])
    nk = dt_out("nk", [4, DEPTH, 256, 128]); nv = dt_out("nv", [4, DEPTH, 256, 128])
    sc = lambda name, shape: nc.dram_tensor(name, list(shape), BF16, kind="Internal").ap()
    scA = sc("scA", [DEPTH, 128, 8192]); scB = sc("scB", [DEPTH, 1, 1024]); scC = sc("scC", [DEPTH, 128, 4096])
    scG = sc("scG", [DEPTH, 8, 128, 4096]); scU = sc("scU", [DEPTH, NFF, 128, 2048])
    dbg = {}
    if debug:
        for name, shape in debug.items():
            dbg[name] = dt_out("dbg_" + name, shape)

    kb = KB(nc)
    sb = lambda name, shape, dt=BF16: nc.sbuf_tensor(name, list(shape), dt).__enter__()
    X = sb("X", [128, 16, D], F32)
    HT = sb("HT", [128, 8, 2048])
    RA = sb("RA", [128, 21504])
    RB = sb("RB", [128, 12288])
    RW = sb("RW", [128, 14336])
    identb = sb("identb", [128, 128]); identf = sb("identf", [128, 128], F32)
    maskc = sb("maskc", [128, 2, 128])
    mask01 = sb("mask01", [128, 384])
    onesb = sb("onesb", [128, 128]); onesf = sb("onesf", [128, 128], F32); onesd = sb("onesd", [128, 128], F32)
    rope = sb("rope", [128, 2, 16, 32], F32)
    dft256 = sb("dft256", [128, 2, 2, 256])
    bcs = sb("bcs", [128, 4, 128])
    NCV = 328
    CV = sb("CV", [128, DEPTH, NCV], F32)
    MODT = sb("MODT", [128, DEPTH, 48, 2], F32)
    scT = sb("scT", [128, 8, 2])
    esink = sb("esink", [128, DEPTH, 4], F32)
    e8 = sb("e8", [128, 8], F32)
    stg = sb("stg", [128, 128], F32)
    e8r = sb("e8r", [1, DEPTH, 8], F32)
    onesr = sb("onesr", [1, 256])
    c_e8r = Cell()
    dgf = sb("dgf", [128, 128], F32)
    c_dgf = Cell()
    stats = sb("stats", [128, 4, 12], F32)
    mv = sb("mv", [128, 4, 4], F32)
    c_const = cells(1)[0]
    c_stg = Cell(); c_cv = Cell(); c_mod = Cell(); c_sct = Cell(); c_es = Cell()
    c_stats = cells(4)
    c_x = cells(16)
    c_ht = cells(8, 16)

    op, dma, bank = kb.op, kb.dma, kb.bank

    dma("sp", identb[:], c_identb, "const", writes=[c_const])
    dma("sp", identf[:], c_identf, "const", writes=[c_const])
    dma("sp", maskc[:], c_mask.rearrange("m p q -> p m q"), "const", writes=[c_const])
    dma("sp", mask01[:], c_mask01, "const", writes=[c_const])
    dma("sp", rope[:], c_rope.rearrange("c p t d -> p c t d"), "const", writes=[c_const])
    dma("sp", dft256[:], c_dft256.rearrange("c (k p) n -> p c k n", p=128), "const", writes=[c_const])
    dma("sp", bcs[:], c_bcs.rearrange("m p q -> p m q"), "const", writes=[c_const])
    op("dve", lambda e: e.memset(onesb[:], 1.0), writes=[c_const])
    op("dve", lambda e: e.memset(onesf[:], 1.0), writes=[c_const])
    op("dve", lambda e: e.memset(onesd[:], 1.0 / 256.0), writes=[c_const])
    epsc = sb("epsc", [128, 2], F32)
    op("dve", lambda e: e.memset(epsc[:], EPS), writes=[c_const])

    stgs = [stg, sb("stg1", [128, 128], F32), sb("stg2", [128, 128], F32)]
    stg_r = cells(3)
    lc_state = {"k": 0}

    def load_cols(dst, items):
        col = 0
        rows = 0
        bufc = []
        def flush():
            nonlocal rows, col, bufc
            if not rows:
                return
            k = lc_state["k"]
            pb, pc = bank()
            op("pe", lambda e: e.transpose(out=pb[:, 0:rows], in_=stgs[k][0:rows, :], identity=identf[0:rows, 0:rows]),
               reads=[stg_r[k], c_const] + bufc, writes=[pc])
            c0 = col
            op("dve", lambda e: e.tensor_copy(out=dst[:, c0:c0 + rows], in_=pb[:, 0:rows]), reads=[pc], writes=[c_cv])
            col += rows
            rows = 0
            bufc = []
            lc_state["k"] = (k + 1) % 3
        for ap, n in items:
            done = 0
            while done < n:
                k = lc_state["k"]
                take = min(n - done, 128 - rows)
                r0 = rows
                kb._deps(kb.engs["sp"], [], [stg_r[k]])
                fc = Cell()
                dma("sp", stgs[k][r0:r0 + take, :], ap[done:done + take, :], ("stg", "stg1", "stg2")[k], writes=[fc])
                bufc.append(fc)
                rows += take
                done += take
                if rows == 128:
                    flush()
        flush()
        return col

    CVOFF = {}
    _o = 0
    for _name, _n in (("bmod", 48), ("bconv", 4), ("bgate", 24), ("convb", 2), ("lng", 2), ("lnb", 2), ("bfnet", 8),
                      ("fcw", 132), ("fcb", 44), ("cw", 62)):
        CVOFF[_name] = _o
        _o += _n
    assert _o == NCV
    dma("sp", stgs[2][0:16, :], cvec.rearrange("g (k p) -> (g k) p", p=128), "stg2", writes=[c_stg])
    pb, pc = bank()
    op("pe", lambda e: e.transpose(out=pb[:, 0:16], in_=stgs[2][0:16, :], identity=identf[0:16, 0:16]),
       reads=[c_stg, c_const, stg_r[2]], writes=[pc])
    op("act", lambda e: e.activation(out=scT[:], in_=pb[:, 0:16].rearrange("p (g k) -> p k g", g=2), func=AF.Silu),
       reads=[pc], writes=[c_sct])
    dma("sp", e8r[:], sink.rearrange("(o l) h -> o l h", o=1), "g1", writes=[c_e8r])
    op("act", lambda e: e.activation(out=e8r[:], in_=e8r[:], func=AF.Exp), reads=[c_e8r], writes=[c_e8r])
    op("dve", lambda e: e.memset(onesr[:], 1.0), writes=[c_const])
    class ModJob:
        def __init__(self, l, slots, keys, pm, pmc):
            self.l, self.slots, self.keys = l, slots, keys
            self.cw = cells(len(slots))
            self.pm, self.pmc = pm, pmc
            self.pmv = pm[:, 0:96].rearrange("p (c g) -> p c g", g=2)
            self.nload = 0
            self.ndone = 0

        def load(self):
            if self.nload >= 12:
                return
            blk = self.nload
            s = blk % len(self.slots)
            dma("pool", self.slots[s], w_mod[self.l, :, blk * 512:(blk + 1) * 512].rearrange("(k p) n -> p k n", p=128),
                self.keys[s], writes=[self.cw[s]])
            self.nload += 1

        def step(self):
            if self.ndone >= 12:
                return
            blk = self.ndone
            while self.nload <= min(blk + len(self.slots) - 1, 11):
                self.load()
            s = blk % len(self.slots)
            for cc in range(4):
                ch = blk * 4 + cc
                for k in range(8):
                    op("pe", lambda e: e.matmul(out=self.pmv[:, ch, :], lhsT=self.slots[s][:, k, cc * 128:(cc + 1) * 128],
                                                rhs=scT[:, k, :], start=(k == 0), stop=(k == 7)),
                       reads=[self.cw[s], c_sct], writes=[self.pmc], signal=(k == 7 and cc == 3))
            self.ndone += 1

        def finish(self):
            while self.ndone < 12:
                self.step()
            l = self.l
            bm = CV[:, l, CVOFF["bmod"]:CVOFF["bmod"] + 48].unsqueeze(2).broadcast_to([128, 48, 2])
            op("dve", lambda e: e.tensor_tensor(out=MODT[:, l, :, :], in0=self.pmv, in1=bm, op=ALU.add),
               reads=[self.pmc, c_cv], writes=[c_mod])
            for base in (8, 32):
                op("dve", lambda e: e.tensor_scalar_add(out=MODT[:, l, base:base + 8, :], in0=MODT[:, l, base:base + 8, :],
                                                        scalar1=1.0), reads=[c_mod], writes=[c_mod])

    wm = [carve(RB, s * 4096, [128, 8, 512]) for s in range(3)]
    mod_late = (DEPTH - 1) if ("S" in segs and stop is None) else None
    jobs = []
    for l in range(DEPTH):
        if l == mod_late:
            continue
        pm, pmc = bank()
        job = ModJob(l, wm, ["w0", "w1", "w2"], pm, pmc)
        for _ in range(12):
            job.step()
        jobs.append(job)
    cvc = lambda l, name, i=0: CV[:, l, CVOFF[name] + i:CVOFF[name] + i + 1]

    def chk(tag):
        if stop == tag:
            raise StopBuild()

    def dump(name, src, reads=()):
        if name in dbg:
            st = dma("pool", dbg[name], src, "dbg", reads=list(reads))
            kb.out_stamps.append(st)

    ring = {"st": 0}
    MV = sb("MV", [128, 2, 16, 4], F32)
    c_mvs = cells(2, 4)
    c_mvt = cells(2, 16)

    c_stats2 = cells(4, 2)

    def stats_block(tiles, si):
        rs = []
        for i in tiles:
            r = ring["st"] % 4
            ring["st"] += 1
            rs.append(r)
            for h in range(2):
                op("dve", lambda e: e.bn_stats(out=stats[:, r, h * 6:(h + 1) * 6], in_=X[:, i, h * 512:(h + 1) * 512]),
                   reads=[c_x[i]], writes=[c_stats2[r][h]])
        for i, r in zip(tiles, rs):
            op("dve", lambda e: e.bn_aggr(out=MV[:, si, i, 0:2], in_=stats[:, r, :]), reads=c_stats2[r], writes=[c_mvt[si][i]])

    def stats_finish(si, i0, i1):
        c = c_mvs[si][i0 // 4]
        op("act", lambda e: e.activation(out=MV[:, si, i0:i1, 2], in_=MV[:, si, i0:i1, 1], func=AF.Sqrt, bias=epsc[:, 0:1]),
           reads=[c_const] + c_mvt[si][i0:i1], writes=[c])
        op("dve", lambda e: e.reciprocal(out=MV[:, si, i0:i1, 2], in_=MV[:, si, i0:i1, 2]), reads=[c], writes=[c])
        op("dve", lambda e: e.scalar_tensor_tensor(out=MV[:, si, i0:i1, 3], in0=MV[:, si, i0:i1, 0], scalar=-1.0,
                                                   in1=MV[:, si, i0:i1, 2], op0=ALU.mult, op1=ALU.mult),
           reads=[c] + c_mvt[si][i0:i1], writes=[c])

    def affine_block(tiles, lg, lb, c_l):
        for i in tiles:
            op("act", lambda e: e.activation(out=X[:, i, :], in_=X[:, i, :], func=AF.Identity, bias=MV[:, 0, i, 3:4],
                                             scale=MV[:, 0, i, 2:3]), reads=[c_mvs[0][i // 4]], writes=[c_x[i]])
        for i in tiles:
            op("dve", lambda e: e.tensor_tensor(out=X[:, i, :], in0=X[:, i, :], in1=lg, op=ALU.mult),
               reads=[c_l], writes=[c_x[i]])
        for i in tiles:
            op("dve", lambda e: e.tensor_tensor(out=X[:, i, :], in0=X[:, i, :], in1=lb, op=ALU.add),
               reads=[c_l], writes=[c_x[i]])

    def ln_pipeline(NT, s1, s2, s3):
        nb = NT // 4
        for step in range(nb + 2):
            if step < nb:
                s1(step)
            if 0 <= step - 1 < nb and s2 is not None:
                s2(step - 1)
            if 0 <= step - 2 < nb and s3 is not None:
                s3(step - 2)

    xnb_state = {"i": 0}

    def ht_block(sg, l, s_shift, xnb, c_xnb, tiles):
        assert len(xnb) >= len(tiles)
        rr = []
        for i in tiles:
            r = xnb_state["i"] % len(xnb)
            xnb_state["i"] += 1
            rr.append(r)
            op("act", lambda e: e.activation(out=xnb[r], in_=X[:, i, :], func=AF.Identity, bias=MV[:, 1, i, 3:4],
                                             scale=MV[:, 1, i, 2:3]), reads=[c_x[i], c_mvs[1][i // 4]], writes=[c_xnb[r]])
        bks = []
        for i, r in zip(tiles, rr):
            pb, pc = bank()
            pbv = pb[:].bitcast(BF16).rearrange("p (k q) -> p k q", k=8)
            bks.append((pbv, pc))
            for k in range(8):
                op("pe", lambda e: e.transpose(out=pbv[:, k, :], in_=xnb[r][:, k * 128:(k + 1) * 128], identity=identb[:]),
                   reads=[c_xnb[r], c_const], writes=[pc], signal=(k == 7))
        for i, (pbv, pc) in zip(tiles, bks):
            for k in range(8):
                dst = HT[:, k, i * 128:(i + 1) * 128]
                sc_ = modc(l, s_shift + 1, k, sg.g)
                sh_ = modc(l, s_shift, k, sg.g)
                if i % 2 == 0:
                    op("act", lambda e: e.activation(out=dst, in_=pbv[:, k, :], func=AF.Identity, bias=sh_, scale=sc_),
                       reads=[pc, c_mod], writes=[c_ht[k][i]])
                else:
                    op("dve", lambda e: e.tensor_scalar(out=dst, in0=pbv[:, k, :], scalar1=sc_, scalar2=sh_,
                                                        op0=ALU.mult, op1=ALU.add),
                       reads=[pc, c_mod], writes=[c_ht[k][i]])

    pref = {}
    wA = carve(RW, 0, [128, 8, 1024]); brow = carve(RW, 8192, [128, 1024])
    wA_flat = carve(RW, 0, [128, 8192])

    def load_wA(sg, l, extra=()):
        c_wA = Cell()
        ex = list(extra)
        if sg.latent:
            dma("pool", wA[:, :, 0:768], w_in[l, :, 0:768].rearrange("(k p) n -> p k n", p=128), "w0", writes=[c_wA] + ex)
            dma("pool", wA[:, :, 768:1024], w_in[l, :, 1280:1536].rearrange("(k p) n -> p k n", p=128), "w0", writes=[c_wA])
            dma("pool", brow[0:1, 0:768], b_in[l, 0:768].rearrange("(o n) -> o n", o=1), "w0", writes=[c_wA])
            dma("pool", brow[0:1, 768:1024], b_in[l, 1280:1536].rearrange("(o n) -> o n", o=1), "w0", writes=[c_wA])
        else:
            dma("sp", wA_flat, scA[l], "w0", writes=[c_wA] + ex)
            dma("sp", brow[0:1, :], scB[l], "w0", writes=[c_wA])
        return c_wA

    def run_layer(sg, l, first, last, nxt=None):
        T, NT, NN, L, nseq, g = sg.T, sg.NT, sg.NN, sg.L, sg.nseq, sg.g
        TPS = L // 128
        if first:
            xnb = [carve(RB, r * 1024, [128, 1024]) for r in range(4)]
            c_xnb = cells(4)
            if pre0.pop(sg.name, False):
                for b_ in range(NT // 4):
                    ht_block(sg, l, 0, xnb, c_xnb, list(range(4 * b_, 4 * b_ + 4)))
            else:
                for i in range(NT):
                    dma("sp", X[:, i, :], sg.xin[i * 128:(i + 1) * 128, :], f"x{i % 8}", reads=([c_x[i - 8]] if i >= 8 else []), writes=[c_x[i]])
                def p0_s1(b):
                    stats_block(list(range(4 * b, 4 * b + 4)), 1)
                    stats_finish(1, 4 * b, 4 * b + 4)
                ln_pipeline(NT, p0_s1, lambda b: ht_block(sg, l, 0, xnb, c_xnb, list(range(4 * b, 4 * b + 4))), None)
            kb.barrier()
        dump(f"ht_{sg.name}{l}", HT[:, :, 0:T], [c for row in c_ht for c in row])
        chk(f"p0_{sg.name}{l}")

        qT = carve(RA, 0, [128, 4, 2048]); kT = carve(RA, 8192, [128, 2048]); kcT = carve(RA, 10240, [128, 256])
        vtok = carve(RA, 10496, [128, 18, 128]); ftok = carve(RA, 12800, [128, 16, 256])
        gluT = carve(RA, 16896, [128, 2, 2304])
        c_qa = cells(4, 16)
        c_kt = cells(16); c_kc = Cell(); c_v = cells(18); c_f = cells(16); c_glu = cells(2, 4)
        wC = carve(RW, 9216, [128, 8, 512])
        c_wC = Cell()
        wC_flat = carve(RW, 9216, [128, 4096])
        c_wA = pref.pop((sg.name, l)) if (sg.name, l) in pref else load_wA(sg, l)
        if sg.latent:
            dma("pool", wC, w_in[l, :, 768:1280].rearrange("(k p) n -> p k n", p=128), "w1", writes=[c_wC])
            dma("sp", scA[l], wA_flat, "g0", reads=[c_wA])
            dma("sp", scB[l], brow[0:1, :], "g0", reads=[c_wA])
            dma("sp", scC[l], wC_flat, "g0", reads=[c_wC])
        else:
            dma("sp", wC_flat, scC[l], "w1", writes=[c_wC])
        if sg.latent:
            kctok = carve(RB, 0, [128, 2, 128])
            c_kct = Cell()
            dma("pool", kctok, ck[l].rearrange("(t p) n -> p t n", p=128), "w2", writes=[c_kct])
            dma("pool", vtok[:, 16:18, :], cv[l].rearrange("(t p) n -> p t n", p=128), "g1", writes=[c_v[16], c_v[17]])
        op("pool", lambda e: e.memset(gluT, 0.0), writes=[c for row in c_glu for c in row])

        qtok = [carve(RB, 1024 + s * 2560, [128, 4, 640]) for s in range(2)]
        c_qtok = cells(2)
        tA = carve(RB, 6144, [128, 640], F32); tB = carve(RB, 7424, [128, 640], F32)
        c_tAB = Cell()
        kvo = [carve(RB, 8704 + s * 512, [128, 256], F32) for s in range(2)]
        c_kvo = cells(2)
        cosb = lambda i, h: rope[:, 0, i, :].unsqueeze(1).unsqueeze(1).broadcast_to([128, h, 2, 32])
        sinb = lambda i, h: rope[:, 1, i, :].unsqueeze(1).unsqueeze(1).broadcast_to([128, h, 2, 32])
        for n in range(NN):
            s = n % 2
            for t in range(4):
                i = n * 4 + t
                bq, cq = bank()
                br, cr = bank()
                for bb, cb, c0 in ((bq, cq, 0), (br, cr, 512)):
                    for k in range(8):
                        op("pe", lambda e: e.matmul(out=bb[:], lhsT=HT[:, k, i * 128:(i + 1) * 128],
                                                    rhs=wA[:, k, c0:c0 + 512], start=(k == 0), stop=False),
                           reads=[c_ht[k][i], c_wA], writes=[cb], signal=False)
                    op("pe", lambda e: e.matmul(out=bb[:], lhsT=onesb[0:1, :], rhs=brow[0:1, c0:c0 + 512],
                                                start=False, stop=True), reads=[c_wA, c_const], writes=[cb])
                qdst = qtok[s][:, t, 0:512]
                kdst = qtok[s][:, t, 512:640]
                if sg.latent:
                    for src, dst, h, w in ((bq[:], qdst, 8, 512), (br[:, 0:128], kdst, 2, 128)):
                        s4 = src.rearrange("p (h two d) -> p h two d", two=2, d=32)
                        a4 = tA[:, 0:w].rearrange("p (h two d) -> p h two d", two=2, d=32)
                        b4 = tB[:, 0:w].rearrange("p (h two d) -> p h two d", two=2, d=32)
                        d4 = dst.rearrange("p (h two d) -> p h two d", two=2, d=32)
                        rc = [cq if h == 8 else cr, c_const]
                        op("dve", lambda e: e.tensor_tensor(out=a4, in0=s4, in1=cosb(i, h), op=ALU.mult),
                           reads=rc, writes=[c_tAB])
                        op("dve", lambda e: e.tensor_tensor(out=b4, in0=s4, in1=sinb(i, h), op=ALU.mult),
                           reads=rc, writes=[c_tAB])
                        op("dve", lambda e: e.tensor_tensor(out=d4[:, :, 0, :], in0=a4[:, :, 0, :], in1=b4[:, :, 1, :],
                                                            op=ALU.subtract), reads=[c_tAB], writes=[c_qtok[s]])
                        op("dve", lambda e: e.tensor_tensor(out=d4[:, :, 1, :], in0=a4[:, :, 1, :], in1=b4[:, :, 0, :],
                                                            op=ALU.add), reads=[c_tAB], writes=[c_qtok[s]])
                else:
                    op("dve", lambda e: e.tensor_copy(out=qdst, in_=bq[:]), reads=[cq], writes=[c_qtok[s]])
                    op("act", lambda e: e.copy(out=kdst, in_=br[:, 0:128]), reads=[cr], writes=[c_qtok[s]])
                    ko = i % 2
                    op("act", lambda e: e.copy(out=kvo[ko], in_=br[:, 0:256]), reads=[cr], writes=[c_kvo[ko]])
                    sq, tt = i // TPS, i % TPS
                    st1 = dma("sp", nk[sq, l, tt * 128:(tt + 1) * 128, :], kvo[ko][:, 0:128], f"g{ko}", reads=[c_kvo[ko]])
                    st2 = dma("sp", nv[sq, l, tt * 128:(tt + 1) * 128, :], kvo[ko][:, 128:256], f"g{ko}", reads=[c_kvo[ko]])
                    kb.out_stamps.append(st2)
                if sg.latent:
                    op("dve", lambda e: e.tensor_copy(out=vtok[:, i, :], in_=br[:, 128:256]), reads=[cr], writes=[c_v[i]])
                    op("dve", lambda e: e.tensor_copy(out=ftok[:, i, :], in_=br[:, 256:512]), reads=[cr], writes=[c_f[i]])
                else:
                    op("act", lambda e: e.copy(out=vtok[:, i, :], in_=br[:, 128:256]), reads=[cr], writes=[c_v[i]])
                    op("act", lambda e: e.copy(out=ftok[:, i, :], in_=br[:, 256:512]), reads=[cr], writes=[c_f[i]])
            for grp in ((0, 1), (2, 3), (4,)):
                pb, pc = bank()
                pbv = pb[:].bitcast(BF16).rearrange("p (c t q) -> p c t q", c=2, t=4)
                for ci, c in enumerate(grp):
                    for t in range(4):
                        sig_ = (ci == len(grp) - 1 and t == 3)
                        if c < 4:
                            for hq in range(2):
                                cq0 = hq * 256 + c * 64
                                op("pe", lambda e: e.transpose(out=pbv[hq * 64:(hq + 1) * 64, ci, t, :], in_=qtok[s][:, t, cq0:cq0 + 64],
                                                               identity=identb[:]),
                                   reads=[c_qtok[s], c_const], writes=[pc], signal=(sig_ and hq == 1))
                        else:
                            op("pe", lambda e: e.transpose(out=pbv[:, ci, t, :], in_=qtok[s][:, t, 512:640], identity=identb[:]),
                               reads=[c_qtok[s], c_const], writes=[pc], signal=sig_)
                for ci, c in enumerate(grp):
                    src = pbv[:, ci, :, :].rearrange("p t q -> p (t q)")
                    if c < 4:
                        wr = [c_qa[c][n * 4 + t] for t in range(4)]
                        dst = qT[:, c, n * 512:(n + 1) * 512]
                    else:
                        wr = [c_kt[n * 4 + t] for t in range(4)]
                        dst = kT[:, n * 512:(n + 1) * 512]
                    if grp[0] != 2:
                        op("act", lambda e: e.copy(out=dst, in_=src), reads=[pc], writes=wr)
                    else:
                        op("dve", lambda e: e.tensor_copy(out=dst, in_=src), reads=[pc], writes=wr)
        sig = [carve(RB, 9728 + s * 1024, [128, 512], F32) for s in range(2)]
        c_sig = cells(2)
        si = 0
        for n in range(NN):
            bs = [bank() for _ in range(4)]
            for cidx in range(4):
                bb, cb = bs[cidx]
                for k in range(8):
                    op("pe", lambda e: e.matmul(out=bb[:], lhsT=wC[:, k, cidx * 128:(cidx + 1) * 128],
                                                rhs=HT[:, k, n * 512:(n + 1) * 512], start=(k == 0), stop=(k == 7)),
                       reads=c_ht[k][n * 4:n * 4 + 4] + [c_wC], writes=[cb], signal=(k == 7))
            for c in range(2):
                s = si % 2
                si += 1
                ba, ca = bs[c]
                bg_, cg_ = bs[2 + c]
                op("act", lambda e: e.activation(out=sig[s], in_=bg_[:], func=AF.Sigmoid, bias=cvc(l, "bconv", 2 + c)),
                   reads=[cg_, c_cv], writes=[c_sig[s]])
                nsq = 512 // L if L < 512 else 1
                if L >= 512:
                    dst = gluT[:, c, 15 + n * 512:15 + (n + 1) * 512]
                    in0 = ba[:]
                    in1 = sig[s]
                else:
                    dst = gluT[:, c, 0:nseq * sg.gpad].rearrange("p (s w) -> p s w", w=sg.gpad)[:, n * nsq:(n + 1) * nsq, 15:15 + L]
                    in0 = ba[:].rearrange("p (s w) -> p s w", w=L)
                    in1 = sig[s].rearrange("p (s w) -> p s w", w=L)
                op("dve", lambda e: e.scalar_tensor_tensor(out=dst, in0=in0, scalar=cvc(l, "bconv", c), in1=in1,
                                                           op0=ALU.add, op1=ALU.mult),
                   reads=[ca, c_sig[s], c_cv], writes=[c_glu[c][n]])
        if sg.latent:
            pb, pc = bank()
            pbv = pb[:].bitcast(BF16)
            for t in range(2):
                op("pe", lambda e: e.transpose(out=pbv[:, t * 128:(t + 1) * 128], in_=kctok[:, t, :], identity=identb[:]),
                   reads=[c_kct, c_const], writes=[pc], signal=(t == 1))
            op("dve", lambda e: e.tensor_copy(out=kcT, in_=pbv[:, 0:256]), reads=[pc], writes=[c_kc])
        kb.barrier()
        dump(f"qT_{sg.name}{l}", qT[:, :, 0:T]); dump(f"kT_{sg.name}{l}", kT[:, 0:T])
        dump(f"glu_{sg.name}{l}", gluT)
        dump(f"vtok_{sg.name}{l}", vtok); dump(f"ftok_{sg.name}{l}", ftok)
        chk(f"A_{sg.name}{l}")

        aoT = qT
        cuT = carve(RB, 0, [128, 2, 2048]); fmT = carve(RB, 4096, [128, 2, 2048])
        c_cu = cells(2, 4); c_fm = cells(2, 4)
        LA = 2 if (sg.latent and not (l == 0 and mod_late is not None)) else 1
        NPT = 6 if LA == 2 else 4
        PTW = 384 if LA == 2 else 640
        pTr = [carve(RB, 8192 + s * PTW, [128, PTW]) for s in range(NPT)]
        c_pT = cells(NPT)
        rec2 = [carve(RB, 10752, [128, 256], F32), carve(RB, 11264, [128, 256], F32)]
        c_rec = cells(2)
        vAB = carve(RW, 0, [128, 18, 2, 128])
        srow = carve(RW, 4608, [128, 4, 2, 128])
        c_vab = Cell()
        nvt = 18 if sg.latent else NT
        op("dve", lambda e: e.memset(vAB, 1.0), writes=[c_vab])
        op("dve", lambda e: e.tensor_copy(out=vAB[:, 0:nvt, 0, 0:64], in_=vtok[:, 0:nvt, 0:64]), reads=c_v[0:nvt], writes=[c_vab])
        op("dve", lambda e: e.tensor_copy(out=vAB[:, 0:nvt, 1, 64:128], in_=vtok[:, 0:nvt, 64:128]), reads=c_v[0:nvt], writes=[c_vab])
        c_srow = cells(4, 2)
        op("dve", lambda e: e.memset(srow[0:1], 0.0), writes=[c_ for row in c_srow for c_ in row])
        for c in range(4):
            for hh in range(2):
                h_ = c + 4 * hh
                lo = 64 if hh == 0 else 0
                op("dve", lambda e: e.tensor_copy(out=srow[0:1, c, hh, lo:lo + 64], in_=e8r[0:1, l, h_:h_ + 1].broadcast_to([1, 64])),
                   reads=[c_e8r], writes=[c_srow[c][hh]])
        pti = 0
        QW = 128 if sg.latent else L
        nqb = T // QW
        pstate = {"pti": 0}

        def keychunks(qb):
            if sg.latent:
                kch = []
                if qb > 0:
                    kch.append(("loc", qb - 1, 0))
                kch.append(("loc", qb, None))
                if qb < nqb - 1:
                    kch.append(("loc", qb + 1, 1))
                kch += [("ctx", 0, None), ("ctx", 1, None)]
                return [kch[0:3], kch[3:]]
            return [[("loc", qb * TPS + t, None) for t in range(TPS)]]

        def s_stage(qb, c, hh):
            q0 = qb * QW
            r0 = hh * 64
            qcells_idx = range(q0 // 128, (q0 + QW) // 128)
            pts = []
            for grp in keychunks(qb):
                if not grp:
                    continue
                sbk, sbc = bank()
                s = pstate["pti"] % NPT
                pstate["pti"] += 1
                w = len(grp) * QW
                for j, (kind, kt_, mk) in enumerate(grp):
                    if kind == "loc":
                        lhs = kT[r0:r0 + 64, kt_ * 128:(kt_ + 1) * 128]
                        rd = [c_kt[kt_]]
                    else:
                        lhs = kcT[r0:r0 + 64, kt_ * 128:(kt_ + 1) * 128]
                        rd = [c_kc]
                    rd += [c_qa[c][x_] for x_ in qcells_idx]
                    op("pe", lambda e: e.matmul(out=sbk[:, j * QW:(j + 1) * QW], lhsT=lhs,
                                                rhs=qT[r0:r0 + 64, c, q0:q0 + QW], start=True, stop=True),
                       reads=rd, writes=[sbc], signal=(j == len(grp) - 1))
                op("act", lambda e: e.activation(out=pTr[s][:, 0:w], in_=sbk[:, 0:w], func=AF.Exp, scale=0.125),
                   reads=[sbc], writes=[c_pT[s]])
                for j, (kind, kt_, mk) in enumerate(grp):
                    if mk is not None:
                        m0 = 0 if mk == 0 else 256
                        op("dve", lambda e: e.tensor_tensor(out=pTr[s][:, j * QW:(j + 1) * QW], in0=pTr[s][:, j * QW:(j + 1) * QW],
                                                             in1=mask01[:, m0:m0 + 128], op=ALU.mult),
                           reads=[c_const], writes=[c_pT[s]])
                for j, (kind, kt_, mk) in enumerate(grp):
                    vt = kt_ if kind == "loc" else 16 + kt_
                    pts.append((s, j, vt))
            return pts

        obanks = {}

        def v_stage(qb, c, hh, pts):
            q0 = qb * QW
            qcells_idx = range(q0 // 128, (q0 + QW) // 128)
            ob, oc = bank()
            obanks[hh] = (ob, oc)
            for jj, (s, j, vt) in enumerate(pts):
                op("pe", lambda e: e.matmul(out=ob[:, 0:QW], lhsT=vAB[:, vt, hh, :], rhs=pTr[s][:, j * QW:(j + 1) * QW],
                                            start=(jj == 0), stop=False),
                   reads=[c_pT[s], c_vab], writes=[oc], signal=False)
            op("pe", lambda e: e.matmul(out=ob[:, 0:QW], lhsT=srow[0:1, c, hh, :], rhs=onesr[0:1, 0:QW], start=False, stop=True),
               reads=[c_srow[c][hh], c_const], writes=[oc])
            if hh == 1:
                for h2 in range(2):
                    r0 = h2 * 64
                    d0 = 64 - r0
                    ob2, oc2 = obanks[h2]
                    rr_ = rec2[h2]
                    op("act", lambda e: e.activation(out=rr_[r0:r0 + 64, 0:QW], in_=ob2[d0:d0 + 64, 0:QW], func=AF.Ln),
                       reads=[oc2], writes=[c_rec[h2]])
                    op("act", lambda e: e.activation(out=rr_[r0:r0 + 64, 0:QW], in_=rr_[r0:r0 + 64, 0:QW], func=AF.Exp, scale=-1.0),
                       reads=[c_rec[h2]], writes=[c_rec[h2]])
                    op("dve", lambda e: e.tensor_tensor(out=aoT[r0:r0 + 64, c, q0:q0 + QW], in0=ob2[r0:r0 + 64, 0:QW],
                                                        in1=rr_[r0:r0 + 64, 0:QW], op=ALU.mult),
                       reads=[oc2, c_rec[h2]], writes=[c_qa[c][x_] for x_ in qcells_idx])

        job = None
        if sg.latent and l == 0 and mod_late is not None:
            pm, pmc, pmi = kb.reserve()
            job = ModJob(mod_late, [carve(RW, 5632 + s_ * 4096, [128, 8, 512]) for s_ in range(2)], ["w1", "w2"], pm, pmc)
            job.load()
        pendq = []
        ui = 0
        for qb in range(nqb):
            for c in range(4):
                for hh in range(2):
                    pts = s_stage(qb, c, hh)
                    pendq.append((qb, c, hh, pts))
                    if len(pendq) > LA:
                        v_stage(*pendq.pop(0))
                    ui += 1
                    if job is not None and ui % 10 == 0:
                        job.step()
        while pendq:
            v_stage(*pendq.pop(0))
        if job is not None:
            job.finish()
            kb.reserved.discard(pmi)
        kb.barrier()
        dump(f"ao_{sg.name}{l}", aoT[:, :, 0:T])
        chk(f"B1_{sg.name}{l}")

        wG = [carve(RW, s * 4096, [128, 8, 3, 128]) for s in range(2)]
        wO3 = [carve(RW, s * 4096 + 3072, [128, 8, 128]) for s in range(2)]
        wGO_flat = [carve(RW, s * 4096, [128, 4096]) for s in range(2)]
        c_wG = cells(2)
        def load_wG(j):
            s = j % 2
            if sg.latent:
                for gi in range(3):
                    c0 = 1536 + gi * 1024 + j * 128
                    dma("pool", wG[s][:, :, gi, :], w_in[l, :, c0:c0 + 128].rearrange("(k p) n -> p k n", p=128),
                        f"w{s}", writes=[c_wG[s]])
                for hh in range(2):
                    dma("pool", wO3[s][hh * 64:(hh + 1) * 64, 0:4, :],
                        w_attn_o[l, hh * 256:(hh + 1) * 256, j * 128:(j + 1) * 128].rearrange("(c d) n -> d c n", d=64),
                        f"w{s}", writes=[c_wG[s]])
                dma("pool", wO3[s][:, 4:6, :], w_conv_o[l, :, j * 128:(j + 1) * 128].rearrange("(k p) n -> p k n", p=128),
                    f"w{s}", writes=[c_wG[s]])
                dma("pool", wO3[s][:, 6:8, :], w_fnet[l, :, j * 128:(j + 1) * 128].rearrange("(k p) n -> p k n", p=128),
                    f"w{s}", writes=[c_wG[s]])
            else:
                dma("sp", wGO_flat[s], scG[l, j], f"w{s}", writes=[c_wG[s]])

        dfs = [carve(RA, 8192, [128, 2, 4, 512]), carve(RW, 4096, [128, 2, 4, 512]), carve(RW, 8192, [128, 2, 4, 512])]
        dfs_keys = ["g0", "g1", "w2"]
        c_dfs = cells(3)
        if sg.latent:
            dma("sp", dfs[0].rearrange("p a b c -> p (a b c)"), c_dft2k[0, 0], dfs_keys[0], writes=[c_dfs[0]])
        dg = carve(RW, 0, [128, 2, 31, 128])
        c_dg = cells(2, 31)
        for c in range(2):
            for k in range(31):
                op("dve", lambda e: e.tensor_scalar(out=dg[:, c, k, :], in0=identb[:], scalar1=cvc(l, "cw", k * 2 + c),
                                                    scalar2=None, op0=ALU.mult), reads=[c_cv, c_const], writes=[c_dg[c][k]])
        uu = carve(RW, 7936, [128, 2, 512], F32); usq = carve(RW, 9984, [128, 2, 512], F32)
        stt = carve(RW, 12032, [128, 2, 512], F32)
        c_uu = Cell(); c_stt = Cell()
        for n in range(NN):
            cbs = [bank() for _ in range(2)]
            nsq = max(1, 512 // L)
            W = min(L, 512)
            for c in range(2):
                bb, cb = cbs[c]
                for sq in range(nsq):
                    if L >= 512:
                        base = n * 512
                    else:
                        base = (n * nsq + sq) * sg.gpad
                    for k in range(31):
                        op("pe", lambda e: e.matmul(out=bb[:, sq * W:(sq + 1) * W], lhsT=dg[:, c, k, :],
                                                    rhs=gluT[:, c, base + k:base + k + W], start=(k == 0), stop=(k == 30)),
                           reads=[c_dg[c][k]] + c_glu[c], writes=[cb], signal=(k == 30 and sq == nsq - 1))
                op("act", lambda e: e.activation(out=uu[:, c, :], in_=bb[:], func=AF.Identity, bias=cvc(l, "convb", c)),
                   reads=[cb, c_cv], writes=[c_uu])
                op("act", lambda e: e.activation(out=usq[:, c, :], in_=uu[:, c, :], func=AF.Square), writes=[c_uu])
            bm_, cm_ = bank()
            bv_, cv_ = bank()
            for (bb, cb, srcT) in ((bm_, cm_, uu), (bv_, cv_, usq)):
                for c in range(2):
                    op("pe", lambda e: e.matmul(out=bb[:], lhsT=onesd[:], rhs=srcT[:, c, :], start=(c == 0), stop=(c == 1)),
                       reads=[c_uu, c_const], writes=[cb], signal=(c == 1))
            op("dve", lambda e: e.tensor_copy(out=stt[:, 0, :], in_=bm_[:]), reads=[cm_], writes=[c_stt])
            op("dve", lambda e: e.tensor_tensor(out=stt[:, 1, :], in0=stt[:, 0, :], in1=stt[:, 0, :], op=ALU.mult), writes=[c_stt])
            op("dve", lambda e: e.tensor_tensor(out=stt[:, 1, :], in0=bv_[:], in1=stt[:, 1, :], op=ALU.subtract),
               reads=[cv_], writes=[c_stt])
            op("act", lambda e: e.activation(out=stt[:, 1, :], in_=stt[:, 1, :], func=AF.Sqrt, bias=epsc[:, 0:1]),
               reads=[c_stt, c_const], writes=[c_stt])
            op("dve", lambda e: e.reciprocal(out=stt[:, 1, :], in_=stt[:, 1, :]), reads=[c_stt], writes=[c_stt])
            for c in range(2):
                op("dve", lambda e: e.tensor_tensor(out=uu[:, c, :], in0=uu[:, c, :], in1=stt[:, 0, :], op=ALU.subtract),
                   reads=[c_stt], writes=[c_uu])
                op("dve", lambda e: e.tensor_tensor(out=uu[:, c, :], in0=uu[:, c, :], in1=stt[:, 1, :], op=ALU.mult),
                   writes=[c_uu])
                op("act", lambda e: e.activation(out=cuT[:, c, n * 512:(n + 1) * 512], in_=uu[:, c, :], func=AF.Silu,
                                                 bias=cvc(l, "lnb", c), scale=cvc(l, "lng", c)),
                   reads=[c_uu, c_cv], writes=[c_cu[c][n]])
        kb.barrier()
        dump(f"cu_{sg.name}{l}", cuT[:, :, 0:T])
        chk(f"B2_{sg.name}{l}")

        u12 = [carve(RB, 8192 + s * 2048, [128, 2, 2, 512]) for s in range(2)]
        c_u12 = cells(2)
        BCi = 2 if sg.latent else 0
        load_wG(0)
        if sg.latent:
            di = 0
            for n in range(4):
                ub = [[bank() for c in range(2)] for cs in range(2)]
                for kg in range(4):
                    s = di % 3
                    di += 1
                    if di > 1:
                        dma("sp", dfs[s].rearrange("p a b c -> p (a b c)"), c_dft2k[n, kg], dfs_keys[s], writes=[c_dfs[s]])
                    for k4 in range(4):
                        kt_ = kg * 4 + k4
                        for cs in range(2):
                            for c in range(2):
                                bb, cb = ub[cs][c]
                                op("pe", lambda e: e.matmul(out=bb[:], lhsT=ftok[:, kt_, c * 128:(c + 1) * 128],
                                                            rhs=dfs[s][:, cs, k4, :], start=(kt_ == 0), stop=(kt_ == 15)),
                                   reads=[c_f[kt_], c_dfs[s]], writes=[cb], signal=(kt_ == 15 or (k4 == 3 and cs == 1 and c == 1)))
                us = n % 2
                for cs in range(2):
                    for c in range(2):
                        bb, cb = ub[cs][c]
                        if c == 0:
                            op("act", lambda e: e.copy(out=u12[us][:, cs, c, :], in_=bb[:]), reads=[cb], writes=[c_u12[us]])
                        else:
                            op("dve", lambda e: e.tensor_copy(out=u12[us][:, cs, c, :], in_=bb[:]), reads=[cb], writes=[c_u12[us]])
                for c in range(2):
                    yb, yc = bank()
                    for cs in range(2):
                        op("pe", lambda e: e.matmul(out=yb[:], lhsT=bcs[:, BCi + cs, :], rhs=u12[us][:, cs, c, :],
                                                    start=(cs == 0), stop=(cs == 1)),
                           reads=[c_u12[us], c_const], writes=[yc], signal=(cs == 1))
                    op("act", lambda e: e.copy(out=fmT[:, c, n * 512:(n + 1) * 512], in_=yb[:]), reads=[yc], writes=[c_fm[c][n]])
        else:
            for sp_ in range(nseq // 2):
                us = sp_ % 2
                for cs in range(2):
                    for c in range(2):
                        bb, cb = bank()
                        for sq in range(2):
                            sidx = sp_ * 2 + sq
                            for k2 in range(2):
                                kt_ = sidx * 2 + k2
                                op("pe", lambda e: e.matmul(out=bb[:, sq * 256:(sq + 1) * 256],
                                                            lhsT=ftok[:, kt_, c * 128:(c + 1) * 128],
                                                            rhs=dft256[:, cs, k2, :], start=(k2 == 0), stop=(k2 == 1)),
                                   reads=[c_f[kt_], c_const], writes=[cb], signal=(sq == 1 and k2 == 1))
                        if c == 0:
                            op("act", lambda e: e.copy(out=u12[us][:, cs, c, :], in_=bb[:]), reads=[cb], writes=[c_u12[us]])
                        else:
                            op("dve", lambda e: e.tensor_copy(out=u12[us][:, cs, c, :], in_=bb[:]), reads=[cb], writes=[c_u12[us]])
                for c in range(2):
                    yb, yc = bank()
                    for cs in range(2):
                        op("pe", lambda e: e.matmul(out=yb[:], lhsT=bcs[:, BCi + cs, :], rhs=u12[us][:, cs, c, :],
                                                    start=(cs == 0), stop=(cs == 1)),
                           reads=[c_u12[us], c_const], writes=[yc], signal=(cs == 1))
                    op("act", lambda e: e.copy(out=fmT[:, c, sp_ * 512:(sp_ + 1) * 512], in_=yb[:]), reads=[yc],
                       writes=[c_fm[c][sp_]])
        kb.barrier()
        dump(f"fm_{sg.name}{l}", fmT[:, :, 0:T])
        chk(f"B3_{sg.name}{l}")

        mT = [carve(RA, 8192 + j * 2048, [128, 2048]) for j in range(6)] + \
             [carve(RB, 8192 + (j - 6) * 2048, [128, 2048]) for j in range(6, 8)]
        c_m = cells(8, 4)
        sgt = [carve(RW, 10240 + s * 1024, [128, 512], F32) for s in range(3)]
        c_sgt = cells(3)
        m1 = carve(RW, 13312, [128, 512], F32)
        c_m1 = Cell()
        sgi = 0
        for j in range(8):
            s = j % 2
            if j + 1 < 8:
                load_wG(j + 1)
            if (not sg.latent) and 2 <= j <= 5:
                for k in (2 * (j - 2), 2 * (j - 2) + 1):
                    dma("sp", X[:, 8 + k, :], w_o[l, k * 128:(k + 1) * 128, :], f"x{k}", writes=[c_x[8 + k]])
            for n in range(NN):
                bg3 = [bank() for _ in range(3)]
                for gi in range(3):
                    bb, cb = bg3[gi]
                    for k in range(8):
                        op("pe", lambda e: e.matmul(out=bb[:], lhsT=wG[s][:, k, gi, :], rhs=HT[:, k, n * 512:(n + 1) * 512],
                                                    start=(k == 0), stop=(k == 7)),
                           reads=[c_wG[s]] + c_ht[k][n * 4:n * 4 + 4], writes=[cb], signal=(k == 7))
                ba3 = [bank() for _ in range(3)]
                srcs = [(aoT, 0, 4, [c_qa[c][n * 4 + t] for c in range(4) for t in range(4)]),
                        (cuT, 4, 2, [c_cu[c][n] for c in range(2)]), (fmT, 6, 2, [c_fm[c][n] for c in range(2)])]
                for ai, (srcT, w0, nk_, rd) in enumerate(srcs):
                    bb, cb = ba3[ai]
                    for kk in range(nk_):
                        op("pe", lambda e: e.matmul(out=bb[:], lhsT=wO3[s][:, w0 + kk, :], rhs=srcT[:, kk, n * 512:(n + 1) * 512],
                                                    start=(kk == 0), stop=(kk == nk_ - 1)),
                           reads=[c_wG[s]] + rd, writes=[cb], signal=(kk == nk_ - 1))
                sl = []
                for gi in range(3):
                    ss = sgi % 3
                    sgi += 1
                    sl.append(ss)
                    op("act", lambda e: e.activation(out=sgt[ss], in_=bg3[gi][0][:], func=AF.Sigmoid,
                                                     bias=cvc(l, "bgate", gi * 8 + j)),
                       reads=[bg3[gi][1], c_cv], writes=[c_sgt[ss]])
                op("dve", lambda e: e.tensor_tensor(out=sgt[sl[0]], in0=sgt[sl[0]], in1=ba3[0][0][:], op=ALU.mult),
                   reads=[ba3[0][1]], writes=[c_sgt[sl[0]]])
                op("dve", lambda e: e.tensor_tensor(out=sgt[sl[1]], in0=sgt[sl[1]], in1=ba3[1][0][:], op=ALU.mult),
                   reads=[ba3[1][1]], writes=[c_sgt[sl[1]]])
                op("dve", lambda e: e.scalar_tensor_tensor(out=sgt[sl[2]], in0=ba3[2][0][:], scalar=cvc(l, "bfnet", j),
                                                           in1=sgt[sl[2]], op0=ALU.add, op1=ALU.mult),
                   reads=[ba3[2][1], c_cv], writes=[c_sgt[sl[2]]])
                op("dve", lambda e: e.tensor_tensor(out=m1, in0=sgt[sl[0]], in1=sgt[sl[1]], op=ALU.add),
                   reads=[c_sgt[sl[0]], c_sgt[sl[1]]], writes=[c_m1])
                op("dve", lambda e: e.tensor_tensor(out=mT[j][:, n * 512:(n + 1) * 512], in0=m1, in1=sgt[sl[2]], op=ALU.add),
                   reads=[c_sgt[sl[2]], c_m1], writes=[c_m[j][n]])
            if sg.latent:
                dma("sp", scG[l, j], wGO_flat[s], f"g{s}", reads=[c_wG[s]])
        kb.barrier()
        if any(k_.startswith("mg_") for k_ in dbg):
            for j in range(8):
                if f"mg_{sg.name}{l}" in dbg:
                    st = dma("pool", dbg[f"mg_{sg.name}{l}"][:, j, :], mT[j][:, 0:T], "dbg")
                    kb.out_stamps.append(st)

        chk(f"C_{sg.name}{l}")
        wo = carve(RW, 0, [128, 8, 1024])
        c_wo = cells(8)
        lng_ = carve(RW, 8192, [128, 1024], F32); lnb_ = carve(RW, 10240, [128, 1024], F32)
        c_lnt = Cell()
        stg2 = [carve(RB, s * 2048, [128, 1024], F32) for s in range(2)]
        c_stg2 = cells(2)
        gbc = carve(RB, 4096, [128, 1024], F32)
        c_gbc = Cell()
        xnb = [carve(RA, s * 1024, [128, 1024]) for s in range(4)]
        c_xnb = cells(4)

        def make_gbc(s_idx):
            for half in range(2):
                pb, pc = bank()
                for jj in range(4):
                    j = half * 4 + jj
                    op("dve", lambda e: e.tensor_scalar(out=dgf[:], in0=identf[:], scalar1=modc(l, s_idx, j, g), scalar2=None,
                                                        op0=ALU.mult), reads=[c_mod, c_const], writes=[c_dgf])
                    op("pe", lambda e: e.matmul(out=pb[:, jj * 128:(jj + 1) * 128], lhsT=onesf[:], rhs=dgf[:], start=True, stop=True),
                       reads=[c_dgf, c_const], writes=[pc])
                op("act", lambda e: e.copy(out=gbc[:, half * 512:(half + 1) * 512], in_=pb[:]), reads=[pc], writes=[c_gbc])

        make_gbc(2)
        dma("sp", lng_, ln1_g[l].partition_broadcast(128), "w2", writes=[c_lnt])
        dma("sp", lnb_, ln1_b[l].partition_broadcast(128), "w2", writes=[c_lnt])
        for k in range(8):
            s = k % 2
            if sg.latent:
                dma("sp", stg2[s], w_o[l, k * 128:(k + 1) * 128, :], f"g{s}", writes=[c_stg2[s]])
                src_w, rc_w = stg2[s], c_stg2[s]
            else:
                src_w, rc_w = X[:, 8 + k, :], c_x[8 + k]
            op("pool" if (sg.latent or k % 2) else "dve", lambda e: e.tensor_tensor(out=wo[:, k, :], in0=src_w, in1=gbc, op=ALU.mult),
               reads=[rc_w, c_gbc], writes=[c_wo[k]])
        wU = [carve(RW, s * 2048, [128, 8, 2, 128]) for s in range(2)]
        wU_flat = [carve(RW, s * 2048, [128, 2048]) for s in range(2)]
        c_wU = cells(2)
        def load_wU(j, extra=()):
            s = j % 2
            ex = list(extra)
            if sg.latent:
                for b in range(2):
                    c0 = b * DFF + j * 128
                    dma("pool", wU[s][:, :, b, :], w_up[l, :, c0:c0 + 128].rearrange("(k p) n -> p k n", p=128),
                        f"w{s}", writes=[c_wU[s]] + ex)
                    ex = []
            else:
                dma("sp", wU_flat[s], scU[l, j], f"w{s}", writes=[c_wU[s]] + ex)

        def d_s1(b):
            for i0 in (4 * b, 4 * b + 2):
                for i in (i0, i0 + 1):
                    n = i // 4
                    for half in range(2):
                        pb, pc = bank()
                        for k in range(8):
                            op("pe", lambda e: e.matmul(out=pb[:], lhsT=mT[k][:, i * 128:(i + 1) * 128],
                                                        rhs=wo[:, k, half * 512:(half + 1) * 512], start=(k == 0), stop=(k == 7)),
                               reads=[c_m[k][n], c_wo[k]], writes=[pc], signal=(k == 7))
                        xs_ = X[:, i, half * 512:(half + 1) * 512]
                        op("dve", lambda e: e.scalar_tensor_tensor(out=xs_, in0=xs_, scalar=ALPHA, in1=pb[:], op0=ALU.mult, op1=ALU.add),
                           reads=[pc], writes=[c_x[i]])
                stats_block([i0, i0 + 1], 0)
            stats_finish(0, 4 * b, 4 * b + 4)
            if b == NT // 4 - 1:
                load_wU(0, extra=c_wo)

        def d_s2(b):
            blk = list(range(4 * b, 4 * b + 4))
            affine_block(blk, lng_, lnb_, c_lnt)
            stats_block(blk, 1)
            stats_finish(1, 4 * b, 4 * b + 4)

        ln_pipeline(NT, d_s1, d_s2, lambda b: ht_block(sg, l, 3, xnb, c_xnb, list(range(4 * b, 4 * b + 4))))
        kb.barrier()
        dump(f"x1_{sg.name}{l}", X[:, 0:NT, :])
        dump(f"h2_{sg.name}{l}", HT[:, :, 0:T])
        chk(f"D_{sg.name}{l}")

        gT = carve(RA, 0, [128, 6, 2048])
        c_gT = cells(6, 4)
        UW = 2080
        ub_ = [[carve(RA, 12288 + (s * 2 + b) * UW, [128, UW]) for b in range(2)] for s in range(2)]
        c_ub = cells(2, 2, 4)
        ct = [carve(RB, s * 1024, [128, 512], F32) for s in range(2)]
        c_ct = cells(2)
        dgu = [carve(RB, 2048 + s * 768, [128, 6, 128]) for s in range(2)]
        tv = [carve(RB, 10240 + s * 1024, [128, 512], F32) for s in range(2)]
        c_tv = cells(2)
        c_dgu = cells(2, 6)
        stg3 = [carve(RB, 4096 + s * 2048, [128, 1024], F32) for s in range(2)]
        c_stg3 = cells(2)
        gbc2 = carve(RB, 8192, [128, 1024], F32)
        xnb2 = [carve(RA, 12288 + s * 1024, [128, 1024]) for s in range(4)]
        c_xnb2 = cells(4)
        wD = carve(RW, 4096, [128, 6, 1024])
        c_wD = cells(6)
        lng2 = carve(RW, 10240, [128, 1024], F32); lnb2 = carve(RW, 12288, [128, 1024], F32)
        c_lnt2 = Cell()
        gbc = gbc2
        for s in range(2):
            for b in range(2):
                op("pool", lambda e: e.memset(ub_[s][b], 0.0), writes=c_ub[s][b])
        c_gbc = Cell()

        def build_gbc2():
            for half in range(2):
                pb, pc = bank()
                for jj in range(4):
                    j = half * 4 + jj
                    op("dve", lambda e: e.tensor_scalar(out=dgf[:], in0=identf[:], scalar1=modc(l, 5, j, g), scalar2=None,
                                                        op0=ALU.mult), reads=[c_mod, c_const], writes=[c_dgf])
                    op("pe", lambda e: e.matmul(out=pb[:, jj * 128:(jj + 1) * 128], lhsT=onesf[:], rhs=dgf[:], start=True, stop=True),
                       reads=[c_dgf, c_const], writes=[pc])
                op("act", lambda e: e.copy(out=gbc2[:, half * 512:(half + 1) * 512], in_=pb[:]), reads=[pc], writes=[c_gbc])

        dma("sp", lng2, ln2_g[l].partition_broadcast(128), "w2", writes=[c_lnt2])
        dma("sp", lnb2, ln2_b[l].partition_broadcast(128), "w2", writes=[c_lnt2])
        wui = 0
        cti = 0
        sdi = 0
        upad = sg.upad

        def uview(buf, n, shift):
            if L >= 512:
                return buf[:, n * 512 + shift:n * 512 + shift + 512]
            nsq = 512 // L
            return buf[:, 0:nseq * upad].rearrange("p (s w) -> p s w", w=upad)[:, n * nsq:(n + 1) * nsq, shift:shift + L]

        def tview(t):
            return t if L >= 512 else t.rearrange("p (s w) -> p s w", w=L)

        for gi, (j0, nj) in enumerate(FFN_GROUPS):
            for jj in range(nj):
                j = j0 + jj
                s = j % 2
                if j + 1 < NFF:
                    load_wU(j + 1)
                for n in range(NN):
                    for b in range(2):
                        pb, pc = bank()
                        for k in range(8):
                            op("pe", lambda e: e.matmul(out=pb[:], lhsT=wU[s][:, k, b, :], rhs=HT[:, k, n * 512:(n + 1) * 512],
                                                        start=(k == 0), stop=(k == 7)),
                               reads=[c_wU[s]] + c_ht[k][n * 4:n * 4 + 4], writes=[pc], signal=(k == 7))
                        op("act", lambda e: e.copy(out=uview(ub_[s][b], n, 1), in_=tview(pb[:])), reads=[pc],
                           writes=[c_ub[s][b][n]])
                if j == 0:
                    build_gbc2()
                for k3 in range(3):
                    col = k3 * 44 + j
                    op("dve", lambda e: e.tensor_scalar(out=dgu[s][:, k3, :], in0=identb[:], scalar1=cvc(l, "fcw", col),
                                                        scalar2=None, op0=ALU.mult), reads=[c_cv, c_const], writes=[c_dgu[s][k3]])
                for n in range(NN):
                    rdn = [m_ for m_ in (n - 1, n, n + 1) if 0 <= m_ < NN]
                    pb, pc = bank()
                    for k3 in range(3):
                        op("pe", lambda e: e.matmul(out=tview(pb[:]), lhsT=dgu[s][:, k3, :], rhs=uview(ub_[s][0], n, k3),
                                                    start=(k3 == 0), stop=(k3 == 2)),
                           reads=[c_dgu[s][k3]] + [c_ub[s][0][m_] for m_ in rdn], writes=[pc], signal=(k3 == 2))
                    r = cti % 2
                    cti += 1
                    op("act", lambda e: e.activation(out=ct[r], in_=pb[:], func=AF.Silu, bias=cvc(l, "fcb", j)),
                       reads=[pc, c_cv], writes=[c_ct[r]])
                    colv = NFF + j
                    rdv = [c_ub[s][1][m_] for m_ in rdn]
                    op("act", lambda e: e.activation(out=tview(tv[r]), in_=uview(ub_[s][1], n, 1), func=AF.Identity,
                                                     scale=cvc(l, "fcw", 44 + colv), bias=cvc(l, "fcb", colv)),
                       reads=rdv + [c_cv], writes=[c_tv[r]])
                    for sh, wrow in ((0, 0), (2, 2)):
                        op("dve", lambda e: e.scalar_tensor_tensor(out=tview(tv[r]), in0=uview(ub_[s][1], n, sh),
                                                                   scalar=cvc(l, "fcw", wrow * 44 + colv), in1=tview(tv[r]),
                                                                   op0=ALU.mult, op1=ALU.add), reads=rdv + [c_cv], writes=[c_tv[r]])
                    op("dve", lambda e: e.tensor_tensor(out=gT[:, jj, n * 512:(n + 1) * 512], in0=tv[r], in1=ct[r], op=ALU.mult),
                       reads=[c_tv[r], c_ct[r]], writes=[c_gT[jj][n]])
                sd = sdi % 2
                sdi += 1
                dma("sp", stg3[sd], w_down[l, j * 128:(j + 1) * 128, :], f"g{sd}", writes=[c_stg3[sd]])
                if sg.latent:
                    dma("sp", scU[l, j], wU_flat[s], f"x{s}", reads=[c_wU[s]])
                op("pool", lambda e: e.tensor_tensor(out=wD[:, jj, :], in0=stg3[sd], in1=gbc2, op=ALU.mult),
                   reads=[c_stg3[sd], c_gbc], writes=[c_wD[jj]])
            lastg = gi == len(FFN_GROUPS) - 1

            def e_s1(b, gi=gi, nj=nj, lastg=lastg):
                for i0 in (4 * b, 4 * b + 2):
                    for i in (i0, i0 + 1):
                        n = i // 4
                        for half in range(2):
                            pb, pc = bank()
                            for jj in range(nj):
                                op("pe", lambda e: e.matmul(out=pb[:], lhsT=gT[:, jj, i * 128:(i + 1) * 128],
                                                            rhs=wD[:, jj, half * 512:(half + 1) * 512], start=(jj == 0), stop=(jj == nj - 1)),
                                   reads=[c_gT[jj][n], c_wD[jj]], writes=[pc], signal=(jj == nj - 1))
                            xs_ = X[:, i, half * 512:(half + 1) * 512]
                            if gi == 0:
                                op("dve", lambda e: e.scalar_tensor_tensor(out=xs_, in0=xs_, scalar=ALPHA, in1=pb[:], op0=ALU.mult,
                                                                           op1=ALU.add), reads=[pc], writes=[c_x[i]])
                            else:
                                op("dve", lambda e: e.tensor_tensor(out=xs_, in0=xs_, in1=pb[:], op=ALU.add), reads=[pc],
                                   writes=[c_x[i]])
                    if lastg:
                        stats_block([i0, i0 + 1], 0)
                if lastg:
                    stats_finish(0, 4 * b, 4 * b + 4)
                    if b == NT // 4 - 1 and nxt is not None:
                        pref[(nxt[0].name, nxt[1])] = load_wA(nxt[0], nxt[1], extra=c_wU + c_wD)

            def e_s2(b):
                blk = list(range(4 * b, 4 * b + 4))
                affine_block(blk, lng2, lnb2, c_lnt2)
                if last:
                    for i in blk:
                        st = dma("sp", sg.yout[i * 128:(i + 1) * 128, :], X[:, i, :], f"x{i % 8}", reads=[c_x[i]])
                        kb.out_stamps.append(st)
                else:
                    stats_block(blk, 1)
                    stats_finish(1, 4 * b, 4 * b + 4)

            if not lastg:
                ln_pipeline(NT, e_s1, None, None)
            else:
                ln_pipeline(NT, e_s1, e_s2,
                            (lambda b: ht_block(sg, l + 1, 0, xnb2, c_xnb2, list(range(4 * b, 4 * b + 4)))) if not last else None)
        kb.barrier()

    segS_ = Seg("S", 2048, 1, True, 1, xs, ys)
    pre0 = {}
    if stop is None:
        sg0 = [q for q in ("S", "P") if q in segs][0]
        nt0 = 16 if sg0 == "S" else 8
        xin0 = xs if sg0 == "S" else xp
        for i in range(nt0):
            dma("sp", X[:, i, :], xin0[i * 128:(i + 1) * 128, :], f"x{i % 8}", reads=([c_x[i - 8]] if i >= 8 else []), writes=[c_x[i]])
        for b_ in range(nt0 // 4):
            stats_block(list(range(4 * b_, 4 * b_ + 4)), 1)
            stats_finish(1, 4 * b_, 4 * b_ + 4)
        pre0[sg0] = True
    for l in range(DEPTH):
        items = [("bmod", b_mod[l].rearrange("(c p) -> c p", p=128), 48),
                 ("bconv", b_in[l, 768:1280].rearrange("(c p) -> c p", p=128), 4),
                 ("bgate", b_in[l, 1536:4608].rearrange("(c p) -> c p", p=128), 24),
                 ("convb", conv_b[l].rearrange("(c p) -> c p", p=128), 2),
                 ("lng", conv_ln_g[l].rearrange("(c p) -> c p", p=128), 2),
                 ("lnb", conv_ln_b[l].rearrange("(c p) -> c p", p=128), 2),
                 ("bfnet", b_fnet[l].rearrange("(c p) -> c p", p=128), 8),
                 ("fcw", ffn_conv_w[l].rearrange("k (c p) -> (k c) p", p=128), 132),
                 ("fcb", ffn_conv_b[l].rearrange("(c p) -> c p", p=128), 44),
                 ("cw", conv_w[l].rearrange("k (c p) -> (k c) p", p=128), 62)]
        o = 0
        for name, ap, n in items:
            CVOFF[name] = o
            o += n
        assert o == NCV
        load_cols(CV[:, l, :], [(ap, n) for _, ap, n in items])
    if stop is None and "S" in segs:
        pref[("S", 0)] = load_wA(segS_, 0)
    for job in jobs:
        job.finish()
    modc = lambda l, s, k, g: MODT[:, l, s * 8 + k, g:g + 1]
    kb.barrier()
    if "modT" in dbg:
        kb.out_stamps.append(kb.dma("sp", dbg["modT"], MODT[:].rearrange("p l c g -> p (l c g)"), "dbg"))
        kb.out_stamps.append(kb.dma("sp", dbg["cv"], CV[:].rearrange("p l c -> p (l c)"), "dbg"))

    segP = Seg("P", 1024, 4, False, 0, xp, yp)
    segS = Seg("S", 2048, 1, True, 1, xs, ys)
    try:
        chk("mod")
        for sg in (segS, segP):
            if sg.name not in segs:
                continue
            for l in range(DEPTH):
                order = [q for q in (segS, segP) if q.name in segs]
                if l < DEPTH - 1:
                    nxt = (sg, l + 1)
                else:
                    qi = order.index(sg)
                    nxt = (order[qi + 1], 0) if qi + 1 < len(order) else None
                if stop is not None:
                    nxt = None
                run_layer(sg, l, l == 0, l == DEPTH - 1, nxt)
                chk(f"E_{sg.name}{l}")
    except StopBuild:
        pass
    kb.barrier()
    return nc


def host_consts():
    bf = ml_dtypes.bfloat16
    c = {}
    c["c_identb"] = np.eye(128, dtype=np.float32).astype(bf)
    c["c_identf"] = np.eye(128, dtype=np.float32)
    j = np.arange(128)[:, None]; i = np.arange(128)[None, :]
    mL = np.where(j >= i, 0.0, NEG); mR = np.where(j <= i, 0.0, NEG)
    c["c_mask"] = np.stack([mL, mR]).astype(np.float32).astype(bf)
    c["c_mask01"] = np.concatenate([(j >= i), np.ones((128, 128), bool), (j <= i)], 1).astype(np.float32).astype(bf)
    pos = np.arange(2048)
    row = (pos // 64).astype(np.float32); col = (pos % 64).astype(np.float32)
    inv = (10000.0 ** (-np.arange(16, dtype=np.float32) / 16)).astype(np.float32)
    ang = np.concatenate([row[:, None] * inv, col[:, None] * inv], -1).astype(np.float32)
    cs = np.stack([np.cos(ang), np.sin(ang)]).astype(np.float32)
    c["c_rope"] = np.ascontiguousarray(cs.reshape(2, 16, 128, 32).transpose(0, 2, 1, 3))
    def dft(Ln):
        t = np.arange(Ln, dtype=np.int64)
        ph = (2.0 * np.pi / Ln) * ((t[:, None] * t[None, :]) % Ln).astype(np.float64)
        return np.stack([np.cos(ph), np.sin(ph)])
    c["c_dft256"] = dft(256).astype(np.float32).astype(bf)
    d2k = dft(2048).astype(np.float32).astype(bf)
    d2k = d2k.reshape(2, 4, 4, 128, 4, 512).transpose(4, 1, 3, 0, 2, 5)
    c["c_dft2k"] = np.ascontiguousarray(d2k).reshape(4, 4, 128, 2 * 4 * 512)
    d64 = dft(64)
    blocks = []
    for Ln in (256, 2048):
        nrm = 1.0 / np.sqrt(Ln * 64.0)
        bc = np.zeros((128, 128)); bs = np.zeros((128, 128))
        for gq in range(2):
            bc[gq * 64:(gq + 1) * 64, gq * 64:(gq + 1) * 64] = d64[0] * nrm
            bs[gq * 64:(gq + 1) * 64, gq * 64:(gq + 1) * 64] = -d64[1] * nrm
        blocks += [bc, bs]
    c["c_bcs"] = np.stack(blocks).astype(np.float32).astype(bf)
    return c


_NC_CACHE = {}


def kernel(x_prompt, x_sample, cache_k, cache_v, c, c_ctx, w_mod, b_mod, w_in, b_in, sink, w_attn_o, conv_w, conv_b,
           conv_ln_g, conv_ln_b, w_conv_o, w_fnet, b_fnet, w_o, ln1_g, ln1_b, w_up, ffn_conv_w, ffn_conv_b, w_down,
           ln2_g, ln2_b, _debug=None):
    f = lambda a: np.ascontiguousarray(np.asarray(a, dtype=np.float32))
    _stop = None
    _segs = "PS"
    _ncores = 8
    if _debug:
        _debug = dict(_debug)
        _stop = _debug.pop("_stop", None)
        _segs = _debug.pop("_segs", "PS")
        _ncores = _debug.pop("_ncores", 8)
    key = (str(sorted(_debug.items())) if _debug else None, _stop, _segs)
    if key not in _NC_CACHE:
        _NC_CACHE[key] = build(_debug, _stop, _segs)
    nc = _NC_CACHE[key]
    consts = host_consts()
    shared = dict(w_mod=f(w_mod), b_mod=f(b_mod), w_in=f(w_in), b_in=f(b_in), sink=f(sink), w_attn_o=f(w_attn_o),
                  conv_w=f(conv_w), conv_b=f(conv_b), conv_ln_g=f(conv_ln_g), conv_ln_b=f(conv_ln_b), w_conv_o=f(w_conv_o),
                  w_fnet=f(w_fnet), b_fnet=f(b_fnet), w_o=f(w_o), ln1_g=f(ln1_g), ln1_b=f(ln1_b), w_up=f(w_up),
                  ffn_conv_w=f(ffn_conv_w), ffn_conv_b=f(ffn_conv_b), w_down=f(w_down), ln2_g=f(ln2_g), ln2_b=f(ln2_b))
    shared.update(consts)
    xp = f(x_prompt); xs = f(x_sample); ck_ = f(cache_k); cv_ = f(cache_v); cc = f(c); cx = f(c_ctx)
    in_maps = []
    for b in range(_ncores):
        m = dict(shared)
        m["xp"] = np.ascontiguousarray(xp[4 * b:4 * b + 4].reshape(1024, D))
        m["xs"] = np.ascontiguousarray(xs[b])
        m["ck"] = np.ascontiguousarray(ck_[b].reshape(DEPTH, 256, 128))
        m["cv"] = np.ascontiguousarray(cv_[b].reshape(DEPTH, 256, 128))
        m["cvec"] = np.ascontiguousarray(np.stack([cx, cc[b]]))
        in_maps.append(m)
    res = run_bass_kernel_spmd(nc, in_maps, core_ids=list(range(_ncores)))
    R = res.results
    if _debug is not None:
        kernel._dbg = [{k: v for k, v in r.items()} for r in R]
        return None
    y_p = np.concatenate([r["yp"].reshape(4, 256, D) for r in R], 0)
    y_s = np.stack([r["ys"] for r in R], 0)
    nk_ = np.concatenate([r["nk"].reshape(4, DEPTH, 256, 2, 64) for r in R], 0)
    nv_ = np.concatenate([r["nv"].reshape(4, DEPTH, 256, 2, 64) for r in R], 0)
    if _debug:
        kernel._dbg = [{k: v for k, v in r.items() if k.startswith("dbg_")} for r in R]
    return (y_p.astype(np.float32), y_s.astype(np.float32), nk_.astype(np.float32), nv_.astype(np.float32))
```

```python
import numpy as np
import ml_dtypes
import concourse.bass as bass
import concourse.mybir as mybir
from concourse.bass_utils import run_bass_kernel_spmd

F32 = mybir.dt.float32
BF16 = mybir.dt.bfloat16
AF = mybir.ActivationFunctionType
ALU = mybir.AluOpType

D = 1024
DEPTH = 2
NIN = 4608
DFF = 2816
NFF = 22
ALPHA = float((2 * DEPTH) ** 0.25)
EPS = 1e-6
FFN_GROUPS = [(0, 6), (6, 6), (12, 5), (17, 5)]
NEG = -30000.0
import os
P0_LEVEL = int(os.environ.get('P0_LEVEL', '9'))
EVAC_MODE = int(os.environ.get('EVAC_MODE', '0'))


class Eng:
    def __init__(self, name, e, sem):
        self.name, self.e, self.sem = name, e, sem
        self.count = 0
        self.seen = {}


class Cell:
    __slots__ = ("w", "r")

    def __init__(self):
        self.w = None
        self.r = {}


def cells(*dims):
    if len(dims) == 1:
        return [Cell() for _ in range(dims[0])]
    return [cells(*dims[1:]) for _ in range(dims[0])]


class KB:
    def __init__(self, nc):
        self.nc = nc
        self.engs = {}
        self.sems = {}
        for name, e in (("pe", nc.tensor), ("act", nc.scalar), ("dve", nc.vector),
                        ("pool", nc.gpsimd), ("sp", nc.sync)):
            sem = nc.semaphore("s_" + name).__enter__()
            self.engs[name] = Eng(name, e, sem)
            self.sems[name] = sem
        self.dcount = {}
        for key in ["const", "stg", "stg1", "stg2", "w0", "w1", "w2", "g0", "g1", "dbg"] + [f"x{i}" for i in range(8)]:
            self.sems[key] = nc.semaphore("d_" + key).__enter__()
            self.dcount[key] = 0
        self.banks = [nc.psum_tensor(f"ps{i}", [128, 512], F32).__enter__() for i in range(8)]
        self.bcells = cells(8)
        self.bi = 0
        self.reserved = set()
        self.out_stamps = []

    def bank(self):
        while self.bi in self.reserved:
            self.bi = (self.bi + 1) % 8
        i = self.bi
        self.bi = (self.bi + 1) % 8
        return self.banks[i], self.bcells[i]

    def reserve(self):
        b, c = self.bank()
        i = self.banks.index(b)
        self.reserved.add(i)
        return b, c, i

    def _wait(self, eng, key, val, war=False):
        if key == eng.name and (war or eng.name in ("pe", "sp")):
            return
        if eng.seen.get(key, 0) >= val:
            return
        have = self.engs[key].count if key in self.engs else self.dcount[key]
        assert val <= have, ("waiting on un-emitted signal", eng.name, key, val, have)
        eng.e.wait_ge(self.sems[key], val)
        eng.seen[key] = val

    def _deps(self, eng, reads, writes):
        for c in reads:
            if c.w is not None:
                self._wait(eng, *c.w)
        for c in writes:
            if c.w is not None:
                self._wait(eng, *c.w)
            for k, v in c.r.items():
                self._wait(eng, k, v, war=True)

    def _stamp(self, stamp, reads, writes):
        for c in reads:
            if c.r.get(stamp[0], 0) < stamp[1]:
                c.r[stamp[0]] = stamp[1]
        for c in writes:
            c.w = stamp
            c.r = {}

    def op(self, en, fn, reads=(), writes=(), signal=True):
        eng = self.engs[en]
        self._deps(eng, reads, writes)
        ins = fn(eng.e)
        if signal:
            eng.count += 1
            ins.then_inc(eng.sem, 1)
            stamp = (en, eng.count)
        else:
            stamp = (en, eng.count + 1)
        self._stamp(stamp, reads, writes)
        return ins

    def dma(self, q, out, in_, key, reads=(), writes=(), **kw):
        eng = self.engs[q]
        assert key in self.sems, key
        self._deps(eng, reads, writes)
        ins = eng.e.dma_start(out=out, in_=in_, **kw)
        self.dcount[key] += 16
        ins.then_inc(self.sems[key], 16)
        stamp = (key, self.dcount[key])
        self._stamp(stamp, reads, writes)
        return stamp

    def barrier(self):
        for en, eng in self.engs.items():
            for fn, f in self.engs.items():
                if f.count > 0:
                    self._wait(eng, fn, f.count)
            for key, cnt in self.dcount.items():
                if cnt > 0:
                    self._wait(eng, key, cnt)


def carve(R, off, shape, dt=BF16):
    n = int(np.prod(shape[1:]))
    nb = n * 2 if dt == F32 else n
    assert off % 2 == 0 and off + nb <= R.shape[1], (off, nb, R.shape)
    v = R[:, off:off + nb]
    if dt == F32:
        v = v.bitcast(F32)
    if len(shape) == 3:
        v = v.rearrange("p (a b) -> p a b", a=shape[1], b=shape[2])
    elif len(shape) == 4:
        v = v.rearrange("p (a b c) -> p a b c", a=shape[1], b=shape[2], c=shape[3])
    return v


class Seg:
    def __init__(self, name, T, nseq, latent, g, xin, yout):
        self.name, self.T, self.nseq, self.latent, self.g = name, T, nseq, latent, g
        self.L = T // nseq
        self.NT = T // 128
        self.NN = T // 512
        self.xin, self.yout = xin, yout
        self.gpad = self.L + 30
        self.upad = self.L + 2


class StopBuild(Exception):
    pass


def build(debug=None, stop=None, segs="PS"):
    nc = bass.Bass("TRN2", target_bir_lowering=False)
    dt_in = lambda name, shape, dt=F32: nc.dram_tensor(name, list(shape), dt, kind="ExternalInput").ap()
    dt_out = lambda name, shape, dt=F32: nc.dram_tensor(name, list(shape), dt, kind="ExternalOutput").ap()
    xp = dt_in("xp", [1024, D]); xs = dt_in("xs", [2048, D])
    ck = dt_in("ck", [DEPTH, 256, 128]); cv = dt_in("cv", [DEPTH, 256, 128])
    cvec = dt_in("cvec", [2, D])
    w_mod = dt_in("w_mod", [DEPTH, D, 6 * D]); b_mod = dt_in("b_mod", [DEPTH, 6 * D])
    w_in = dt_in("w_in", [DEPTH, D, NIN]); b_in = dt_in("b_in", [DEPTH, NIN])
    sink = dt_in("sink", [DEPTH, 8])
    w_attn_o = dt_in("w_attn_o", [DEPTH, 512, D])
    conv_w = dt_in("conv_w", [DEPTH, 31, 256]); conv_b = dt_in("conv_b", [DEPTH, 256])
    conv_ln_g = dt_in("conv_ln_g", [DEPTH, 256]); conv_ln_b = dt_in("conv_ln_b", [DEPTH, 256])
    w_conv_o = dt_in("w_conv_o", [DEPTH, 256, D]); w_fnet = dt_in("w_fnet", [DEPTH, 256, D])
    b_fnet = dt_in("b_fnet", [DEPTH, D]); w_o = dt_in("w_o", [DEPTH, D, D])
    ln1_g = dt_in("ln1_g", [DEPTH, D]); ln1_b = dt_in("ln1_b", [DEPTH, D])
    w_up = dt_in("w_up", [DEPTH, D, 2 * DFF]); ffn_conv_w = dt_in("ffn_conv_w", [DEPTH, 3, 2 * DFF])
    ffn_conv_b = dt_in("ffn_conv_b", [DEPTH, 2 * DFF]); w_down = dt_in("w_down", [DEPTH, DFF, D])
    ln2_g = dt_in("ln2_g", [DEPTH, D]); ln2_b = dt_in("ln2_b", [DEPTH, D])
    c_identb = dt_in("c_identb", [128, 128], BF16); c_identf = dt_in("c_identf", [128, 128])
    c_mask = dt_in("c_mask", [2, 128, 128], BF16)
    c_mask01 = dt_in("c_mask01", [128, 384], BF16)
    c_rope = dt_in("c_rope", [2, 128, 16, 32])
    c_dft256 = dt_in("c_dft256", [2, 256, 256], BF16)
    c_dft2k = dt_in("c_dft2k", [4, 4, 128, 2 * 4 * 512], BF16)
    c_bcs = dt_in("c_bcs", [4, 128, 128], BF16)
    yp = dt_out("yp", [1024, D]); ys = dt_out("ys", [2048, D])
    nk = dt_out("nk", [4, DEPTH, 256, 128]); nv = dt_out("nv", [4, DEPTH, 256, 128])
    sc = lambda name, shape: nc.dram_tensor(name, list(shape), BF16, kind="Internal").ap()
    scA = sc("scA", [DEPTH, 128, 8192]); scB = sc("scB", [DEPTH, 1, 1024]); scC = sc("scC", [DEPTH, 128, 4096])
    scG = sc("scG", [DEPTH, 8, 128, 4096]); scU = sc("scU", [DEPTH, NFF, 128, 2048])
    dbg = {}
    if debug:
        for name, shape in debug.items():
            dbg[name] = dt_out("dbg_" + name, shape)

    kb = KB(nc)
    sb = lambda name, shape, dt=BF16: nc.sbuf_tensor(name, list(shape), dt).__enter__()
    X = sb("X", [128, 16, D], F32)
    HT = sb("HT", [128, 8, 2048])
    RA = sb("RA", [128, 21504])
    RB = sb("RB", [128, 12288])
    RW = sb("RW", [128, 14336])
    identb = sb("identb", [128, 128]); identf = sb("identf", [128, 128], F32)
    maskc = sb("maskc", [128, 2, 128])
    mask01 = sb("mask01", [128, 384])
    onesb = sb("onesb", [128, 128]); onesf = sb("onesf", [128, 128], F32); onesd = sb("onesd", [128, 128], F32)
    rope = sb("rope", [128, 2, 16, 32], F32)
    dft256 = sb("dft256", [128, 2, 2, 256])
    bcs = sb("bcs", [128, 4, 128])
    NCV = 328
    CV = sb("CV", [128, DEPTH, NCV], F32)
    MODT = sb("MODT", [128, DEPTH, 48, 2], F32)
    scT = sb("scT", [128, 8, 2])
    esink = sb("esink", [128, DEPTH, 4], F32)
    e8 = sb("e8", [128, 8], F32)
    stg = sb("stg", [128, 128], F32)
    e8r = sb("e8r", [1, DEPTH, 8], F32)
    onesr = sb("onesr", [1, 256])
    c_e8r = Cell()
    dgf = sb("dgf", [128, 128], F32)
    c_dgf = Cell()
    stats = sb("stats", [128, 4, 12], F32)
    mv = sb("mv", [128, 4, 4], F32)
    c_const = cells(1)[0]
    c_stg = Cell(); c_cv = Cell(); c_mod = Cell(); c_sct = Cell(); c_es = Cell()
    c_stats = cells(4)
    c_x = cells(16)
    c_ht = cells(8, 16)

    op, dma, bank = kb.op, kb.dma, kb.bank

    dma("sp", identb[:], c_identb, "const", writes=[c_const])
    dma("sp", identf[:], c_identf, "const", writes=[c_const])
    dma("sp", maskc[:], c_mask.rearrange("m p q -> p m q"), "const", writes=[c_const])
    dma("sp", mask01[:], c_mask01, "const", writes=[c_const])
    dma("sp", rope[:], c_rope.rearrange("c p t d -> p c t d"), "const", writes=[c_const])
    dma("sp", dft256[:], c_dft256.rearrange("c (k p) n -> p c k n", p=128), "const", writes=[c_const])
    dma("sp", bcs[:], c_bcs.rearrange("m p q -> p m q"), "const", writes=[c_const])
    op("dve", lambda e: e.memset(onesb[:], 1.0), writes=[c_const])
    op("dve", lambda e: e.memset(onesf[:], 1.0), writes=[c_const])
    op("dve", lambda e: e.memset(onesd[:], 1.0 / 256.0), writes=[c_const])
    epsc = sb("epsc", [128, 2], F32)
    op("dve", lambda e: e.memset(epsc[:], EPS), writes=[c_const])

    stgs = [stg, sb("stg1", [128, 128], F32), sb("stg2", [128, 128], F32)]
    stg_r = cells(3)
    lc_state = {"k": 0}

    def load_cols(dst, items):
        col = 0
        rows = 0
        bufc = []
        def flush():
            nonlocal rows, col, bufc
            if not rows:
                return
            k = lc_state["k"]
            pb, pc = bank()
            op("pe", lambda e: e.transpose(out=pb[:, 0:rows], in_=stgs[k][0:rows, :], identity=identf[0:rows, 0:rows]),
               reads=[stg_r[k], c_const] + bufc, writes=[pc])
            c0 = col
            op("dve", lambda e: e.tensor_copy(out=dst[:, c0:c0 + rows], in_=pb[:, 0:rows]), reads=[pc], writes=[c_cv])
            col += rows
            rows = 0
            bufc = []
            lc_state["k"] = (k + 1) % 3
        for ap, n in items:
            done = 0
            while done < n:
                k = lc_state["k"]
                take = min(n - done, 128 - rows)
                r0 = rows
                kb._deps(kb.engs["sp"], [], [stg_r[k]])
                fc = Cell()
                dma("sp", stgs[k][r0:r0 + take, :], ap[done:done + take, :], ("stg", "stg1", "stg2")[k], writes=[fc])
                bufc.append(fc)
                rows += take
                done += take
                if rows == 128:
                    flush()
        flush()
        return col

    CVOFF = {}
    _o = 0
    for _name, _n in (("bmod", 48), ("bconv", 4), ("bgate", 24), ("convb", 2), ("lng", 2), ("lnb", 2), ("bfnet", 8),
                      ("fcw", 132), ("fcb", 44), ("cw", 62)):
        CVOFF[_name] = _o
        _o += _n
    assert _o == NCV
    dma("sp", stgs[2][0:16, :], cvec.rearrange("g (k p) -> (g k) p", p=128), "stg2", writes=[c_stg])
    pb, pc = bank()
    op("pe", lambda e: e.transpose(out=pb[:, 0:16], in_=stgs[2][0:16, :], identity=identf[0:16, 0:16]),
       reads=[c_stg, c_const, stg_r[2]], writes=[pc])
    op("act", lambda e: e.activation(out=scT[:], in_=pb[:, 0:16].rearrange("p (g k) -> p k g", g=2), func=AF.Silu),
       reads=[pc], writes=[c_sct])
    dma("sp", e8r[:], sink.rearrange("(o l) h -> o l h", o=1), "g1", writes=[c_e8r])
    op("act", lambda e: e.activation(out=e8r[:], in_=e8r[:], func=AF.Exp), reads=[c_e8r], writes=[c_e8r])
    op("dve", lambda e: e.memset(onesr[:], 1.0), writes=[c_const])
    class ModJob:
        def __init__(self, l, slots, keys, pm, pmc):
            self.l, self.slots, self.keys = l, slots, keys
            self.cw = cells(len(slots))
            self.pm, self.pmc = pm, pmc
            self.pmv = pm[:, 0:96].rearrange("p (c g) -> p c g", g=2)
            self.nload = 0
            self.ndone = 0

        def load(self):
            if self.nload >= 12:
                return
            blk = self.nload
            s = blk % len(self.slots)
            dma("pool", self.slots[s], w_mod[self.l, :, blk * 512:(blk + 1) * 512].rearrange("(k p) n -> p k n", p=128),
                self.keys[s], writes=[self.cw[s]])
            self.nload += 1

        def step(self):
            if self.ndone >= 12:
                return
            blk = self.ndone
            while self.nload <= min(blk + len(self.slots) - 1, 11):
                self.load()
            s = blk % len(self.slots)
            for cc in range(4):
                ch = blk * 4 + cc
                for k in range(8):
                    op("pe", lambda e: e.matmul(out=self.pmv[:, ch, :], lhsT=self.slots[s][:, k, cc * 128:(cc + 1) * 128],
                                                rhs=scT[:, k, :], start=(k == 0), stop=(k == 7)),
                       reads=[self.cw[s], c_sct], writes=[self.pmc], signal=(k == 7 and cc == 3))
            self.ndone += 1

        def finish(self):
            while self.ndone < 12:
                self.step()
            l = self.l
            bm = CV[:, l, CVOFF["bmod"]:CVOFF["bmod"] + 48].unsqueeze(2).broadcast_to([128, 48, 2])
            op("dve", lambda e: e.tensor_tensor(out=MODT[:, l, :, :], in0=self.pmv, in1=bm, op=ALU.add),
               reads=[self.pmc, c_cv], writes=[c_mod])
            for base in (8, 32):
                op("dve", lambda e: e.tensor_scalar_add(out=MODT[:, l, base:base + 8, :], in0=MODT[:, l, base:base + 8, :],
                                                        scalar1=1.0), reads=[c_mod], writes=[c_mod])

    wm = [carve(RB, s * 4096, [128, 8, 512]) for s in range(3)]
    mod_late = (DEPTH - 1) if ("S" in segs and stop is None) else None
    jobs = []
    for l in range(DEPTH):
        if l == mod_late:
            continue
        pm, pmc = bank()
        job = ModJob(l, wm, ["w0", "w1", "w2"], pm, pmc)
        for _ in range(12):
            job.step()
        jobs.append(job)
    cvc = lambda l, name, i=0: CV[:, l, CVOFF[name] + i:CVOFF[name] + i + 1]

    def chk(tag):
        if stop == tag:
            raise StopBuild()

    def dump(name, src, reads=()):
        if name in dbg:
            st = dma("pool", dbg[name], src, "dbg", reads=list(reads))
            kb.out_stamps.append(st)

    ring = {"st": 0}
    MV = sb("MV", [128, 2, 16, 4], F32)
    c_mvs = cells(2, 4)
    c_mvt = cells(2, 16)

    c_stats2 = cells(4, 2)

    def stats_block(tiles, si):
        rs = []
        for i in tiles:
            r = ring["st"] % 4
            ring["st"] += 1
            rs.append(r)
            for h in range(2):
                op("dve", lambda e: e.bn_stats(out=stats[:, r, h * 6:(h + 1) * 6], in_=X[:, i, h * 512:(h + 1) * 512]),
                   reads=[c_x[i]], writes=[c_stats2[r][h]])
        for i, r in zip(tiles, rs):
            op("dve", lambda e: e.bn_aggr(out=MV[:, si, i, 0:2], in_=stats[:, r, :]), reads=c_stats2[r], writes=[c_mvt[si][i]])

    def stats_finish(si, i0, i1):
        c = c_mvs[si][i0 // 4]
        op("act", lambda e: e.activation(out=MV[:, si, i0:i1, 2], in_=MV[:, si, i0:i1, 1], func=AF.Sqrt, bias=epsc[:, 0:1]),
           reads=[c_const] + c_mvt[si][i0:i1], writes=[c])
        op("dve", lambda e: e.reciprocal(out=MV[:, si, i0:i1, 2], in_=MV[:, si, i0:i1, 2]), reads=[c], writes=[c])
        op("dve", lambda e: e.scalar_tensor_tensor(out=MV[:, si, i0:i1, 3], in0=MV[:, si, i0:i1, 0], scalar=-1.0,
                                                   in1=MV[:, si, i0:i1, 2], op0=ALU.mult, op1=ALU.mult),
           reads=[c] + c_mvt[si][i0:i1], writes=[c])

    def affine_block(tiles, lg, lb, c_l):
        for i in tiles:
            op("act", lambda e: e.activation(out=X[:, i, :], in_=X[:, i, :], func=AF.Identity, bias=MV[:, 0, i, 3:4],
                                             scale=MV[:, 0, i, 2:3]), reads=[c_mvs[0][i // 4]], writes=[c_x[i]])
        for i in tiles:
            op("dve", lambda e: e.tensor_tensor(out=X[:, i, :], in0=X[:, i, :], in1=lg, op=ALU.mult),
               reads=[c_l], writes=[c_x[i]])
        for i in tiles:
            op("dve", lambda e: e.tensor_tensor(out=X[:, i, :], in0=X[:, i, :], in1=lb, op=ALU.add),
               reads=[c_l], writes=[c_x[i]])

    def ln_pipeline(NT, s1, s2, s3):
        nb = NT // 4
        for step in range(nb + 2):
            if step < nb:
                s1(step)
            if 0 <= step - 1 < nb and s2 is not None:
                s2(step - 1)
            if 0 <= step - 2 < nb and s3 is not None:
                s3(step - 2)

    xnb_state = {"i": 0}

    def ht_block(sg, l, s_shift, xnb, c_xnb, tiles):
        assert len(xnb) >= len(tiles)
        rr = []
        for i in tiles:
            r = xnb_state["i"] % len(xnb)
            xnb_state["i"] += 1
            rr.append(r)
            op("act", lambda e: e.activation(out=xnb[r], in_=X[:, i, :], func=AF.Identity, bias=MV[:, 1, i, 3:4],
                                             scale=MV[:, 1, i, 2:3]), reads=[c_x[i], c_mvs[1][i // 4]], writes=[c_xnb[r]])
        bks = []
        for i, r in zip(tiles, rr):
            pb, pc = bank()
            pbv = pb[:].bitcast(BF16).rearrange("p (k q) -> p k q", k=8)
            bks.append((pbv, pc))
            for k in range(8):
                op("pe", lambda e: e.transpose(out=pbv[:, k, :], in_=xnb[r][:, k * 128:(k + 1) * 128], identity=identb[:]),
                   reads=[c_xnb[r], c_const], writes=[pc], signal=(k == 7))
        for i, (pbv, pc) in zip(tiles, bks):
            for k in range(8):
                dst = HT[:, k, i * 128:(i + 1) * 128]
                sc_ = modc(l, s_shift + 1, k, sg.g)
                sh_ = modc(l, s_shift, k, sg.g)
                if i % 2 == 0:
                    op("act", lambda e: e.activation(out=dst, in_=pbv[:, k, :], func=AF.Identity, bias=sh_, scale=sc_),
                       reads=[pc, c_mod], writes=[c_ht[k][i]])
                else:
                    op("dve", lambda e: e.tensor_scalar(out=dst, in0=pbv[:, k, :], scalar1=sc_, scalar2=sh_,
                                                        op0=ALU.mult, op1=ALU.add),
                       reads=[pc, c_mod], writes=[c_ht[k][i]])

    pref = {}
    wA = carve(RW, 0, [128, 8, 1024]); brow = carve(RW, 8192, [128, 1024])
    wA_flat = carve(RW, 0, [128, 8192])

    def load_wA(sg, l, extra=()):
        c_wA = Cell()
        ex = list(extra)
        if sg.latent:
            dma("pool", wA[:, :, 0:768], w_in[l, :, 0:768].rearrange("(k p) n -> p k n", p=128), "w0", writes=[c_wA] + ex)
            dma("pool", wA[:, :, 768:1024], w_in[l, :, 1280:1536].rearrange("(k p) n -> p k n", p=128), "w0", writes=[c_wA])
            dma("pool", brow[0:1, 0:768], b_in[l, 0:768].rearrange("(o n) -> o n", o=1), "w0", writes=[c_wA])
            dma("pool", brow[0:1, 768:1024], b_in[l, 1280:1536].rearrange("(o n) -> o n", o=1), "w0", writes=[c_wA])
        else:
            dma("sp", wA_flat, scA[l], "w0", writes=[c_wA] + ex)
            dma("sp", brow[0:1, :], scB[l], "w0", writes=[c_wA])
        return c_wA

    def run_layer(sg, l, first, last, nxt=None):
        T, NT, NN, L, nseq, g = sg.T, sg.NT, sg.NN, sg.L, sg.nseq, sg.g
        TPS = L // 128
        if first:
            xnb = [carve(RB, r * 1024, [128, 1024]) for r in range(4)]
            c_xnb = cells(4)
            if pre0.pop(sg.name, False):
                for b_ in range(NT // 4):
                    ht_block(sg, l, 0, xnb, c_xnb, list(range(4 * b_, 4 * b_ + 4)))
            else:
                for i in range(NT):
                    dma("sp", X[:, i, :], sg.xin[i * 128:(i + 1) * 128, :], f"x{i % 8}", reads=([c_x[i - 8]] if i >= 8 else []), writes=[c_x[i]])
                def p0_s1(b):
                    stats_block(list(range(4 * b, 4 * b + 4)), 1)
                    stats_finish(1, 4 * b, 4 * b + 4)
                ln_pipeline(NT, p0_s1, lambda b: ht_block(sg, l, 0, xnb, c_xnb, list(range(4 * b, 4 * b + 4))), None)
            kb.barrier()
        dump(f"ht_{sg.name}{l}", HT[:, :, 0:T], [c for row in c_ht for c in row])
        chk(f"p0_{sg.name}{l}")

        qT = carve(RA, 0, [128, 4, 2048]); kT = carve(RA, 8192, [128, 2048]); kcT = carve(RA, 10240, [128, 256])
        vtok = carve(RA, 10496, [128, 18, 128]); ftok = carve(RA, 12800, [128, 16, 256])
        gluT = carve(RA, 16896, [128, 2, 2304])
        c_qa = cells(4, 16)
        c_kt = cells(16); c_kc = Cell(); c_v = cells(18); c_f = cells(16); c_glu = cells(2, 4)
        wC = carve(RW, 9216, [128, 8, 512])
        c_wC = Cell()
        wC_flat = carve(RW, 9216, [128, 4096])
        c_wA = pref.pop((sg.name, l)) if (sg.name, l) in pref else load_wA(sg, l)
        if sg.latent:
            dma("pool", wC, w_in[l, :, 768:1280].rearrange("(k p) n -> p k n", p=128), "w1", writes=[c_wC])
            dma("sp", scA[l], wA_flat, "g0", reads=[c_wA])
            dma("sp", scB[l], brow[0:1, :], "g0", reads=[c_wA])
            dma("sp", scC[l], wC_flat, "g0", reads=[c_wC])
        else:
            dma("sp", wC_flat, scC[l], "w1", writes=[c_wC])
        if sg.latent:
            kctok = carve(RB, 0, [128, 2, 128])
            c_kct = Cell()
            dma("pool", kctok, ck[l].rearrange("(t p) n -> p t n", p=128), "w2", writes=[c_kct])
            dma("pool", vtok[:, 16:18, :], cv[l].rearrange("(t p) n -> p t n", p=128), "g1", writes=[c_v[16], c_v[17]])
        op("pool", lambda e: e.memset(gluT, 0.0), writes=[c for row in c_glu for c in row])

        qtok = [carve(RB, 1024 + s * 2560, [128, 4, 640]) for s in range(2)]
        c_qtok = cells(2)
        tA = carve(RB, 6144, [128, 640], F32); tB = carve(RB, 7424, [128, 640], F32)
        c_tAB = Cell()
        kvo = [carve(RB, 8704 + s * 512, [128, 256], F32) for s in range(2)]
        c_kvo = cells(2)
        cosb = lambda i, h: rope[:, 0, i, :].unsqueeze(1).unsqueeze(1).broadcast_to([128, h, 2, 32])
        sinb = lambda i, h: rope[:, 1, i, :].unsqueeze(1).unsqueeze(1).broadcast_to([128, h, 2, 32])
        for n in range(NN):
            s = n % 2
            for t in range(4):
                i = n * 4 + t
                bq, cq = bank()
                br, cr = bank()
                for bb, cb, c0 in ((bq, cq, 0), (br, cr, 512)):
                    for k in range(8):
                        op("pe", lambda e: e.matmul(out=bb[:], lhsT=HT[:, k, i * 128:(i + 1) * 128],
                                                    rhs=wA[:, k, c0:c0 + 512], start=(k == 0), stop=False),
                           reads=[c_ht[k][i], c_wA], writes=[cb], signal=False)
                    op("pe", lambda e: e.matmul(out=bb[:], lhsT=onesb[0:1, :], rhs=brow[0:1, c0:c0 + 512],
                                                start=False, stop=True), reads=[c_wA, c_const], writes=[cb])
                qdst = qtok[s][:, t, 0:512]
                kdst = qtok[s][:, t, 512:640]
                if sg.latent:
                    for src, dst, h, w in ((bq[:], qdst, 8, 512), (br[:, 0:128], kdst, 2, 128)):
                        s4 = src.rearrange("p (h two d) -> p h two d", two=2, d=32)
                        a4 = tA[:, 0:w].rearrange("p (h two d) -> p h two d", two=2, d=32)
                        b4 = tB[:, 0:w].rearrange("p (h two d) -> p h two d", two=2, d=32)
                        d4 = dst.rearrange("p (h two d) -> p h two d", two=2, d=32)
                        rc = [cq if h == 8 else cr, c_const]
                        op("dve", lambda e: e.tensor_tensor(out=a4, in0=s4, in1=cosb(i, h), op=ALU.mult),
                           reads=rc, writes=[c_tAB])
                        op("dve", lambda e: e.tensor_tensor(out=b4, in0=s4, in1=sinb(i, h), op=ALU.mult),
                           reads=rc, writes=[c_tAB])
                        op("dve", lambda e: e.tensor_tensor(out=d4[:, :, 0, :], in0=a4[:, :, 0, :], in1=b4[:, :, 1, :],
                                                            op=ALU.subtract), reads=[c_tAB], writes=[c_qtok[s]])
                        op("dve", lambda e: e.tensor_tensor(out=d4[:, :, 1, :], in0=a4[:, :, 1, :], in1=b4[:, :, 0, :],
                                                            op=ALU.add), reads=[c_tAB], writes=[c_qtok[s]])
                else:
                    op("dve", lambda e: e.tensor_copy(out=qdst, in_=bq[:]), reads=[cq], writes=[c_qtok[s]])
                    op("act", lambda e: e.copy(out=kdst, in_=br[:, 0:128]), reads=[cr], writes=[c_qtok[s]])
                    ko = i % 2
                    op("act", lambda e: e.copy(out=kvo[ko], in_=br[:, 0:256]), reads=[cr], writes=[c_kvo[ko]])
                    sq, tt = i // TPS, i % TPS
                    st1 = dma("sp", nk[sq, l, tt * 128:(tt + 1) * 128, :], kvo[ko][:, 0:128], f"g{ko}", reads=[c_kvo[ko]])
                    st2 = dma("sp", nv[sq, l, tt * 128:(tt + 1) * 128, :], kvo[ko][:, 128:256], f"g{ko}", reads=[c_kvo[ko]])
                    kb.out_stamps.append(st2)
                if sg.latent:
                    op("dve", lambda e: e.tensor_copy(out=vtok[:, i, :], in_=br[:, 128:256]), reads=[cr], writes=[c_v[i]])
                    op("dve", lambda e: e.tensor_copy(out=ftok[:, i, :], in_=br[:, 256:512]), reads=[cr], writes=[c_f[i]])
                else:
                    op("act", lambda e: e.copy(out=vtok[:, i, :], in_=br[:, 128:256]), reads=[cr], writes=[c_v[i]])
                    op("act", lambda e: e.copy(out=ftok[:, i, :], in_=br[:, 256:512]), reads=[cr], writes=[c_f[i]])
            for grp in ((0, 1), (2, 3), (4,)):
                pb, pc = bank()
                pbv = pb[:].bitcast(BF16).rearrange("p (c t q) -> p c t q", c=2, t=4)
                for ci, c in enumerate(grp):
                    for t in range(4):
                        sig_ = (ci == len(grp) - 1 and t == 3)
                        if c < 4:
                            for hq in range(2):
                                cq0 = hq * 256 + c * 64
                                op("pe", lambda e: e.transpose(out=pbv[hq * 64:(hq + 1) * 64, ci, t, :], in_=qtok[s][:, t, cq0:cq0 + 64],
                                                               identity=identb[:]),
                                   reads=[c_qtok[s], c_const], writes=[pc], signal=(sig_ and hq == 1))
                        else:
                            op("pe", lambda e: e.transpose(out=pbv[:, ci, t, :], in_=qtok[s][:, t, 512:640], identity=identb[:]),
                               reads=[c_qtok[s], c_const], writes=[pc], signal=sig_)
                for ci, c in enumerate(grp):
                    src = pbv[:, ci, :, :].rearrange("p t q -> p (t q)")
                    if c < 4:
                        wr = [c_qa[c][n * 4 + t] for t in range(4)]
                        dst = qT[:, c, n * 512:(n + 1) * 512]
                    else:
                        wr = [c_kt[n * 4 + t] for t in range(4)]
                        dst = kT[:, n * 512:(n + 1) * 512]
                    if grp[0] != 2:
                        op("act", lambda e: e.copy(out=dst, in_=src), reads=[pc], writes=wr)
                    else:
                        op("dve", lambda e: e.tensor_copy(out=dst, in_=src), reads=[pc], writes=wr)
        sig = [carve(RB, 9728 + s * 1024, [128, 512], F32) for s in range(2)]
        c_sig = cells(2)
        si = 0
        for n in range(NN):
            bs = [bank() for _ in range(4)]
            for cidx in range(4):
                bb, cb = bs[cidx]
                for k in range(8):
                    op("pe", lambda e: e.matmul(out=bb[:], lhsT=wC[:, k, cidx * 128:(cidx + 1) * 128],
                                                rhs=HT[:, k, n * 512:(n + 1) * 512], start=(k == 0), stop=(k == 7)),
                       reads=c_ht[k][n * 4:n * 4 + 4] + [c_wC], writes=[cb], signal=(k == 7))
            for c in range(2):
                s = si % 2
                si += 1
                ba, ca = bs[c]
                bg_, cg_ = bs[2 + c]
                op("act", lambda e: e.activation(out=sig[s], in_=bg_[:], func=AF.Sigmoid, bias=cvc(l, "bconv", 2 + c)),
                   reads=[cg_, c_cv], writes=[c_sig[s]])
                nsq = 512 // L if L < 512 else 1
                if L >= 512:
                    dst = gluT[:, c, 15 + n * 512:15 + (n + 1) * 512]
                    in0 = ba[:]
                    in1 = sig[s]
                else:
                    dst = gluT[:, c, 0:nseq * sg.gpad].rearrange("p (s w) -> p s w", w=sg.gpad)[:, n * nsq:(n + 1) * nsq, 15:15 + L]
                    in0 = ba[:].rearrange("p (s w) -> p s w", w=L)
                    in1 = sig[s].rearrange("p (s w) -> p s w", w=L)
                op("dve", lambda e: e.scalar_tensor_tensor(out=dst, in0=in0, scalar=cvc(l, "bconv", c), in1=in1,
                                                           op0=ALU.add, op1=ALU.mult),
                   reads=[ca, c_sig[s], c_cv], writes=[c_glu[c][n]])
        if sg.latent:
            pb, pc = bank()
            pbv = pb[:].bitcast(BF16)
            for t in range(2):
                op("pe", lambda e: e.transpose(out=pbv[:, t * 128:(t + 1) * 128], in_=kctok[:, t, :], identity=identb[:]),
                   reads=[c_kct, c_const], writes=[pc], signal=(t == 1))
            op("dve", lambda e: e.tensor_copy(out=kcT, in_=pbv[:, 0:256]), reads=[pc], writes=[c_kc])
        kb.barrier()
        dump(f"qT_{sg.name}{l}", qT[:, :, 0:T]); dump(f"kT_{sg.name}{l}", kT[:, 0:T])
        dump(f"glu_{sg.name}{l}", gluT)
        dump(f"vtok_{sg.name}{l}", vtok); dump(f"ftok_{sg.name}{l}", ftok)
        chk(f"A_{sg.name}{l}")

        aoT = qT
        cuT = carve(RB, 0, [128, 2, 2048]); fmT = carve(RB, 4096, [128, 2, 2048])
        c_cu = cells(2, 4); c_fm = cells(2, 4)
        LA = 1 if (sg.latent and l == 0 and mod_late is not None) else 2
        NPT = 6 if LA == 2 else 4
        PTW = (384 if sg.latent else 640) if LA == 2 else 640
        pTr = [carve(RB, 8192 + s * PTW, [128, PTW]) for s in range(NPT)]
        c_pT = cells(NPT)
        if sg.latent:
            rec2 = [carve(RB, 10752, [128, 256], F32), carve(RB, 11264, [128, 256], F32)]
        else:
            rec2 = [carve(RW, 5632, [128, 256], F32), carve(RW, 6144, [128, 256], F32)]
        c_rec = cells(2)
        vAB = carve(RW, 0, [128, 18, 2, 128])
        srow = carve(RW, 4608, [128, 4, 2, 128])
        c_vab = Cell()
        nvt = 18 if sg.latent else NT
        op("dve", lambda e: e.memset(vAB, 1.0), writes=[c_vab])
        op("dve", lambda e: e.tensor_copy(out=vAB[:, 0:nvt, 0, 0:64], in_=vtok[:, 0:nvt, 0:64]), reads=c_v[0:nvt], writes=[c_vab])
        op("dve", lambda e: e.tensor_copy(out=vAB[:, 0:nvt, 1, 64:128], in_=vtok[:, 0:nvt, 64:128]), reads=c_v[0:nvt], writes=[c_vab])
        c_srow = cells(4, 2)
        op("dve", lambda e: e.memset(srow[0:1], 0.0), writes=[c_ for row in c_srow for c_ in row])
        for c in range(4):
            for hh in range(2):
                h_ = c + 4 * hh
                lo = 64 if hh == 0 else 0
                op("dve", lambda e: e.tensor_copy(out=srow[0:1, c, hh, lo:lo + 64], in_=e8r[0:1, l, h_:h_ + 1].broadcast_to([1, 64])),
                   reads=[c_e8r], writes=[c_srow[c][hh]])
        pti = 0
        QW = 128 if sg.latent else L
        nqb = T // QW
        pstate = {"pti": 0}

        def keychunks(qb):
            if sg.latent:
                kch = []
                if qb > 0:
                    kch.append(("loc", qb - 1, 0))
                kch.append(("loc", qb, None))
                if qb < nqb - 1:
                    kch.append(("loc", qb + 1, 1))
                kch += [("ctx", 0, None), ("ctx", 1, None)]
                return [kch[0:3], kch[3:]]
            return [[("loc", qb * TPS + t, None) for t in range(TPS)]]

        def s_stage(qb, c, hh):
            q0 = qb * QW
            r0 = hh * 64
            qcells_idx = range(q0 // 128, (q0 + QW) // 128)
            pts = []
            for grp in keychunks(qb):
                if not grp:
                    continue
                sbk, sbc = bank()
                s = pstate["pti"] % NPT
                pstate["pti"] += 1
                w = len(grp) * QW
                for j, (kind, kt_, mk) in enumerate(grp):
                    if kind == "loc":
                        lhs = kT[r0:r0 + 64, kt_ * 128:(kt_ + 1) * 128]
                        rd = [c_kt[kt_]]
                    else:
                        lhs = kcT[r0:r0 + 64, kt_ * 128:(kt_ + 1) * 128]
                        rd = [c_kc]
                    rd += [c_qa[c][x_] for x_ in qcells_idx]
                    op("pe", lambda e: e.matmul(out=sbk[:, j * QW:(j + 1) * QW], lhsT=lhs,
                                                rhs=qT[r0:r0 + 64, c, q0:q0 + QW], start=True, stop=True),
                       reads=rd, writes=[sbc], signal=(j == len(grp) - 1))
                op("act", lambda e: e.activation(out=pTr[s][:, 0:w], in_=sbk[:, 0:w], func=AF.Exp, scale=0.125),
                   reads=[sbc], writes=[c_pT[s]])
                for j, (kind, kt_, mk) in enumerate(grp):
                    if mk is not None:
                        m0 = 0 if mk == 0 else 256
                        op("dve", lambda e: e.tensor_tensor(out=pTr[s][:, j * QW:(j + 1) * QW], in0=pTr[s][:, j * QW:(j + 1) * QW],
                                                             in1=mask01[:, m0:m0 + 128], op=ALU.mult),
                           reads=[c_const], writes=[c_pT[s]])
                for j, (kind, kt_, mk) in enumerate(grp):
                    vt = kt_ if kind == "loc" else 16 + kt_
                    pts.append((s, j, vt))
            return pts

        obanks = {}

        def v_stage(qb, c, hh, pts):
            q0 = qb * QW
            qcells_idx = range(q0 // 128, (q0 + QW) // 128)
            ob, oc = bank()
            obanks[hh] = (ob, oc)
            for jj, (s, j, vt) in enumerate(pts):
                op("pe", lambda e: e.matmul(out=ob[:, 0:QW], lhsT=vAB[:, vt, hh, :], rhs=pTr[s][:, j * QW:(j + 1) * QW],
                                            start=(jj == 0), stop=False),
                   reads=[c_pT[s], c_vab], writes=[oc], signal=False)
            op("pe", lambda e: e.matmul(out=ob[:, 0:QW], lhsT=srow[0:1, c, hh, :], rhs=onesr[0:1, 0:QW], start=False, stop=True),
               reads=[c_srow[c][hh], c_const], writes=[oc])
            if hh == 1:
                for h2 in range(2):
                    r0 = h2 * 64
                    d0 = 64 - r0
                    ob2, oc2 = obanks[h2]
                    rr_ = rec2[h2]
                    op("act", lambda e: e.activation(out=rr_[r0:r0 + 64, 0:QW], in_=ob2[d0:d0 + 64, 0:QW], func=AF.Ln),
                       reads=[oc2], writes=[c_rec[h2]])
                    op("act", lambda e: e.activation(out=rr_[r0:r0 + 64, 0:QW], in_=rr_[r0:r0 + 64, 0:QW], func=AF.Exp, scale=-1.0),
                       reads=[c_rec[h2]], writes=[c_rec[h2]])
                    op("dve", lambda e: e.tensor_tensor(out=aoT[r0:r0 + 64, c, q0:q0 + QW], in0=ob2[r0:r0 + 64, 0:QW],
                                                        in1=rr_[r0:r0 + 64, 0:QW], op=ALU.mult),
                       reads=[oc2, c_rec[h2]], writes=[c_qa[c][x_] for x_ in qcells_idx])

        job = None
        if sg.latent and l == 0 and mod_late is not None:
            pm, pmc, pmi = kb.reserve()
            job = ModJob(mod_late, [carve(RW, 5632 + s_ * 4096, [128, 8, 512]) for s_ in range(2)], ["w1", "w2"], pm, pmc)
            job.load()
        pendq = []
        ui = 0
        for qb in range(nqb):
            for c in range(4):
                for hh in range(2):
                    pts = s_stage(qb, c, hh)
                    pendq.append((qb, c, hh, pts))
                    if len(pendq) > LA:
                        v_stage(*pendq.pop(0))
                    ui += 1
                    if job is not None and ui % 10 == 0:
                        job.step()
        while pendq:
            v_stage(*pendq.pop(0))
        if job is not None:
            job.finish()
            kb.reserved.discard(pmi)
        kb.barrier()
        dump(f"ao_{sg.name}{l}", aoT[:, :, 0:T])
        chk(f"B1_{sg.name}{l}")

        wG = [carve(RW, s * 4096, [128, 8, 3, 128]) for s in range(2)]
        wO3 = [carve(RW, s * 4096 + 3072, [128, 8, 128]) for s in range(2)]
        wGO_flat = [carve(RW, s * 4096, [128, 4096]) for s in range(2)]
        c_wG = cells(2)
        def load_wG(j):
            s = j % 2
            if sg.latent:
                for gi in range(3):
                    c0 = 1536 + gi * 1024 + j * 128
                    dma("pool", wG[s][:, :, gi, :], w_in[l, :, c0:c0 + 128].rearrange("(k p) n -> p k n", p=128),
                        f"w{s}", writes=[c_wG[s]])
                for hh in range(2):
                    dma("pool", wO3[s][hh * 64:(hh + 1) * 64, 0:4, :],
                        w_attn_o[l, hh * 256:(hh + 1) * 256, j * 128:(j + 1) * 128].rearrange("(c d) n -> d c n", d=64),
                        f"w{s}", writes=[c_wG[s]])
                dma("pool", wO3[s][:, 4:6, :], w_conv_o[l, :, j * 128:(j + 1) * 128].rearrange("(k p) n -> p k n", p=128),
                    f"w{s}", writes=[c_wG[s]])
                dma("pool", wO3[s][:, 6:8, :], w_fnet[l, :, j * 128:(j + 1) * 128].rearrange("(k p) n -> p k n", p=128),
                    f"w{s}", writes=[c_wG[s]])
            else:
                dma("sp", wGO_flat[s], scG[l, j], f"w{s}", writes=[c_wG[s]])

        dfs = [carve(RA, 8192, [128, 2, 4, 512]), carve(RW, 4096, [128, 2, 4, 512]), carve(RW, 8192, [128, 2, 4, 512])]
        dfs_keys = ["g0", "g1", "w2"]
        c_dfs = cells(3)
        if sg.latent:
            dma("sp", dfs[0].rearrange("p a b c -> p (a b c)"), c_dft2k[0, 0], dfs_keys[0], writes=[c_dfs[0]])
        dg = carve(RW, 0, [128, 2, 31, 128])
        c_dg = cells(2, 31)
        for c in range(2):
            for k in range(31):
                op("dve", lambda e: e.tensor_scalar(out=dg[:, c, k, :], in0=identb[:], scalar1=cvc(l, "cw", k * 2 + c),
                                                    scalar2=None, op0=ALU.mult), reads=[c_cv, c_const], writes=[c_dg[c][k]])
        uu = carve(RW, 7936, [128, 2, 512], F32); usq = carve(RW, 9984, [128, 2, 512], F32)
        stt = carve(RW, 12032, [128, 2, 512], F32)
        c_uu = Cell(); c_stt = Cell()
        for n in range(NN):
            cbs = [bank() for _ in range(2)]
            nsq = max(1, 512 // L)
            W = min(L, 512)
            for c in range(2):
                bb, cb = cbs[c]
                for sq in range(nsq):
                    if L >= 512:
                        base = n * 512
                    else:
                        base = (n * nsq + sq) * sg.gpad
                    for k in range(31):
                        op("pe", lambda e: e.matmul(out=bb[:, sq * W:(sq + 1) * W], lhsT=dg[:, c, k, :],
                                                    rhs=gluT[:, c, base + k:base + k + W], start=(k == 0), stop=(k == 30)),
                           reads=[c_dg[c][k]] + c_glu[c], writes=[cb], signal=(k == 30 and sq == nsq - 1))
                op("act", lambda e: e.activation(out=uu[:, c, :], in_=bb[:], func=AF.Identity, bias=cvc(l, "convb", c)),
                   reads=[cb, c_cv], writes=[c_uu])
                op("act", lambda e: e.activation(out=usq[:, c, :], in_=uu[:, c, :], func=AF.Square), writes=[c_uu])
            bm_, cm_ = bank()
            bv_, cv_ = bank()
            for (bb, cb, srcT) in ((bm_, cm_, uu), (bv_, cv_, usq)):
                for c in range(2):
                    op("pe", lambda e: e.matmul(out=bb[:], lhsT=onesd[:], rhs=srcT[:, c, :], start=(c == 0), stop=(c == 1)),
                       reads=[c_uu, c_const], writes=[cb], signal=(c == 1))
            op("dve", lambda e: e.tensor_copy(out=stt[:, 0, :], in_=bm_[:]), reads=[cm_], writes=[c_stt])
            op("dve", lambda e: e.tensor_tensor(out=stt[:, 1, :], in0=stt[:, 0, :], in1=stt[:, 0, :], op=ALU.mult), writes=[c_stt])
            op("dve", lambda e: e.tensor_tensor(out=stt[:, 1, :], in0=bv_[:], in1=stt[:, 1, :], op=ALU.subtract),
               reads=[cv_], writes=[c_stt])
            op("act", lambda e: e.activation(out=stt[:, 1, :], in_=stt[:, 1, :], func=AF.Sqrt, bias=epsc[:, 0:1]),
               reads=[c_stt, c_const], writes=[c_stt])
            op("dve", lambda e: e.reciprocal(out=stt[:, 1, :], in_=stt[:, 1, :]), reads=[c_stt], writes=[c_stt])
            for c in range(2):
                op("dve", lambda e: e.tensor_tensor(out=uu[:, c, :], in0=uu[:, c, :], in1=stt[:, 0, :], op=ALU.subtract),
                   reads=[c_stt], writes=[c_uu])
                op("dve", lambda e: e.tensor_tensor(out=uu[:, c, :], in0=uu[:, c, :], in1=stt[:, 1, :], op=ALU.mult),
                   writes=[c_uu])
                op("act", lambda e: e.activation(out=cuT[:, c, n * 512:(n + 1) * 512], in_=uu[:, c, :], func=AF.Silu,
                                                 bias=cvc(l, "lnb", c), scale=cvc(l, "lng", c)),
                   reads=[c_uu, c_cv], writes=[c_cu[c][n]])
        kb.barrier()
        dump(f"cu_{sg.name}{l}", cuT[:, :, 0:T])
        chk(f"B2_{sg.name}{l}")

        u12 = [carve(RB, 8192 + s * 2048, [128, 2, 2, 512]) for s in range(2)]
        c_u12 = cells(2)
        BCi = 2 if sg.latent else 0
        load_wG(0)
        if sg.latent:
            di = 0
            for n in range(4):
                ub = [[bank() for c in range(2)] for cs in range(2)]
                for kg in range(4):
                    s = di % 3
                    di += 1
                    if di > 1:
                        dma("sp", dfs[s].rearrange("p a b c -> p (a b c)"), c_dft2k[n, kg], dfs_keys[s], writes=[c_dfs[s]])
                    for k4 in range(4):
                        kt_ = kg * 4 + k4
                        for cs in range(2):
                            for c in range(2):
                                bb, cb = ub[cs][c]
                                op("pe", lambda e: e.matmul(out=bb[:], lhsT=ftok[:, kt_, c * 128:(c + 1) * 128],
                                                            rhs=dfs[s][:, cs, k4, :], start=(kt_ == 0), stop=(kt_ == 15)),
                                   reads=[c_f[kt_], c_dfs[s]], writes=[cb], signal=(kt_ == 15 or (k4 == 3 and cs == 1 and c == 1)))
                us = n % 2
                for cs in range(2):
                    for c in range(2):
                        bb, cb = ub[cs][c]
                        if c == 0:
                            op("act", lambda e: e.copy(out=u12[us][:, cs, c, :], in_=bb[:]), reads=[cb], writes=[c_u12[us]])
                        else:
                            op("dve", lambda e: e.tensor_copy(out=u12[us][:, cs, c, :], in_=bb[:]), reads=[cb], writes=[c_u12[us]])
                for c in range(2):
                    yb, yc = bank()
                    for cs in range(2):
                        op("pe", lambda e: e.matmul(out=yb[:], lhsT=bcs[:, BCi + cs, :], rhs=u12[us][:, cs, c, :],
                                                    start=(cs == 0), stop=(cs == 1)),
                           reads=[c_u12[us], c_const], writes=[yc], signal=(cs == 1))
                    op("act", lambda e: e.copy(out=fmT[:, c, n * 512:(n + 1) * 512], in_=yb[:]), reads=[yc], writes=[c_fm[c][n]])
        else:
            for sp_ in range(nseq // 2):
                us = sp_ % 2
                for cs in range(2):
                    for c in range(2):
                        bb, cb = bank()
                        for sq in range(2):
                            sidx = sp_ * 2 + sq
                            for k2 in range(2):
                                kt_ = sidx * 2 + k2
                                op("pe", lambda e: e.matmul(out=bb[:, sq * 256:(sq + 1) * 256],
                                                            lhsT=ftok[:, kt_, c * 128:(c + 1) * 128],
                                                            rhs=dft256[:, cs, k2, :], start=(k2 == 0), stop=(k2 == 1)),
                                   reads=[c_f[kt_], c_const], writes=[cb], signal=(sq == 1 and k2 == 1))
                        if c == 0:
                            op("act", lambda e: e.copy(out=u12[us][:, cs, c, :], in_=bb[:]), reads=[cb], writes=[c_u12[us]])
                        else:
                            op("dve", lambda e: e.tensor_copy(out=u12[us][:, cs, c, :], in_=bb[:]), reads=[cb], writes=[c_u12[us]])
                for c in range(2):
                    yb, yc = bank()
                    for cs in range(2):
                        op("pe", lambda e: e.matmul(out=yb[:], lhsT=bcs[:, BCi + cs, :], rhs=u12[us][:, cs, c, :],
                                                    start=(cs == 0), stop=(cs == 1)),
                           reads=[c_u12[us], c_const], writes=[yc], signal=(cs == 1))
                    op("act", lambda e: e.copy(out=fmT[:, c, sp_ * 512:(sp_ + 1) * 512], in_=yb[:]), reads=[yc],
                       writes=[c_fm[c][sp_]])
        kb.barrier()
        dump(f"fm_{sg.name}{l}", fmT[:, :, 0:T])
        chk(f"B3_{sg.name}{l}")

        mT = [carve(RA, 8192 + j * 2048, [128, 2048]) for j in range(6)] + \
             [carve(RB, 8192 + (j - 6) * 2048, [128, 2048]) for j in range(6, 8)]
        c_m = cells(8, 4)
        sgt = [carve(RW, 10240 + s * 1024, [128, 512], F32) for s in range(3)]
        c_sgt = cells(3)
        m1 = carve(RW, 13312, [128, 512], F32)
        c_m1 = Cell()
        sgi = 0
        for j in range(8):
            s = j % 2
            if j + 1 < 8:
                load_wG(j + 1)
            if (not sg.latent) and 2 <= j <= 5:
                for k in (2 * (j - 2), 2 * (j - 2) + 1):
                    dma("sp", X[:, 8 + k, :], w_o[l, k * 128:(k + 1) * 128, :], f"x{k}", writes=[c_x[8 + k]])
            for n in range(NN):
                bg3 = [bank() for _ in range(3)]
                for gi in range(3):
                    bb, cb = bg3[gi]
                    for k in range(8):
                        op("pe", lambda e: e.matmul(out=bb[:], lhsT=wG[s][:, k, gi, :], rhs=HT[:, k, n * 512:(n + 1) * 512],
                                                    start=(k == 0), stop=(k == 7)),
                           reads=[c_wG[s]] + c_ht[k][n * 4:n * 4 + 4], writes=[cb], signal=(k == 7))
                ba3 = [bank() for _ in range(3)]
                srcs = [(aoT, 0, 4, [c_qa[c][n * 4 + t] for c in range(4) for t in range(4)]),
                        (cuT, 4, 2, [c_cu[c][n] for c in range(2)]), (fmT, 6, 2, [c_fm[c][n] for c in range(2)])]
                for ai, (srcT, w0, nk_, rd) in enumerate(srcs):
                    bb, cb = ba3[ai]
                    for kk in range(nk_):
                        op("pe", lambda e: e.matmul(out=bb[:], lhsT=wO3[s][:, w0 + kk, :], rhs=srcT[:, kk, n * 512:(n + 1) * 512],
                                                    start=(kk == 0), stop=(kk == nk_ - 1)),
                           reads=[c_wG[s]] + rd, writes=[cb], signal=(kk == nk_ - 1))
                sl = []
                for gi in range(3):
                    ss = sgi % 3
                    sgi += 1
                    sl.append(ss)
                    op("act", lambda e: e.activation(out=sgt[ss], in_=bg3[gi][0][:], func=AF.Sigmoid,
                                                     bias=cvc(l, "bgate", gi * 8 + j)),
                       reads=[bg3[gi][1], c_cv], writes=[c_sgt[ss]])
                op("dve", lambda e: e.tensor_tensor(out=sgt[sl[0]], in0=sgt[sl[0]], in1=ba3[0][0][:], op=ALU.mult),
                   reads=[ba3[0][1]], writes=[c_sgt[sl[0]]])
                op("dve", lambda e: e.tensor_tensor(out=sgt[sl[1]], in0=sgt[sl[1]], in1=ba3[1][0][:], op=ALU.mult),
                   reads=[ba3[1][1]], writes=[c_sgt[sl[1]]])
                op("dve", lambda e: e.scalar_tensor_tensor(out=sgt[sl[2]], in0=ba3[2][0][:], scalar=cvc(l, "bfnet", j),
                                                           in1=sgt[sl[2]], op0=ALU.add, op1=ALU.mult),
                   reads=[ba3[2][1], c_cv], writes=[c_sgt[sl[2]]])
                op("dve", lambda e: e.tensor_tensor(out=m1, in0=sgt[sl[0]], in1=sgt[sl[1]], op=ALU.add),
                   reads=[c_sgt[sl[0]], c_sgt[sl[1]]], writes=[c_m1])
                op("dve", lambda e: e.tensor_tensor(out=mT[j][:, n * 512:(n + 1) * 512], in0=m1, in1=sgt[sl[2]], op=ALU.add),
                   reads=[c_sgt[sl[2]], c_m1], writes=[c_m[j][n]])
            if sg.latent:
                dma("sp", scG[l, j], wGO_flat[s], f"g{s}", reads=[c_wG[s]])
        kb.barrier()
        if any(k_.startswith("mg_") for k_ in dbg):
            for j in range(8):
                if f"mg_{sg.name}{l}" in dbg:
                    st = dma("pool", dbg[f"mg_{sg.name}{l}"][:, j, :], mT[j][:, 0:T], "dbg")
                    kb.out_stamps.append(st)

        chk(f"C_{sg.name}{l}")
        wo = carve(RW, 0, [128, 8, 1024])
        c_wo = cells(8)
        lng_ = carve(RW, 8192, [128, 1024], F32); lnb_ = carve(RW, 10240, [128, 1024], F32)
        c_lnt = Cell()
        stg2 = [carve(RB, s * 2048, [128, 1024], F32) for s in range(2)]
        c_stg2 = cells(2)
        gbc = carve(RB, 4096, [128, 1024], F32)
        c_gbc = Cell()
        xnb = [carve(RA, s * 1024, [128, 1024]) for s in range(4)]
        c_xnb = cells(4)

        def make_gbc(s_idx):
            for half in range(2):
                pb, pc = bank()
                for jj in range(4):
                    j = half * 4 + jj
                    op("dve", lambda e: e.tensor_scalar(out=dgf[:], in0=identf[:], scalar1=modc(l, s_idx, j, g), scalar2=None,
                                                        op0=ALU.mult), reads=[c_mod, c_const], writes=[c_dgf])
                    op("pe", lambda e: e.matmul(out=pb[:, jj * 128:(jj + 1) * 128], lhsT=onesf[:], rhs=dgf[:], start=True, stop=True),
                       reads=[c_dgf, c_const], writes=[pc])
                op("act", lambda e: e.copy(out=gbc[:, half * 512:(half + 1) * 512], in_=pb[:]), reads=[pc], writes=[c_gbc])

        make_gbc(2)
        dma("sp", lng_, ln1_g[l].partition_broadcast(128), "w2", writes=[c_lnt])
        dma("sp", lnb_, ln1_b[l].partition_broadcast(128), "w2", writes=[c_lnt])
        for k in range(8):
            s = k % 2
            if sg.latent:
                dma("sp", stg2[s], w_o[l, k * 128:(k + 1) * 128, :], f"g{s}", writes=[c_stg2[s]])
                src_w, rc_w = stg2[s], c_stg2[s]
            else:
                src_w, rc_w = X[:, 8 + k, :], c_x[8 + k]
            op("pool" if (sg.latent or k % 2) else "dve", lambda e: e.tensor_tensor(out=wo[:, k, :], in0=src_w, in1=gbc, op=ALU.mult),
               reads=[rc_w, c_gbc], writes=[c_wo[k]])
        wU = [carve(RW, s * 2048, [128, 8, 2, 128]) for s in range(2)]
        wU_flat = [carve(RW, s * 2048, [128, 2048]) for s in range(2)]
        c_wU = cells(2)
        def load_wU(j, extra=()):
            s = j % 2
            ex = list(extra)
            if sg.latent:
                for b in range(2):
                    c0 = b * DFF + j * 128
                    dma("pool", wU[s][:, :, b, :], w_up[l, :, c0:c0 + 128].rearrange("(k p) n -> p k n", p=128),
                        f"w{s}", writes=[c_wU[s]] + ex)
                    ex = []
            else:
                dma("sp", wU_flat[s], scU[l, j], f"w{s}", writes=[c_wU[s]] + ex)

        def d_s1(b):
            for i0 in (4 * b, 4 * b + 2):
                for i in (i0, i0 + 1):
                    n = i // 4
                    for half in range(2):
                        pb, pc = bank()
                        for k in range(8):
                            op("pe", lambda e: e.matmul(out=pb[:], lhsT=mT[k][:, i * 128:(i + 1) * 128],
                                                        rhs=wo[:, k, half * 512:(half + 1) * 512], start=(k == 0), stop=(k == 7)),
                               reads=[c_m[k][n], c_wo[k]], writes=[pc], signal=(k == 7))
                        xs_ = X[:, i, half * 512:(half + 1) * 512]
                        op("dve", lambda e: e.scalar_tensor_tensor(out=xs_, in0=xs_, scalar=ALPHA, in1=pb[:], op0=ALU.mult, op1=ALU.add),
                           reads=[pc], writes=[c_x[i]])
                stats_block([i0, i0 + 1], 0)
            stats_finish(0, 4 * b, 4 * b + 4)
            if b == NT // 4 - 1:
                load_wU(0, extra=c_wo)

        def d_s2(b):
            blk = list(range(4 * b, 4 * b + 4))
            affine_block(blk, lng_, lnb_, c_lnt)
            stats_block(blk, 1)
            stats_finish(1, 4 * b, 4 * b + 4)

        ln_pipeline(NT, d_s1, d_s2, lambda b: ht_block(sg, l, 3, xnb, c_xnb, list(range(4 * b, 4 * b + 4))))
        kb.barrier()
        dump(f"x1_{sg.name}{l}", X[:, 0:NT, :])
        dump(f"h2_{sg.name}{l}", HT[:, :, 0:T])
        chk(f"D_{sg.name}{l}")

        gT = carve(RA, 0, [128, 6, 2048])
        c_gT = cells(6, 4)
        UW = 2080
        ub_ = [[carve(RA, 12288 + (s * 2 + b) * UW, [128, UW]) for b in range(2)] for s in range(2)]
        c_ub = cells(2, 2, 4)
        ct = [carve(RB, s * 1024, [128, 512], F32) for s in range(2)]
        c_ct = cells(2)
        dgu = [carve(RB, 2048 + s * 768, [128, 6, 128]) for s in range(2)]
        tv = [carve(RB, 10240 + s * 1024, [128, 512], F32) for s in range(2)]
        c_tv = cells(2)
        c_dgu = cells(2, 6)
        stg3 = [carve(RB, 4096 + s * 2048, [128, 1024], F32) for s in range(2)]
        c_stg3 = cells(2)
        gbc2 = carve(RB, 8192, [128, 1024], F32)
        xnb2 = [carve(RA, 12288 + s * 1024, [128, 1024]) for s in range(4)]
        c_xnb2 = cells(4)
        wD = carve(RW, 4096, [128, 6, 1024])
        c_wD = cells(6)
        lng2 = carve(RW, 10240, [128, 1024], F32); lnb2 = carve(RW, 12288, [128, 1024], F32)
        c_lnt2 = Cell()
        gbc = gbc2
        for s in range(2):
            for b in range(2):
                op("pool", lambda e: e.memset(ub_[s][b], 0.0), writes=c_ub[s][b])
        c_gbc = Cell()

        def build_gbc2():
            for half in range(2):
                pb, pc = bank()
                for jj in range(4):
                    j = half * 4 + jj
                    op("dve", lambda e: e.tensor_scalar(out=dgf[:], in0=identf[:], scalar1=modc(l, 5, j, g), scalar2=None,
                                                        op0=ALU.mult), reads=[c_mod, c_const], writes=[c_dgf])
                    op("pe", lambda e: e.matmul(out=pb[:, jj * 128:(jj + 1) * 128], lhsT=onesf[:], rhs=dgf[:], start=True, stop=True),
                       reads=[c_dgf, c_const], writes=[pc])
                op("act", lambda e: e.copy(out=gbc2[:, half * 512:(half + 1) * 512], in_=pb[:]), reads=[pc], writes=[c_gbc])

        dma("sp", lng2, ln2_g[l].partition_broadcast(128), "w2", writes=[c_lnt2])
        dma("sp", lnb2, ln2_b[l].partition_broadcast(128), "w2", writes=[c_lnt2])
        wui = 0
        cti = 0
        sdi = 0
        upad = sg.upad

        def uview(buf, n, shift):
            if L >= 512:
                return buf[:, n * 512 + shift:n * 512 + shift + 512]
            nsq = 512 // L
            return buf[:, 0:nseq * upad].rearrange("p (s w) -> p s w", w=upad)[:, n * nsq:(n + 1) * nsq, shift:shift + L]

        def tview(t):
            return t if L >= 512 else t.rearrange("p (s w) -> p s w", w=L)

        for gi, (j0, nj) in enumerate(FFN_GROUPS):
            for jj in range(nj):
                j = j0 + jj
                s = j % 2
                if j + 1 < NFF:
                    load_wU(j + 1)
                for n in range(NN):
                    for b in range(2):
                        pb, pc = bank()
                        for k in range(8):
                            op("pe", lambda e: e.matmul(out=pb[:], lhsT=wU[s][:, k, b, :], rhs=HT[:, k, n * 512:(n + 1) * 512],
                                                        start=(k == 0), stop=(k == 7)),
                               reads=[c_wU[s]] + c_ht[k][n * 4:n * 4 + 4], writes=[pc], signal=(k == 7))
                        op("act", lambda e: e.copy(out=uview(ub_[s][b], n, 1), in_=tview(pb[:])), reads=[pc],
                           writes=[c_ub[s][b][n]])
                if j == 0:
                    build_gbc2()
                for k3 in range(3):
                    col = k3 * 44 + j
                    op("dve", lambda e: e.tensor_scalar(out=dgu[s][:, k3, :], in0=identb[:], scalar1=cvc(l, "fcw", col),
                                                        scalar2=None, op0=ALU.mult), reads=[c_cv, c_const], writes=[c_dgu[s][k3]])
                for n in range(NN):
                    rdn = [m_ for m_ in (n - 1, n, n + 1) if 0 <= m_ < NN]
                    pb, pc = bank()
                    for k3 in range(3):
                        op("pe", lambda e: e.matmul(out=tview(pb[:]), lhsT=dgu[s][:, k3, :], rhs=uview(ub_[s][0], n, k3),
                                                    start=(k3 == 0), stop=(k3 == 2)),
                           reads=[c_dgu[s][k3]] + [c_ub[s][0][m_] for m_ in rdn], writes=[pc], signal=(k3 == 2))
                    r = cti % 2
                    cti += 1
                    op("act", lambda e: e.activation(out=ct[r], in_=pb[:], func=AF.Silu, bias=cvc(l, "fcb", j)),
                       reads=[pc, c_cv], writes=[c_ct[r]])
                    colv = NFF + j
                    rdv = [c_ub[s][1][m_] for m_ in rdn]
                    op("act", lambda e: e.activation(out=tview(tv[r]), in_=uview(ub_[s][1], n, 1), func=AF.Identity,
                                                     scale=cvc(l, "fcw", 44 + colv), bias=cvc(l, "fcb", colv)),
                       reads=rdv + [c_cv], writes=[c_tv[r]])
                    for sh, wrow in ((0, 0), (2, 2)):
                        op("dve", lambda e: e.scalar_tensor_tensor(out=tview(tv[r]), in0=uview(ub_[s][1], n, sh),
                                                                   scalar=cvc(l, "fcw", wrow * 44 + colv), in1=tview(tv[r]),
                                                                   op0=ALU.mult, op1=ALU.add), reads=rdv + [c_cv], writes=[c_tv[r]])
                    op("dve", lambda e: e.tensor_tensor(out=gT[:, jj, n * 512:(n + 1) * 512], in0=tv[r], in1=ct[r], op=ALU.mult),
                       reads=[c_tv[r], c_ct[r]], writes=[c_gT[jj][n]])
                sd = sdi % 2
                sdi += 1
                dma("sp", stg3[sd], w_down[l, j * 128:(j + 1) * 128, :], f"g{sd}", writes=[c_stg3[sd]])
                if sg.latent:
                    dma("sp", scU[l, j], wU_flat[s], f"x{s}", reads=[c_wU[s]])
                op("pool", lambda e: e.tensor_tensor(out=wD[:, jj, :], in0=stg3[sd], in1=gbc2, op=ALU.mult),
                   reads=[c_stg3[sd], c_gbc], writes=[c_wD[jj]])
            lastg = gi == len(FFN_GROUPS) - 1

            def e_s1(b, gi=gi, nj=nj, lastg=lastg):
                for i0 in (4 * b, 4 * b + 2):
                    for i in (i0, i0 + 1):
                        n = i // 4
                        for half in range(2):
                            pb, pc = bank()
                            for jj in range(nj):
                                op("pe", lambda e: e.matmul(out=pb[:], lhsT=gT[:, jj, i * 128:(i + 1) * 128],
                                                            rhs=wD[:, jj, half * 512:(half + 1) * 512], start=(jj == 0), stop=(jj == nj - 1)),
                                   reads=[c_gT[jj][n], c_wD[jj]], writes=[pc], signal=(jj == nj - 1))
                            xs_ = X[:, i, half * 512:(half + 1) * 512]
                            if gi == 0:
                                op("dve", lambda e: e.scalar_tensor_tensor(out=xs_, in0=xs_, scalar=ALPHA, in1=pb[:], op0=ALU.mult,
                                                                           op1=ALU.add), reads=[pc], writes=[c_x[i]])
                            else:
                                op("dve", lambda e: e.tensor_tensor(out=xs_, in0=xs_, in1=pb[:], op=ALU.add), reads=[pc],
                                   writes=[c_x[i]])
                    if lastg:
                        stats_block([i0, i0 + 1], 0)
                if lastg:
                    stats_finish(0, 4 * b, 4 * b + 4)
                    if b == NT // 4 - 1 and nxt is not None:
                        pref[(nxt[0].name, nxt[1])] = load_wA(nxt[0], nxt[1], extra=c_wU + c_wD)

            def e_s2(b):
                blk = list(range(4 * b, 4 * b + 4))
                affine_block(blk, lng2, lnb2, c_lnt2)
                if last:
                    for i in blk:
                        st = dma("sp", sg.yout[i * 128:(i + 1) * 128, :], X[:, i, :], f"x{i % 8}", reads=[c_x[i]])
                        kb.out_stamps.append(st)
                else:
                    stats_block(blk, 1)
                    stats_finish(1, 4 * b, 4 * b + 4)

            if not lastg:
                ln_pipeline(NT, e_s1, None, None)
            else:
                ln_pipeline(NT, e_s1, e_s2,
                            (lambda b: ht_block(sg, l + 1, 0, xnb2, c_xnb2, list(range(4 * b, 4 * b + 4)))) if not last else None)
        kb.barrier()

    segS_ = Seg("S", 2048, 1, True, 1, xs, ys)
    pre0 = {}
    if stop is None:
        sg0 = [q for q in ("S", "P") if q in segs][0]
        nt0 = 16 if sg0 == "S" else 8
        xin0 = xs if sg0 == "S" else xp
        for i in range(nt0):
            dma("sp", X[:, i, :], xin0[i * 128:(i + 1) * 128, :], f"x{i % 8}", reads=([c_x[i - 8]] if i >= 8 else []), writes=[c_x[i]])
        for b_ in range(nt0 // 4):
            stats_block(list(range(4 * b_, 4 * b_ + 4)), 1)
            stats_finish(1, 4 * b_, 4 * b_ + 4)
        pre0[sg0] = True
    for l in range(DEPTH):
        items = [("bmod", b_mod[l].rearrange("(c p) -> c p", p=128), 48),
                 ("bconv", b_in[l, 768:1280].rearrange("(c p) -> c p", p=128), 4),
                 ("bgate", b_in[l, 1536:4608].rearrange("(c p) -> c p", p=128), 24),
                 ("convb", conv_b[l].rearrange("(c p) -> c p", p=128), 2),
                 ("lng", conv_ln_g[l].rearrange("(c p) -> c p", p=128), 2),
                 ("lnb", conv_ln_b[l].rearrange("(c p) -> c p", p=128), 2),
                 ("bfnet", b_fnet[l].rearrange("(c p) -> c p", p=128), 8),
                 ("fcw", ffn_conv_w[l].rearrange("k (c p) -> (k c) p", p=128), 132),
                 ("fcb", ffn_conv_b[l].rearrange("(c p) -> c p", p=128), 44),
                 ("cw", conv_w[l].rearrange("k (c p) -> (k c) p", p=128), 62)]
        o = 0
        for name, ap, n in items:
            CVOFF[name] = o
            o += n
        assert o == NCV
        load_cols(CV[:, l, :], [(ap, n) for _, ap, n in items])
    if stop is None and "S" in segs:
        pref[("S", 0)] = load_wA(segS_, 0)
    for job in jobs:
        job.finish()
    modc = lambda l, s, k, g: MODT[:, l, s * 8 + k, g:g + 1]
    kb.barrier()
    if "modT" in dbg:
        kb.out_stamps.append(kb.dma("sp", dbg["modT"], MODT[:].rearrange("p l c g -> p (l c g)"), "dbg"))
        kb.out_stamps.append(kb.dma("sp", dbg["cv"], CV[:].rearrange("p l c -> p (l c)"), "dbg"))

    segP = Seg("P", 1024, 4, False, 0, xp, yp)
    segS = Seg("S", 2048, 1, True, 1, xs, ys)
    try:
        chk("mod")
        for sg in (segS, segP):
            if sg.name not in segs:
                continue
            for l in range(DEPTH):
                order = [q for q in (segS, segP) if q.name in segs]
                if l < DEPTH - 1:
                    nxt = (sg, l + 1)
                else:
                    qi = order.index(sg)
                    nxt = (order[qi + 1], 0) if qi + 1 < len(order) else None
                if stop is not None:
                    nxt = None
                run_layer(sg, l, l == 0, l == DEPTH - 1, nxt)
                chk(f"E_{sg.name}{l}")
    except StopBuild:
        pass
    kb.barrier()
    return nc


def host_consts():
    bf = ml_dtypes.bfloat16
    c = {}
    c["c_identb"] = np.eye(128, dtype=np.float32).astype(bf)
    c["c_identf"] = np.eye(128, dtype=np.float32)
    j = np.arange(128)[:, None]; i = np.arange(128)[None, :]
    mL = np.where(j >= i, 0.0, NEG); mR = np.where(j <= i, 0.0, NEG)
    c["c_mask"] = np.stack([mL, mR]).astype(np.float32).astype(bf)
    c["c_mask01"] = np.concatenate([(j >= i), np.ones((128, 128), bool), (j <= i)], 1).astype(np.float32).astype(bf)
    pos = np.arange(2048)
    row = (pos // 64).astype(np.float32); col = (pos % 64).astype(np.float32)
    inv = (10000.0 ** (-np.arange(16, dtype=np.float32) / 16)).astype(np.float32)
    ang = np.concatenate([row[:, None] * inv, col[:, None] * inv], -1).astype(np.float32)
    cs = np.stack([np.cos(ang), np.sin(ang)]).astype(np.float32)
    c["c_rope"] = np.ascontiguousarray(cs.reshape(2, 16, 128, 32).transpose(0, 2, 1, 3))
    def dft(Ln):
        t = np.arange(Ln, dtype=np.int64)
        ph = (2.0 * np.pi / Ln) * ((t[:, None] * t[None, :]) % Ln).astype(np.float64)
        return np.stack([np.cos(ph), np.sin(ph)])
    c["c_dft256"] = dft(256).astype(np.float32).astype(bf)
    d2k = dft(2048).astype(np.float32).astype(bf)
    d2k = d2k.reshape(2, 4, 4, 128, 4, 512).transpose(4, 1, 3, 0, 2, 5)
    c["c_dft2k"] = np.ascontiguousarray(d2k).reshape(4, 4, 128, 2 * 4 * 512)
    d64 = dft(64)
    blocks = []
    for Ln in (256, 2048):
        nrm = 1.0 / np.sqrt(Ln * 64.0)
        bc = np.zeros((128, 128)); bs = np.zeros((128, 128))
        for gq in range(2):
            bc[gq * 64:(gq + 1) * 64, gq * 64:(gq + 1) * 64] = d64[0] * nrm
            bs[gq * 64:(gq + 1) * 64, gq * 64:(gq + 1) * 64] = -d64[1] * nrm
        blocks += [bc, bs]
    c["c_bcs"] = np.stack(blocks).astype(np.float32).astype(bf)
    return c


_NC_CACHE = {}


def kernel(x_prompt, x_sample, cache_k, cache_v, c, c_ctx, w_mod, b_mod, w_in, b_in, sink, w_attn_o, conv_w, conv_b,
           conv_ln_g, conv_ln_b, w_conv_o, w_fnet, b_fnet, w_o, ln1_g, ln1_b, w_up, ffn_conv_w, ffn_conv_b, w_down,
           ln2_g, ln2_b, _debug=None):
    f = lambda a: np.ascontiguousarray(np.asarray(a, dtype=np.float32))
    _stop = None
    _segs = "PS"
    _ncores = 8
    if _debug:
        _debug = dict(_debug)
        _stop = _debug.pop("_stop", None)
        _segs = _debug.pop("_segs", "PS")
        _ncores = _debug.pop("_ncores", 8)
    key = (str(sorted(_debug.items())) if _debug else None, _stop, _segs)
    if key not in _NC_CACHE:
        _NC_CACHE[key] = build(_debug, _stop, _segs)
    nc = _NC_CACHE[key]
    consts = host_consts()
    shared = dict(w_mod=f(w_mod), b_mod=f(b_mod), w_in=f(w_in), b_in=f(b_in), sink=f(sink), w_attn_o=f(w_attn_o),
                  conv_w=f(conv_w), conv_b=f(conv_b), conv_ln_g=f(conv_ln_g), conv_ln_b=f(conv_ln_b), w_conv_o=f(w_conv_o),
                  w_fnet=f(w_fnet), b_fnet=f(b_fnet), w_o=f(w_o), ln1_g=f(ln1_g), ln1_b=f(ln1_b), w_up=f(w_up),
                  ffn_conv_w=f(ffn_conv_w), ffn_conv_b=f(ffn_conv_b), w_down=f(w_down), ln2_g=f(ln2_g), ln2_b=f(ln2_b))
    shared.update(consts)
    xp = f(x_prompt); xs = f(x_sample); ck_ = f(cache_k); cv_ = f(cache_v); cc = f(c); cx = f(c_ctx)
    in_maps = []
    for b in range(_ncores):
        m = dict(shared)
        m["xp"] = np.ascontiguousarray(xp[4 * b:4 * b + 4].reshape(1024, D))
        m["xs"] = np.ascontiguousarray(xs[b])
        m["ck"] = np.ascontiguousarray(ck_[b].reshape(DEPTH, 256, 128))
        m["cv"] = np.ascontiguousarray(cv_[b].reshape(DEPTH, 256, 128))
        m["cvec"] = np.ascontiguousarray(np.stack([cx, cc[b]]))
        in_maps.append(m)
    res = run_bass_kernel_spmd(nc, in_maps, core_ids=list(range(_ncores)))
    R = res.results
    if _debug is not None:
        kernel._dbg = [{k: v for k, v in r.items()} for r in R]
        return None
    y_p = np.concatenate([r["yp"].reshape(4, 256, D) for r in R], 0)
    y_s = np.stack([r["ys"] for r in R], 0)
    nk_ = np.concatenate([r["nk"].reshape(4, DEPTH, 256, 2, 64) for r in R], 0)
    nv_ = np.concatenate([r["nv"].reshape(4, DEPTH, 256, 2, 64) for r in R], 0)
    if _debug:
        kernel._dbg = [{k: v for k, v in r.items() if k.startswith("dbg_")} for r in R]
    return (y_p.astype(np.float32), y_s.astype(np.float32), nk_.astype(np.float32), nv_.astype(np.float32))
```

```python
import numpy as np
import ml_dtypes
import concourse.bass as bass
import concourse.mybir as mybir
from concourse.bass_utils import run_bass_kernel_spmd

F32 = mybir.dt.float32
BF16 = mybir.dt.bfloat16
AF = mybir.ActivationFunctionType
ALU = mybir.AluOpType

D = 1024
DEPTH = 2
NIN = 4608
DFF = 2816
NFF = 22
ALPHA = float((2 * DEPTH) ** 0.25)
EPS = 1e-6
FFN_GROUPS = [(0, 6), (6, 6), (12, 5), (17, 5)]
NEG = -30000.0
import os
P0_LEVEL = int(os.environ.get('P0_LEVEL', '9'))
EVAC_MODE = int(os.environ.get('EVAC_MODE', '0'))


class Eng:
    def __init__(self, name, e, sem):
        self.name, self.e, self.sem = name, e, sem
        self.count = 0
        self.seen = {}


class Cell:
    __slots__ = ("w", "r")

    def __init__(self):
        self.w = None
        self.r = {}


def cells(*dims):
    if len(dims) == 1:
        return [Cell() for _ in range(dims[0])]
    return [cells(*dims[1:]) for _ in range(dims[0])]


class KB:
    def __init__(self, nc):
        self.nc = nc
        self.engs = {}
        self.sems = {}
        for name, e in (("pe", nc.tensor), ("act", nc.scalar), ("dve", nc.vector),
                        ("pool", nc.gpsimd), ("sp", nc.sync)):
            sem = nc.semaphore("s_" + name).__enter__()
            self.engs[name] = Eng(name, e, sem)
            self.sems[name] = sem
        self.dcount = {}
        for key in ["const", "stg", "stg1", "stg2", "w0", "w1", "w2", "g0", "g1", "dbg"] + [f"x{i}" for i in range(8)]:
            self.sems[key] = nc.semaphore("d_" + key).__enter__()
            self.dcount[key] = 0
        self.banks = [nc.psum_tensor(f"ps{i}", [128, 512], F32).__enter__() for i in range(8)]
        self.bcells = cells(8)
        self.bi = 0
        self.reserved = set()
        self.out_stamps = []

    def bank(self):
        while self.bi in self.reserved:
            self.bi = (self.bi + 1) % 8
        i = self.bi
        self.bi = (self.bi + 1) % 8
        return self.banks[i], self.bcells[i]

    def reserve(self):
        b, c = self.bank()
        i = self.banks.index(b)
        self.reserved.add(i)
        return b, c, i

    def _wait(self, eng, key, val, war=False):
        if key == eng.name and (war or eng.name in ("pe", "sp")):
            return
        if eng.seen.get(key, 0) >= val:
            return
        have = self.engs[key].count if key in self.engs else self.dcount[key]
        assert val <= have, ("waiting on un-emitted signal", eng.name, key, val, have)
        eng.e.wait_ge(self.sems[key], val)
        eng.seen[key] = val

    def _deps(self, eng, reads, writes):
        for c in reads:
            if c.w is not None:
                self._wait(eng, *c.w)
        for c in writes:
            if c.w is not None:
                self._wait(eng, *c.w)
            for k, v in c.r.items():
                self._wait(eng, k, v, war=True)

    def _stamp(self, stamp, reads, writes):
        for c in reads:
            if c.r.get(stamp[0], 0) < stamp[1]:
                c.r[stamp[0]] = stamp[1]
        for c in writes:
            c.w = stamp
            c.r = {}

    def op(self, en, fn, reads=(), writes=(), signal=True):
        eng = self.engs[en]
        self._deps(eng, reads, writes)
        ins = fn(eng.e)
        if signal:
            eng.count += 1
            ins.then_inc(eng.sem, 1)
            stamp = (en, eng.count)
        else:
            stamp = (en, eng.count + 1)
        self._stamp(stamp, reads, writes)
        return ins

    def dma(self, q, out, in_, key, reads=(), writes=(), **kw):
        eng = self.engs[q]
        assert key in self.sems, key
        self._deps(eng, reads, writes)
        ins = eng.e.dma_start(out=out, in_=in_, **kw)
        self.dcount[key] += 16
        ins.then_inc(self.sems[key], 16)
        stamp = (key, self.dcount[key])
        self._stamp(stamp, reads, writes)
        return stamp

    def barrier(self):
        for en, eng in self.engs.items():
            for fn, f in self.engs.items():
                if f.count > 0:
                    self._wait(eng, fn, f.count)
            for key, cnt in self.dcount.items():
                if cnt > 0:
                    self._wait(eng, key, cnt)


def carve(R, off, shape, dt=BF16):
    n = int(np.prod(shape[1:]))
    nb = n * 2 if dt == F32 else n
    assert off % 2 == 0 and off + nb <= R.shape[1], (off, nb, R.shape)
    v = R[:, off:off + nb]
    if dt == F32:
        v = v.bitcast(F32)
    if len(shape) == 3:
        v = v.rearrange("p (a b) -> p a b", a=shape[1], b=shape[2])
    elif len(shape) == 4:
        v = v.rearrange("p (a b c) -> p a b c", a=shape[1], b=shape[2], c=shape[3])
    return v


class Seg:
    def __init__(self, name, T, nseq, latent, g, xin, yout):
        self.name, self.T, self.nseq, self.latent, self.g = name, T, nseq, latent, g
        self.L = T // nseq
        self.NT = T // 128
        self.NN = T // 512
        self.xin, self.yout = xin, yout
        self.gpad = self.L + 30
        self.upad = self.L + 2


class StopBuild(Exception):
    pass


def build(debug=None, stop=None, segs="PS"):
    nc = bass.Bass("TRN2", target_bir_lowering=False)
    dt_in = lambda name, shape, dt=F32: nc.dram_tensor(name, list(shape), dt, kind="ExternalInput").ap()
    dt_out = lambda name, shape, dt=F32: nc.dram_tensor(name, list(shape), dt, kind="ExternalOutput").ap()
    xp = dt_in("xp", [1024, D]); xs = dt_in("xs", [2048, D])
    ck = dt_in("ck", [DEPTH, 256, 128]); cv = dt_in("cv", [DEPTH, 256, 128])
    cvec = dt_in("cvec", [2, D])
    w_mod = dt_in("w_mod", [DEPTH, D, 6 * D]); b_mod = dt_in("b_mod", [DEPTH, 6 * D])
    w_in = dt_in("w_in", [DEPTH, D, NIN]); b_in = dt_in("b_in", [DEPTH, NIN])
    sink = dt_in("sink", [DEPTH, 8])
    w_attn_o = dt_in("w_attn_o", [DEPTH, 512, D])
    conv_w = dt_in("conv_w", [DEPTH, 31, 256]); conv_b = dt_in("conv_b", [DEPTH, 256])
    conv_ln_g = dt_in("conv_ln_g", [DEPTH, 256]); conv_ln_b = dt_in("conv_ln_b", [DEPTH, 256])
    w_conv_o = dt_in("w_conv_o", [DEPTH, 256, D]); w_fnet = dt_in("w_fnet", [DEPTH, 256, D])
    b_fnet = dt_in("b_fnet", [DEPTH, D]); w_o = dt_in("w_o", [DEPTH, D, D])
    ln1_g = dt_in("ln1_g", [DEPTH, D]); ln1_b = dt_in("ln1_b", [DEPTH, D])
    w_up = dt_in("w_up", [DEPTH, D, 2 * DFF]); ffn_conv_w = dt_in("ffn_conv_w", [DEPTH, 3, 2 * DFF])
    ffn_conv_b = dt_in("ffn_conv_b", [DEPTH, 2 * DFF]); w_down = dt_in("w_down", [DEPTH, DFF, D])
    ln2_g = dt_in("ln2_g", [DEPTH, D]); ln2_b = dt_in("ln2_b", [DEPTH, D])
    c_identb = dt_in("c_identb", [128, 128], BF16); c_identf = dt_in("c_identf", [128, 128])
    c_mask = dt_in("c_mask", [2, 128, 128], BF16)
    c_mask01 = dt_in("c_mask01", [128, 384], BF16)
    c_rope = dt_in("c_rope", [2, 128, 16, 32])
    c_dft256 = dt_in("c_dft256", [2, 256, 256], BF16)
    c_dft2k = dt_in("c_dft2k", [4, 4, 128, 2 * 4 * 512], BF16)
    c_bcs = dt_in("c_bcs", [4, 128, 128], BF16)
    yp = dt_out("yp", [1024, D]); ys = dt_out("ys", [2048, D])
    nk = dt_out("nk", [4, DEPTH, 256, 128]); nv = dt_out("nv", [4, DEPTH, 256, 128])
    sc = lambda name, shape: nc.dram_tensor(name, list(shape), BF16, kind="Internal").ap()
    scA = sc("scA", [DEPTH, 128, 8192]); scB = sc("scB", [DEPTH, 1, 1024]); scC = sc("scC", [DEPTH, 128, 4096])
    scG = sc("scG", [DEPTH, 8, 128, 4096]); scU = sc("scU", [DEPTH, NFF, 128, 2048])
    dbg = {}
    if debug:
        for name, shape in debug.items():
            dbg[name] = dt_out("dbg_" + name, shape)

    kb = KB(nc)
    sb = lambda name, shape, dt=BF16: nc.sbuf_tensor(name, list(shape), dt).__enter__()
    X = sb("X", [128, 16, D], F32)
    HT = sb("HT", [128, 8, 2048])
    RA = sb("RA", [128, 21504])
    RB = sb("RB", [128, 12288])
    RW = sb("RW", [128, 14336])
    identb = sb("identb", [128, 128]); identf = sb("identf", [128, 128], F32)
    maskc = sb("maskc", [128, 2, 128])
    mask01 = sb("mask01", [128, 384])
    onesb = sb("onesb", [128, 128]); onesf = sb("onesf", [128, 128], F32); onesd = sb("onesd", [128, 128], F32)
    rope = sb("rope", [128, 2, 16, 32], F32)
    dft256 = sb("dft256", [128, 2, 2, 256])
    bcs = sb("bcs", [128, 4, 128])
    NCV = 328
    CV = sb("CV", [128, DEPTH, NCV], F32)
    MODT = sb("MODT", [128, DEPTH, 48, 2], F32)
    scT = sb("scT", [128, 8, 2])
    esink = sb("esink", [128, DEPTH, 4], F32)
    e8 = sb("e8", [128, 8], F32)
    stg = sb("stg", [128, 128], F32)
    e8r = sb("e8r", [1, DEPTH, 8], F32)
    onesr = sb("onesr", [1, 256])
    c_e8r = Cell()
    dgf = sb("dgf", [128, 128], F32)
    c_dgf = Cell()
    stats = sb("stats", [128, 4, 12], F32)
    mv = sb("mv", [128, 4, 4], F32)
    c_const = cells(1)[0]
    c_stg = Cell(); c_cv = Cell(); c_mod = Cell(); c_sct = Cell(); c_es = Cell()
    c_stats = cells(4)
    c_x = cells(16)
    c_ht = cells(8, 16)

    op, dma, bank = kb.op, kb.dma, kb.bank

    dma("sp", identb[:], c_identb, "const", writes=[c_const])
    dma("sp", identf[:], c_identf, "const", writes=[c_const])
    dma("sp", maskc[:], c_mask.rearrange("m p q -> p m q"), "const", writes=[c_const])
    dma("sp", mask01[:], c_mask01, "const", writes=[c_const])
    dma("sp", rope[:], c_rope.rearrange("c p t d -> p c t d"), "const", writes=[c_const])
    dma("sp", dft256[:], c_dft256.rearrange("c (k p) n -> p c k n", p=128), "const", writes=[c_const])
    dma("sp", bcs[:], c_bcs.rearrange("m p q -> p m q"), "const", writes=[c_const])
    op("dve", lambda e: e.memset(onesb[:], 1.0), writes=[c_const])
    op("dve", lambda e: e.memset(onesf[:], 1.0), writes=[c_const])
    op("dve", lambda e: e.memset(onesd[:], 1.0 / 256.0), writes=[c_const])
    epsc = sb("epsc", [128, 2], F32)
    op("dve", lambda e: e.memset(epsc[:], EPS), writes=[c_const])

    stgs = [stg, sb("stg1", [128, 128], F32), sb("stg2", [128, 128], F32)]
    stg_r = cells(3)
    lc_state = {"k": 0}

    def load_cols(dst, items):
        col = 0
        rows = 0
        bufc = []
        def flush():
            nonlocal rows, col, bufc
            if not rows:
                return
            k = lc_state["k"]
            pb, pc = bank()
            op("pe", lambda e: e.transpose(out=pb[:, 0:rows], in_=stgs[k][0:rows, :], identity=identf[0:rows, 0:rows]),
               reads=[stg_r[k], c_const] + bufc, writes=[pc])
            c0 = col
            op("dve", lambda e: e.tensor_copy(out=dst[:, c0:c0 + rows], in_=pb[:, 0:rows]), reads=[pc], writes=[c_cv])
            col += rows
            rows = 0
            bufc = []
            lc_state["k"] = (k + 1) % 3
        for ap, n in items:
            done = 0
            while done < n:
                k = lc_state["k"]
                take = min(n - done, 128 - rows)
                r0 = rows
                kb._deps(kb.engs["sp"], [], [stg_r[k]])
                fc = Cell()
                dma("sp", stgs[k][r0:r0 + take, :], ap[done:done + take, :], ("stg", "stg1", "stg2")[k], writes=[fc])
                bufc.append(fc)
                rows += take
                done += take
                if rows == 128:
                    flush()
        flush()
        return col

    CVOFF = {}
    _o = 0
    for _name, _n in (("bmod", 48), ("bconv", 4), ("bgate", 24), ("convb", 2), ("lng", 2), ("lnb", 2), ("bfnet", 8),
                      ("fcw", 132), ("fcb", 44), ("cw", 62)):
        CVOFF[_name] = _o
        _o += _n
    assert _o == NCV
    dma("sp", stgs[2][0:16, :], cvec.rearrange("g (k p) -> (g k) p", p=128), "stg2", writes=[c_stg])
    pb, pc = bank()
    op("pe", lambda e: e.transpose(out=pb[:, 0:16], in_=stgs[2][0:16, :], identity=identf[0:16, 0:16]),
       reads=[c_stg, c_const, stg_r[2]], writes=[pc])
    op("act", lambda e: e.activation(out=scT[:], in_=pb[:, 0:16].rearrange("p (g k) -> p k g", g=2), func=AF.Silu),
       reads=[pc], writes=[c_sct])
    dma("sp", e8r[:], sink.rearrange("(o l) h -> o l h", o=1), "g1", writes=[c_e8r])
    op("act", lambda e: e.activation(out=e8r[:], in_=e8r[:], func=AF.Exp), reads=[c_e8r], writes=[c_e8r])
    op("dve", lambda e: e.memset(onesr[:], 1.0), writes=[c_const])
    class ModJob:
        def __init__(self, l, slots, keys, pm, pmc):
            self.l, self.slots, self.keys = l, slots, keys
            self.cw = cells(len(slots))
            self.pm, self.pmc = pm, pmc
            self.pmv = pm[:, 0:96].rearrange("p (c g) -> p c g", g=2)
            self.nload = 0
            self.ndone = 0

        def load(self):
            if self.nload >= 12:
                return
            blk = self.nload
            s = blk % len(self.slots)
            dma("pool", self.slots[s], w_mod[self.l, :, blk * 512:(blk + 1) * 512].rearrange("(k p) n -> p k n", p=128),
                self.keys[s], writes=[self.cw[s]])
            self.nload += 1

        def step(self):
            if self.ndone >= 12:
                return
            blk = self.ndone
            while self.nload <= min(blk + len(self.slots) - 1, 11):
                self.load()
            s = blk % len(self.slots)
            for cc in range(4):
                ch = blk * 4 + cc
                for k in range(8):
                    op("pe", lambda e: e.matmul(out=self.pmv[:, ch, :], lhsT=self.slots[s][:, k, cc * 128:(cc + 1) * 128],
                                                rhs=scT[:, k, :], start=(k == 0), stop=(k == 7)),
                       reads=[self.cw[s], c_sct], writes=[self.pmc], signal=(k == 7 and cc == 3))
            self.ndone += 1

        def finish(self):
            while self.ndone < 12:
                self.step()
            l = self.l
            bm = CV[:, l, CVOFF["bmod"]:CVOFF["bmod"] + 48].unsqueeze(2).broadcast_to([128, 48, 2])
            op("dve", lambda e: e.tensor_tensor(out=MODT[:, l, :, :], in0=self.pmv, in1=bm, op=ALU.add),
               reads=[self.pmc, c_cv], writes=[c_mod])
            for base in (8, 32):
                op("dve", lambda e: e.tensor_scalar_add(out=MODT[:, l, base:base + 8, :], in0=MODT[:, l, base:base + 8, :],
                                                        scalar1=1.0), reads=[c_mod], writes=[c_mod])

    wm = [carve(RB, s * 4096, [128, 8, 512]) for s in range(3)]
    mod_late = (DEPTH - 1) if ("S" in segs and stop is None) else None
    jobs = []
    for l in range(DEPTH):
        if l == mod_late:
            continue
        pm, pmc = bank()
        job = ModJob(l, wm, ["w0", "w1", "w2"], pm, pmc)
        for _ in range(12):
            job.step()
        jobs.append(job)
    cvc = lambda l, name, i=0: CV[:, l, CVOFF[name] + i:CVOFF[name] + i + 1]

    def chk(tag):
        if stop == tag:
            raise StopBuild()

    def dump(name, src, reads=()):
        if name in dbg:
            st = dma("pool", dbg[name], src, "dbg", reads=list(reads))
            kb.out_stamps.append(st)

    ring = {"st": 0}
    MV = sb("MV", [128, 2, 16, 4], F32)
    c_mvs = cells(2, 4)
    c_mvt = cells(2, 16)

    c_stats2 = cells(4, 2)

    def stats_block(tiles, si):
        rs = []
        for i in tiles:
            r = ring["st"] % 4
            ring["st"] += 1
            rs.append(r)
            for h in range(2):
                op("dve", lambda e: e.bn_stats(out=stats[:, r, h * 6:(h + 1) * 6], in_=X[:, i, h * 512:(h + 1) * 512]),
                   reads=[c_x[i]], writes=[c_stats2[r][h]])
        for i, r in zip(tiles, rs):
            op("dve", lambda e: e.bn_aggr(out=MV[:, si, i, 0:2], in_=stats[:, r, :]), reads=c_stats2[r], writes=[c_mvt[si][i]])

    def stats_finish(si, i0, i1):
        c = c_mvs[si][i0 // 4]
        op("act", lambda e: e.activation(out=MV[:, si, i0:i1, 2], in_=MV[:, si, i0:i1, 1], func=AF.Sqrt, bias=epsc[:, 0:1]),
           reads=[c_const] + c_mvt[si][i0:i1], writes=[c])
        op("dve", lambda e: e.reciprocal(out=MV[:, si, i0:i1, 2], in_=MV[:, si, i0:i1, 2]), reads=[c], writes=[c])
        op("dve", lambda e: e.scalar_tensor_tensor(out=MV[:, si, i0:i1, 3], in0=MV[:, si, i0:i1, 0], scalar=-1.0,
                                                   in1=MV[:, si, i0:i1, 2], op0=ALU.mult, op1=ALU.mult),
           reads=[c] + c_mvt[si][i0:i1], writes=[c])

    def affine_block(tiles, lg, lb, c_l):
        for i in tiles:
            op("act", lambda e: e.activation(out=X[:, i, :], in_=X[:, i, :], func=AF.Identity, bias=MV[:, 0, i, 3:4],
                                             scale=MV[:, 0, i, 2:3]), reads=[c_mvs[0][i // 4]], writes=[c_x[i]])
        for i in tiles:
            op("dve", lambda e: e.tensor_tensor(out=X[:, i, :], in0=X[:, i, :], in1=lg, op=ALU.mult),
               reads=[c_l], writes=[c_x[i]])
        for i in tiles:
            op("dve", lambda e: e.tensor_tensor(out=X[:, i, :], in0=X[:, i, :], in1=lb, op=ALU.add),
               reads=[c_l], writes=[c_x[i]])

    def ln_pipeline(NT, s1, s2, s3):
        nb = NT // 4
        for step in range(nb + 2):
            if step < nb:
                s1(step)
            if 0 <= step - 1 < nb and s2 is not None:
                s2(step - 1)
            if 0 <= step - 2 < nb and s3 is not None:
                s3(step - 2)

    xnb_state = {"i": 0}

    def ht_block(sg, l, s_shift, xnb, c_xnb, tiles):
        assert len(xnb) >= len(tiles)
        rr = []
        for i in tiles:
            r = xnb_state["i"] % len(xnb)
            xnb_state["i"] += 1
            rr.append(r)
            op("act", lambda e: e.activation(out=xnb[r], in_=X[:, i, :], func=AF.Identity, bias=MV[:, 1, i, 3:4],
                                             scale=MV[:, 1, i, 2:3]), reads=[c_x[i], c_mvs[1][i // 4]], writes=[c_xnb[r]])
        bks = []
        for i, r in zip(tiles, rr):
            pb, pc = bank()
            pbv = pb[:].bitcast(BF16).rearrange("p (k q) -> p k q", k=8)
            bks.append((pbv, pc))
            for k in range(8):
                op("pe", lambda e: e.transpose(out=pbv[:, k, :], in_=xnb[r][:, k * 128:(k + 1) * 128], identity=identb[:]),
                   reads=[c_xnb[r], c_const], writes=[pc], signal=(k == 7))
        for i, (pbv, pc) in zip(tiles, bks):
            for k in range(8):
                dst = HT[:, k, i * 128:(i + 1) * 128]
                sc_ = modc(l, s_shift + 1, k, sg.g)
                sh_ = modc(l, s_shift, k, sg.g)
                if i % 2 == 0:
                    op("act", lambda e: e.activation(out=dst, in_=pbv[:, k, :], func=AF.Identity, bias=sh_, scale=sc_),
                       reads=[pc, c_mod], writes=[c_ht[k][i]])
                else:
                    op("dve", lambda e: e.tensor_scalar(out=dst, in0=pbv[:, k, :], scalar1=sc_, scalar2=sh_,
                                                        op0=ALU.mult, op1=ALU.add),
                       reads=[pc, c_mod], writes=[c_ht[k][i]])

    pref = {}
    wA = carve(RW, 0, [128, 8, 1024]); brow = carve(RW, 8192, [128, 1024])
    wA_flat = carve(RW, 0, [128, 8192])

    def load_wA(sg, l, extra=()):
        c_wA = Cell()
        ex = list(extra)
        if sg.latent:
            dma("pool", wA[:, :, 0:768], w_in[l, :, 0:768].rearrange("(k p) n -> p k n", p=128), "w0", writes=[c_wA] + ex)
            dma("pool", wA[:, :, 768:1024], w_in[l, :, 1280:1536].rearrange("(k p) n -> p k n", p=128), "w0", writes=[c_wA])
            dma("pool", brow[0:1, 0:768], b_in[l, 0:768].rearrange("(o n) -> o n", o=1), "w0", writes=[c_wA])
            dma("pool", brow[0:1, 768:1024], b_in[l, 1280:1536].rearrange("(o n) -> o n", o=1), "w0", writes=[c_wA])
        else:
            dma("sp", wA_flat, scA[l], "w0", writes=[c_wA] + ex)
            dma("sp", brow[0:1, :], scB[l], "w0", writes=[c_wA])
        return c_wA

    def run_layer(sg, l, first, last, nxt=None):
        T, NT, NN, L, nseq, g = sg.T, sg.NT, sg.NN, sg.L, sg.nseq, sg.g
        TPS = L // 128
        if first:
            xnb = [carve(RB, r * 1024, [128, 1024]) for r in range(4)]
            c_xnb = cells(4)
            if pre0.pop(sg.name, False):
                for b_ in range(NT // 4):
                    ht_block(sg, l, 0, xnb, c_xnb, list(range(4 * b_, 4 * b_ + 4)))
            else:
                for i in range(NT):
                    dma("sp", X[:, i, :], sg.xin[i * 128:(i + 1) * 128, :], f"x{i % 8}", reads=([c_x[i - 8]] if i >= 8 else []), writes=[c_x[i]])
                def p0_s1(b):
                    stats_block(list(range(4 * b, 4 * b + 4)), 1)
                    stats_finish(1, 4 * b, 4 * b + 4)
                ln_pipeline(NT, p0_s1, lambda b: ht_block(sg, l, 0, xnb, c_xnb, list(range(4 * b, 4 * b + 4))), None)
            kb.barrier()
        dump(f"ht_{sg.name}{l}", HT[:, :, 0:T], [c for row in c_ht for c in row])
        chk(f"p0_{sg.name}{l}")

        qT = carve(RA, 0, [128, 4, 2048]); kT = carve(RA, 8192, [128, 2048]); kcT = carve(RA, 10240, [128, 256])
        vtok = carve(RA, 10496, [128, 18, 128]); ftok = carve(RA, 12800, [128, 16, 256])
        gluT = carve(RA, 16896, [128, 2, 2304])
        c_qa = cells(4, 16)
        c_kt = cells(16); c_kc = Cell(); c_v = cells(18); c_f = cells(16); c_glu = cells(2, 4)
        wC = carve(RW, 9216, [128, 8, 512])
        c_wC = Cell()
        wC_flat = carve(RW, 9216, [128, 4096])
        c_wA = pref.pop((sg.name, l)) if (sg.name, l) in pref else load_wA(sg, l)
        if sg.latent:
            dma("pool", wC, w_in[l, :, 768:1280].rearrange("(k p) n -> p k n", p=128), "w1", writes=[c_wC])
            dma("sp", scA[l], wA_flat, "g0", reads=[c_wA])
            dma("sp", scB[l], brow[0:1, :], "g0", reads=[c_wA])
            dma("sp", scC[l], wC_flat, "g0", reads=[c_wC])
        else:
            dma("sp", wC_flat, scC[l], "w1", writes=[c_wC])
        if sg.latent:
            kctok = carve(RB, 0, [128, 2, 128])
            c_kct = Cell()
            dma("pool", kctok, ck[l].rearrange("(t p) n -> p t n", p=128), "w2", writes=[c_kct])
            dma("pool", vtok[:, 16:18, :], cv[l].rearrange("(t p) n -> p t n", p=128), "g1", writes=[c_v[16], c_v[17]])
        op("pool", lambda e: e.memset(gluT, 0.0), writes=[c for row in c_glu for c in row])

        qtok = [carve(RB, 1024 + s * 2560, [128, 4, 640]) for s in range(2)]
        c_qtok = cells(2)
        tA = carve(RB, 6144, [128, 640], F32); tB = carve(RB, 7424, [128, 640], F32)
        c_tAB = Cell()
        kvo = [carve(RB, 8704 + s * 512, [128, 256], F32) for s in range(2)]
        c_kvo = cells(2)
        cosb = lambda i, h: rope[:, 0, i, :].unsqueeze(1).unsqueeze(1).broadcast_to([128, h, 2, 32])
        sinb = lambda i, h: rope[:, 1, i, :].unsqueeze(1).unsqueeze(1).broadcast_to([128, h, 2, 32])
        for n in range(NN):
            s = n % 2
            for t in range(4):
                i = n * 4 + t
                bq, cq = bank()
                br, cr = bank()
                for bb, cb, c0 in ((bq, cq, 0), (br, cr, 512)):
                    for k in range(8):
                        op("pe", lambda e: e.matmul(out=bb[:], lhsT=HT[:, k, i * 128:(i + 1) * 128],
                                                    rhs=wA[:, k, c0:c0 + 512], start=(k == 0), stop=False),
                           reads=[c_ht[k][i], c_wA], writes=[cb], signal=False)
                    op("pe", lambda e: e.matmul(out=bb[:], lhsT=onesb[0:1, :], rhs=brow[0:1, c0:c0 + 512],
                                                start=False, stop=True), reads=[c_wA, c_const], writes=[cb])
                qdst = qtok[s][:, t, 0:512]
                kdst = qtok[s][:, t, 512:640]
                if sg.latent:
                    for src, dst, h, w in ((bq[:], qdst, 8, 512), (br[:, 0:128], kdst, 2, 128)):
                        s4 = src.rearrange("p (h two d) -> p h two d", two=2, d=32)
                        a4 = tA[:, 0:w].rearrange("p (h two d) -> p h two d", two=2, d=32)
                        b4 = tB[:, 0:w].rearrange("p (h two d) -> p h two d", two=2, d=32)
                        d4 = dst.rearrange("p (h two d) -> p h two d", two=2, d=32)
                        rc = [cq if h == 8 else cr, c_const]
                        op("dve", lambda e: e.tensor_tensor(out=a4, in0=s4, in1=cosb(i, h), op=ALU.mult),
                           reads=rc, writes=[c_tAB])
                        op("dve", lambda e: e.tensor_tensor(out=b4, in0=s4, in1=sinb(i, h), op=ALU.mult),
                           reads=rc, writes=[c_tAB])
                        op("dve", lambda e: e.tensor_tensor(out=d4[:, :, 0, :], in0=a4[:, :, 0, :], in1=b4[:, :, 1, :],
                                                            op=ALU.subtract), reads=[c_tAB], writes=[c_qtok[s]])
                        op("dve", lambda e: e.tensor_tensor(out=d4[:, :, 1, :], in0=a4[:, :, 1, :], in1=b4[:, :, 0, :],
                                                            op=ALU.add), reads=[c_tAB], writes=[c_qtok[s]])
                else:
                    op("dve", lambda e: e.tensor_copy(out=qdst, in_=bq[:]), reads=[cq], writes=[c_qtok[s]])
                    op("act", lambda e: e.copy(out=kdst, in_=br[:, 0:128]), reads=[cr], writes=[c_qtok[s]])
                    ko = i % 2
                    op("act", lambda e: e.copy(out=kvo[ko], in_=br[:, 0:256]), reads=[cr], writes=[c_kvo[ko]])
                    sq, tt = i // TPS, i % TPS
                    st1 = dma("sp", nk[sq, l, tt * 128:(tt + 1) * 128, :], kvo[ko][:, 0:128], f"g{ko}", reads=[c_kvo[ko]])
                    st2 = dma("sp", nv[sq, l, tt * 128:(tt + 1) * 128, :], kvo[ko][:, 128:256], f"g{ko}", reads=[c_kvo[ko]])
                    kb.out_stamps.append(st2)
                if sg.latent:
                    op("dve", lambda e: e.tensor_copy(out=vtok[:, i, :], in_=br[:, 128:256]), reads=[cr], writes=[c_v[i]])
                    op("dve", lambda e: e.tensor_copy(out=ftok[:, i, :], in_=br[:, 256:512]), reads=[cr], writes=[c_f[i]])
                else:
                    op("act", lambda e: e.copy(out=vtok[:, i, :], in_=br[:, 128:256]), reads=[cr], writes=[c_v[i]])
                    op("act", lambda e: e.copy(out=ftok[:, i, :], in_=br[:, 256:512]), reads=[cr], writes=[c_f[i]])
            for grp in ((0, 1), (2, 3), (4,)):
                pb, pc = bank()
                pbv = pb[:].bitcast(BF16).rearrange("p (c t q) -> p c t q", c=2, t=4)
                for ci, c in enumerate(grp):
                    for t in range(4):
                        sig_ = (ci == len(grp) - 1 and t == 3)
                        if c < 4:
                            for hq in range(2):
                                cq0 = hq * 256 + c * 64
                                op("pe", lambda e: e.transpose(out=pbv[hq * 64:(hq + 1) * 64, ci, t, :], in_=qtok[s][:, t, cq0:cq0 + 64],
                                                               identity=identb[:]),
                                   reads=[c_qtok[s], c_const], writes=[pc], signal=(sig_ and hq == 1))
                        else:
                            op("pe", lambda e: e.transpose(out=pbv[:, ci, t, :], in_=qtok[s][:, t, 512:640], identity=identb[:]),
                               reads=[c_qtok[s], c_const], writes=[pc], signal=sig_)
                for ci, c in enumerate(grp):
                    src = pbv[:, ci, :, :].rearrange("p t q -> p (t q)")
                    if c < 4:
                        wr = [c_qa[c][n * 4 + t] for t in range(4)]
                        dst = qT[:, c, n * 512:(n + 1) * 512]
                    else:
                        wr = [c_kt[n * 4 + t] for t in range(4)]
                        dst = kT[:, n * 512:(n + 1) * 512]
                    if grp[0] != 2:
                        op("act", lambda e: e.copy(out=dst, in_=src), reads=[pc], writes=wr)
                    else:
                        op("dve", lambda e: e.tensor_copy(out=dst, in_=src), reads=[pc], writes=wr)
        sig = [carve(RB, 9728 + s * 1024, [128, 512], F32) for s in range(2)]
        c_sig = cells(2)
        si = 0
        for n in range(NN):
            bs = [bank() for _ in range(4)]
            for cidx in range(4):
                bb, cb = bs[cidx]
                for k in range(8):
                    op("pe", lambda e: e.matmul(out=bb[:], lhsT=wC[:, k, cidx * 128:(cidx + 1) * 128],
                                                rhs=HT[:, k, n * 512:(n + 1) * 512], start=(k == 0), stop=(k == 7)),
                       reads=c_ht[k][n * 4:n * 4 + 4] + [c_wC], writes=[cb], signal=(k == 7))
            for c in range(2):
                s = si % 2
                si += 1
                ba, ca = bs[c]
                bg_, cg_ = bs[2 + c]
                op("act", lambda e: e.activation(out=sig[s], in_=bg_[:], func=AF.Sigmoid, bias=cvc(l, "bconv", 2 + c)),
                   reads=[cg_, c_cv], writes=[c_sig[s]])
                nsq = 512 // L if L < 512 else 1
                if L >= 512:
                    dst = gluT[:, c, 15 + n * 512:15 + (n + 1) * 512]
                    in0 = ba[:]
                    in1 = sig[s]
                else:
                    dst = gluT[:, c, 0:nseq * sg.gpad].rearrange("p (s w) -> p s w", w=sg.gpad)[:, n * nsq:(n + 1) * nsq, 15:15 + L]
                    in0 = ba[:].rearrange("p (s w) -> p s w", w=L)
                    in1 = sig[s].rearrange("p (s w) -> p s w", w=L)
                op("dve", lambda e: e.scalar_tensor_tensor(out=dst, in0=in0, scalar=cvc(l, "bconv", c), in1=in1,
                                                           op0=ALU.add, op1=ALU.mult),
                   reads=[ca, c_sig[s], c_cv], writes=[c_glu[c][n]])
        if sg.latent:
            pb, pc = bank()
            pbv = pb[:].bitcast(BF16)
            for t in range(2):
                op("pe", lambda e: e.transpose(out=pbv[:, t * 128:(t + 1) * 128], in_=kctok[:, t, :], identity=identb[:]),
                   reads=[c_kct, c_const], writes=[pc], signal=(t == 1))
            op("dve", lambda e: e.tensor_copy(out=kcT, in_=pbv[:, 0:256]), reads=[pc], writes=[c_kc])
        kb.barrier()
        dump(f"qT_{sg.name}{l}", qT[:, :, 0:T]); dump(f"kT_{sg.name}{l}", kT[:, 0:T])
        dump(f"glu_{sg.name}{l}", gluT)
        dump(f"vtok_{sg.name}{l}", vtok); dump(f"ftok_{sg.name}{l}", ftok)
        chk(f"A_{sg.name}{l}")

        aoT = qT
        cuT = carve(RB, 0, [128, 2, 2048]); fmT = carve(RB, 4096, [128, 2, 2048])
        c_cu = cells(2, 4); c_fm = cells(2, 4)
        LA = 2
        NPT = 6 if LA == 2 else 4
        PTW = (384 if sg.latent else 640) if LA == 2 else 640
        pTr = [carve(RB, 8192 + s * PTW, [128, PTW]) for s in range(NPT)]
        c_pT = cells(NPT)
        if sg.latent:
            rec2 = [carve(RB, 10752, [128, 256], F32), carve(RB, 11264, [128, 256], F32)]
        else:
            rec2 = [carve(RW, 5632, [128, 256], F32), carve(RW, 6144, [128, 256], F32)]
        c_rec = cells(2)
        vAB = carve(RW, 0, [128, 18, 2, 128])
        srow = carve(RW, 4608, [128, 4, 2, 128])
        c_vab = Cell()
        nvt = 18 if sg.latent else NT
        op("dve", lambda e: e.memset(vAB, 1.0), writes=[c_vab])
        op("dve", lambda e: e.tensor_copy(out=vAB[:, 0:nvt, 0, 0:64], in_=vtok[:, 0:nvt, 0:64]), reads=c_v[0:nvt], writes=[c_vab])
        op("dve", lambda e: e.tensor_copy(out=vAB[:, 0:nvt, 1, 64:128], in_=vtok[:, 0:nvt, 64:128]), reads=c_v[0:nvt], writes=[c_vab])
        c_srow = cells(4, 2)
        op("dve", lambda e: e.memset(srow[0:1], 0.0), writes=[c_ for row in c_srow for c_ in row])
        for c in range(4):
            for hh in range(2):
                h_ = c + 4 * hh
                lo = 64 if hh == 0 else 0
                op("dve", lambda e: e.tensor_copy(out=srow[0:1, c, hh, lo:lo + 64], in_=e8r[0:1, l, h_:h_ + 1].broadcast_to([1, 64])),
                   reads=[c_e8r], writes=[c_srow[c][hh]])
        pti = 0
        QW = 128 if sg.latent else L
        nqb = T // QW
        pstate = {"pti": 0}

        def keychunks(qb):
            if sg.latent:
                kch = []
                if qb > 0:
                    kch.append(("loc", qb - 1, 0))
                kch.append(("loc", qb, None))
                if qb < nqb - 1:
                    kch.append(("loc", qb + 1, 1))
                kch += [("ctx", 0, None), ("ctx", 1, None)]
                return [kch[0:3], kch[3:]]
            return [[("loc", qb * TPS + t, None) for t in range(TPS)]]

        def s_stage(qb, c, hh):
            q0 = qb * QW
            r0 = hh * 64
            qcells_idx = range(q0 // 128, (q0 + QW) // 128)
            pts = []
            for grp in keychunks(qb):
                if not grp:
                    continue
                sbk, sbc = bank()
                s = pstate["pti"] % NPT
                pstate["pti"] += 1
                w = len(grp) * QW
                for j, (kind, kt_, mk) in enumerate(grp):
                    if kind == "loc":
                        lhs = kT[r0:r0 + 64, kt_ * 128:(kt_ + 1) * 128]
                        rd = [c_kt[kt_]]
                    else:
                        lhs = kcT[r0:r0 + 64, kt_ * 128:(kt_ + 1) * 128]
                        rd = [c_kc]
                    rd += [c_qa[c][x_] for x_ in qcells_idx]
                    op("pe", lambda e: e.matmul(out=sbk[:, j * QW:(j + 1) * QW], lhsT=lhs,
                                                rhs=qT[r0:r0 + 64, c, q0:q0 + QW], start=True, stop=True),
                       reads=rd, writes=[sbc], signal=(j == len(grp) - 1))
                op("act", lambda e: e.activation(out=pTr[s][:, 0:w], in_=sbk[:, 0:w], func=AF.Exp, scale=0.125),
                   reads=[sbc], writes=[c_pT[s]])
                for j, (kind, kt_, mk) in enumerate(grp):
                    if mk is not None:
                        m0 = 0 if mk == 0 else 256
                        op("dve", lambda e: e.tensor_tensor(out=pTr[s][:, j * QW:(j + 1) * QW], in0=pTr[s][:, j * QW:(j + 1) * QW],
                                                             in1=mask01[:, m0:m0 + 128], op=ALU.mult),
                           reads=[c_const], writes=[c_pT[s]])
                for j, (kind, kt_, mk) in enumerate(grp):
                    vt = kt_ if kind == "loc" else 16 + kt_
                    pts.append((s, j, vt))
            return pts

        obanks = {}

        def v_stage(qb, c, hh, pts):
            q0 = qb * QW
            qcells_idx = range(q0 // 128, (q0 + QW) // 128)
            ob, oc = bank()
            obanks[hh] = (ob, oc)
            for jj, (s, j, vt) in enumerate(pts):
                op("pe", lambda e: e.matmul(out=ob[:, 0:QW], lhsT=vAB[:, vt, hh, :], rhs=pTr[s][:, j * QW:(j + 1) * QW],
                                            start=(jj == 0), stop=False),
                   reads=[c_pT[s], c_vab], writes=[oc], signal=False)
            op("pe", lambda e: e.matmul(out=ob[:, 0:QW], lhsT=srow[0:1, c, hh, :], rhs=onesr[0:1, 0:QW], start=False, stop=True),
               reads=[c_srow[c][hh], c_const], writes=[oc])
            if hh == 1:
                for h2 in range(2):
                    r0 = h2 * 64
                    d0 = 64 - r0
                    ob2, oc2 = obanks[h2]
                    rr_ = rec2[h2]
                    op("act", lambda e: e.activation(out=rr_[r0:r0 + 64, 0:QW], in_=ob2[d0:d0 + 64, 0:QW], func=AF.Ln),
                       reads=[oc2], writes=[c_rec[h2]])
                    op("act", lambda e: e.activation(out=rr_[r0:r0 + 64, 0:QW], in_=rr_[r0:r0 + 64, 0:QW], func=AF.Exp, scale=-1.0),
                       reads=[c_rec[h2]], writes=[c_rec[h2]])
                    op("dve", lambda e: e.tensor_tensor(out=aoT[r0:r0 + 64, c, q0:q0 + QW], in0=ob2[r0:r0 + 64, 0:QW],
                                                        in1=rr_[r0:r0 + 64, 0:QW], op=ALU.mult),
                       reads=[oc2, c_rec[h2]], writes=[c_qa[c][x_] for x_ in qcells_idx])

        job = None
        if sg.latent and l == 0 and mod_late is not None:
            pm, pmc, pmi = kb.reserve()
            job = ModJob(mod_late, [carve(RW, 5632 + s_ * 4096, [128, 8, 512]) for s_ in range(2)], ["w1", "w2"], pm, pmc)
            job.load()
        pendq = []
        ui = 0
        for qb in range(nqb):
            for c in range(4):
                for hh in range(2):
                    pts = s_stage(qb, c, hh)
                    pendq.append((qb, c, hh, pts))
                    if len(pendq) > LA:
                        v_stage(*pendq.pop(0))
                    ui += 1
                    if job is not None and ui % 10 == 0:
                        job.step()
        while pendq:
            v_stage(*pendq.pop(0))
        if job is not None:
            job.finish()
            kb.reserved.discard(pmi)
        kb.barrier()
        dump(f"ao_{sg.name}{l}", aoT[:, :, 0:T])
        chk(f"B1_{sg.name}{l}")

        wG = [carve(RW, s * 4096, [128, 8, 3, 128]) for s in range(2)]
        wO3 = [carve(RW, s * 4096 + 3072, [128, 8, 128]) for s in range(2)]
        wGO_flat = [carve(RW, s * 4096, [128, 4096]) for s in range(2)]
        c_wG = cells(2)
        def load_wG(j):
            s = j % 2
            if sg.latent:
                for gi in range(3):
                    c0 = 1536 + gi * 1024 + j * 128
                    dma("pool", wG[s][:, :, gi, :], w_in[l, :, c0:c0 + 128].rearrange("(k p) n -> p k n", p=128),
                        f"w{s}", writes=[c_wG[s]])
                for hh in range(2):
                    dma("pool", wO3[s][hh * 64:(hh + 1) * 64, 0:4, :],
                        w_attn_o[l, hh * 256:(hh + 1) * 256, j * 128:(j + 1) * 128].rearrange("(c d) n -> d c n", d=64),
                        f"w{s}", writes=[c_wG[s]])
                dma("pool", wO3[s][:, 4:6, :], w_conv_o[l, :, j * 128:(j + 1) * 128].rearrange("(k p) n -> p k n", p=128),
                    f"w{s}", writes=[c_wG[s]])
                dma("pool", wO3[s][:, 6:8, :], w_fnet[l, :, j * 128:(j + 1) * 128].rearrange("(k p) n -> p k n", p=128),
                    f"w{s}", writes=[c_wG[s]])
            else:
                dma("sp", wGO_flat[s], scG[l, j], f"w{s}", writes=[c_wG[s]])

        dfs = [carve(RA, 8192, [128, 2, 4, 512]), carve(RW, 4096, [128, 2, 4, 512]), carve(RW, 8192, [128, 2, 4, 512])]
        dfs_keys = ["g0", "g1", "w2"]
        c_dfs = cells(3)
        if sg.latent:
            dma("sp", dfs[0].rearrange("p a b c -> p (a b c)"), c_dft2k[0, 0], dfs_keys[0], writes=[c_dfs[0]])
        dg = carve(RW, 0, [128, 2, 31, 128])
        c_dg = cells(2, 31)
        for c in range(2):
            for k in range(31):
                op("dve", lambda e: e.tensor_scalar(out=dg[:, c, k, :], in0=identb[:], scalar1=cvc(l, "cw", k * 2 + c),
                                                    scalar2=None, op0=ALU.mult), reads=[c_cv, c_const], writes=[c_dg[c][k]])
        uu = carve(RW, 7936, [128, 2, 512], F32); usq = carve(RW, 9984, [128, 2, 512], F32)
        stt = carve(RW, 12032, [128, 2, 512], F32)
        c_uu = Cell(); c_stt = Cell()
        for n in range(NN):
            cbs = [bank() for _ in range(2)]
            nsq = max(1, 512 // L)
            W = min(L, 512)
            for c in range(2):
                bb, cb = cbs[c]
                for sq in range(nsq):
                    if L >= 512:
                        base = n * 512
                    else:
                        base = (n * nsq + sq) * sg.gpad
                    for k in range(31):
                        op("pe", lambda e: e.matmul(out=bb[:, sq * W:(sq + 1) * W], lhsT=dg[:, c, k, :],
                                                    rhs=gluT[:, c, base + k:base + k + W], start=(k == 0), stop=(k == 30)),
                           reads=[c_dg[c][k]] + c_glu[c], writes=[cb], signal=(k == 30 and sq == nsq - 1))
                op("act", lambda e: e.activation(out=uu[:, c, :], in_=bb[:], func=AF.Identity, bias=cvc(l, "convb", c)),
                   reads=[cb, c_cv], writes=[c_uu])
                op("act", lambda e: e.activation(out=usq[:, c, :], in_=uu[:, c, :], func=AF.Square), writes=[c_uu])
            bm_, cm_ = bank()
            bv_, cv_ = bank()
            for (bb, cb, srcT) in ((bm_, cm_, uu), (bv_, cv_, usq)):
                for c in range(2):
                    op("pe", lambda e: e.matmul(out=bb[:], lhsT=onesd[:], rhs=srcT[:, c, :], start=(c == 0), stop=(c == 1)),
                       reads=[c_uu, c_const], writes=[cb], signal=(c == 1))
            op("dve", lambda e: e.tensor_copy(out=stt[:, 0, :], in_=bm_[:]), reads=[cm_], writes=[c_stt])
            op("dve", lambda e: e.tensor_tensor(out=stt[:, 1, :], in0=stt[:, 0, :], in1=stt[:, 0, :], op=ALU.mult), writes=[c_stt])
            op("dve", lambda e: e.tensor_tensor(out=stt[:, 1, :], in0=bv_[:], in1=stt[:, 1, :], op=ALU.subtract),
               reads=[cv_], writes=[c_stt])
            op("act", lambda e: e.activation(out=stt[:, 1, :], in_=stt[:, 1, :], func=AF.Sqrt, bias=epsc[:, 0:1]),
               reads=[c_stt, c_const], writes=[c_stt])
            op("dve", lambda e: e.reciprocal(out=stt[:, 1, :], in_=stt[:, 1, :]), reads=[c_stt], writes=[c_stt])
            for c in range(2):
                op("dve", lambda e: e.tensor_tensor(out=uu[:, c, :], in0=uu[:, c, :], in1=stt[:, 0, :], op=ALU.subtract),
                   reads=[c_stt], writes=[c_uu])
                op("dve", lambda e: e.tensor_tensor(out=uu[:, c, :], in0=uu[:, c, :], in1=stt[:, 1, :], op=ALU.mult),
                   writes=[c_uu])
                op("act", lambda e: e.activation(out=cuT[:, c, n * 512:(n + 1) * 512], in_=uu[:, c, :], func=AF.Silu,
                                                 bias=cvc(l, "lnb", c), scale=cvc(l, "lng", c)),
                   reads=[c_uu, c_cv], writes=[c_cu[c][n]])
        kb.barrier()
        dump(f"cu_{sg.name}{l}", cuT[:, :, 0:T])
        chk(f"B2_{sg.name}{l}")

        u12 = [carve(RB, 8192 + s * 2048, [128, 2, 2, 512]) for s in range(2)]
        c_u12 = cells(2)
        BCi = 2 if sg.latent else 0
        load_wG(0)
        if sg.latent:
            di = 0
            for n in range(4):
                ub = [[bank() for c in range(2)] for cs in range(2)]
                for kg in range(4):
                    s = di % 3
                    di += 1
                    if di > 1:
                        dma("sp", dfs[s].rearrange("p a b c -> p (a b c)"), c_dft2k[n, kg], dfs_keys[s], writes=[c_dfs[s]])
                    for k4 in range(4):
                        kt_ = kg * 4 + k4
                        for cs in range(2):
                            for c in range(2):
                                bb, cb = ub[cs][c]
                                op("pe", lambda e: e.matmul(out=bb[:], lhsT=ftok[:, kt_, c * 128:(c + 1) * 128],
                                                            rhs=dfs[s][:, cs, k4, :], start=(kt_ == 0), stop=(kt_ == 15)),
                                   reads=[c_f[kt_], c_dfs[s]], writes=[cb], signal=(kt_ == 15 or (k4 == 3 and cs == 1 and c == 1)))
                us = n % 2
                for cs in range(2):
                    for c in range(2):
                        bb, cb = ub[cs][c]
                        if c == 0:
                            op("act", lambda e: e.copy(out=u12[us][:, cs, c, :], in_=bb[:]), reads=[cb], writes=[c_u12[us]])
                        else:
                            op("dve", lambda e: e.tensor_copy(out=u12[us][:, cs, c, :], in_=bb[:]), reads=[cb], writes=[c_u12[us]])
                for c in range(2):
                    yb, yc = bank()
                    for cs in range(2):
                        op("pe", lambda e: e.matmul(out=yb[:], lhsT=bcs[:, BCi + cs, :], rhs=u12[us][:, cs, c, :],
                                                    start=(cs == 0), stop=(cs == 1)),
                           reads=[c_u12[us], c_const], writes=[yc], signal=(cs == 1))
                    op("act", lambda e: e.copy(out=fmT[:, c, n * 512:(n + 1) * 512], in_=yb[:]), reads=[yc], writes=[c_fm[c][n]])
        else:
            for sp_ in range(nseq // 2):
                us = sp_ % 2
                for cs in range(2):
                    for c in range(2):
                        bb, cb = bank()
                        for sq in range(2):
                            sidx = sp_ * 2 + sq
                            for k2 in range(2):
                                kt_ = sidx * 2 + k2
                                op("pe", lambda e: e.matmul(out=bb[:, sq * 256:(sq + 1) * 256],
                                                            lhsT=ftok[:, kt_, c * 128:(c + 1) * 128],
                                                            rhs=dft256[:, cs, k2, :], start=(k2 == 0), stop=(k2 == 1)),
                                   reads=[c_f[kt_], c_const], writes=[cb], signal=(sq == 1 and k2 == 1))
                        if c == 0:
                            op("act", lambda e: e.copy(out=u12[us][:, cs, c, :], in_=bb[:]), reads=[cb], writes=[c_u12[us]])
                        else:
                            op("dve", lambda e: e.tensor_copy(out=u12[us][:, cs, c, :], in_=bb[:]), reads=[cb], writes=[c_u12[us]])
                for c in range(2):
                    yb, yc = bank()
                    for cs in range(2):
                        op("pe", lambda e: e.matmul(out=yb[:], lhsT=bcs[:, BCi + cs, :], rhs=u12[us][:, cs, c, :],
                                                    start=(cs == 0), stop=(cs == 1)),
                           reads=[c_u12[us], c_const], writes=[yc], signal=(cs == 1))
                    op("act", lambda e: e.copy(out=fmT[:, c, sp_ * 512:(sp_ + 1) * 512], in_=yb[:]), reads=[yc],
                       writes=[c_fm[c][sp_]])
        kb.barrier()
        dump(f"fm_{sg.name}{l}", fmT[:, :, 0:T])
        chk(f"B3_{sg.name}{l}")

        mT = [carve(RA, 8192 + j * 2048, [128, 2048]) for j in range(6)] + \
             [carve(RB, 8192 + (j - 6) * 2048, [128, 2048]) for j in range(6, 8)]
        c_m = cells(8, 4)
        sgt = [carve(RW, 10240 + s * 1024, [128, 512], F32) for s in range(3)]
        c_sgt = cells(3)
        m1 = carve(RW, 13312, [128, 512], F32)
        c_m1 = Cell()
        sgi = 0
        for j in range(8):
            s = j % 2
            if j + 1 < 8:
                load_wG(j + 1)
            if (not sg.latent) and 2 <= j <= 5:
                for k in (2 * (j - 2), 2 * (j - 2) + 1):
                    dma("sp", X[:, 8 + k, :], w_o[l, k * 128:(k + 1) * 128, :], f"x{k}", writes=[c_x[8 + k]])
            for n in range(NN):
                bg3 = [bank() for _ in range(3)]
                for gi in range(3):
                    bb, cb = bg3[gi]
                    for k in range(8):
                        op("pe", lambda e: e.matmul(out=bb[:], lhsT=wG[s][:, k, gi, :], rhs=HT[:, k, n * 512:(n + 1) * 512],
                                                    start=(k == 0), stop=(k == 7)),
                           reads=[c_wG[s]] + c_ht[k][n * 4:n * 4 + 4], writes=[cb], signal=(k == 7))
                ba3 = [bank() for _ in range(3)]
                srcs = [(aoT, 0, 4, [c_qa[c][n * 4 + t] for c in range(4) for t in range(4)]),
                        (cuT, 4, 2, [c_cu[c][n] for c in range(2)]), (fmT, 6, 2, [c_fm[c][n] for c in range(2)])]
                for ai, (srcT, w0, nk_, rd) in enumerate(srcs):
                    bb, cb = ba3[ai]
                    for kk in range(nk_):
                        op("pe", lambda e: e.matmul(out=bb[:], lhsT=wO3[s][:, w0 + kk, :], rhs=srcT[:, kk, n * 512:(n + 1) * 512],
                                                    start=(kk == 0), stop=(kk == nk_ - 1)),
                           reads=[c_wG[s]] + rd, writes=[cb], signal=(kk == nk_ - 1))
                sl = []
                for gi in range(3):
                    ss = sgi % 3
                    sgi += 1
                    sl.append(ss)
                    op("act", lambda e: e.activation(out=sgt[ss], in_=bg3[gi][0][:], func=AF.Sigmoid,
                                                     bias=cvc(l, "bgate", gi * 8 + j)),
                       reads=[bg3[gi][1], c_cv], writes=[c_sgt[ss]])
                op("dve", lambda e: e.tensor_tensor(out=sgt[sl[0]], in0=sgt[sl[0]], in1=ba3[0][0][:], op=ALU.mult),
                   reads=[ba3[0][1]], writes=[c_sgt[sl[0]]])
                op("dve", lambda e: e.tensor_tensor(out=sgt[sl[1]], in0=sgt[sl[1]], in1=ba3[1][0][:], op=ALU.mult),
                   reads=[ba3[1][1]], writes=[c_sgt[sl[1]]])
                op("dve", lambda e: e.scalar_tensor_tensor(out=sgt[sl[2]], in0=ba3[2][0][:], scalar=cvc(l, "bfnet", j),
                                                           in1=sgt[sl[2]], op0=ALU.add, op1=ALU.mult),
                   reads=[ba3[2][1], c_cv], writes=[c_sgt[sl[2]]])
                op("dve", lambda e: e.tensor_tensor(out=m1, in0=sgt[sl[0]], in1=sgt[sl[1]], op=ALU.add),
                   reads=[c_sgt[sl[0]], c_sgt[sl[1]]], writes=[c_m1])
                op("dve", lambda e: e.tensor_tensor(out=mT[j][:, n * 512:(n + 1) * 512], in0=m1, in1=sgt[sl[2]], op=ALU.add),
                   reads=[c_sgt[sl[2]], c_m1], writes=[c_m[j][n]])
            if sg.latent:
                dma("sp", scG[l, j], wGO_flat[s], f"g{s}", reads=[c_wG[s]])
        kb.barrier()
        if any(k_.startswith("mg_") for k_ in dbg):
            for j in range(8):
                if f"mg_{sg.name}{l}" in dbg:
                    st = dma("pool", dbg[f"mg_{sg.name}{l}"][:, j, :], mT[j][:, 0:T], "dbg")
                    kb.out_stamps.append(st)

        chk(f"C_{sg.name}{l}")
        wo = carve(RW, 0, [128, 8, 1024])
        c_wo = cells(8)
        lng_ = carve(RW, 8192, [128, 1024], F32); lnb_ = carve(RW, 10240, [128, 1024], F32)
        c_lnt = Cell()
        stg2 = [carve(RB, s * 2048, [128, 1024], F32) for s in range(2)]
        c_stg2 = cells(2)
        gbc = carve(RB, 4096, [128, 1024], F32)
        c_gbc = Cell()
        xnb = [carve(RA, s * 1024, [128, 1024]) for s in range(4)]
        c_xnb = cells(4)

        def make_gbc(s_idx):
            for half in range(2):
                pb, pc = bank()
                for jj in range(4):
                    j = half * 4 + jj
                    op("dve", lambda e: e.tensor_scalar(out=dgf[:], in0=identf[:], scalar1=modc(l, s_idx, j, g), scalar2=None,
                                                        op0=ALU.mult), reads=[c_mod, c_const], writes=[c_dgf])
                    op("pe", lambda e: e.matmul(out=pb[:, jj * 128:(jj + 1) * 128], lhsT=onesf[:], rhs=dgf[:], start=True, stop=True),
                       reads=[c_dgf, c_const], writes=[pc])
                op("act", lambda e: e.copy(out=gbc[:, half * 512:(half + 1) * 512], in_=pb[:]), reads=[pc], writes=[c_gbc])

        make_gbc(2)
        dma("sp", lng_, ln1_g[l].partition_broadcast(128), "w2", writes=[c_lnt])
        dma("sp", lnb_, ln1_b[l].partition_broadcast(128), "w2", writes=[c_lnt])
        for k in range(8):
            s = k % 2
            if sg.latent:
                dma("sp", stg2[s], w_o[l, k * 128:(k + 1) * 128, :], f"g{s}", writes=[c_stg2[s]])
                src_w, rc_w = stg2[s], c_stg2[s]
            else:
                src_w, rc_w = X[:, 8 + k, :], c_x[8 + k]
            op("pool" if (sg.latent or k % 2) else "dve", lambda e: e.tensor_tensor(out=wo[:, k, :], in0=src_w, in1=gbc, op=ALU.mult),
               reads=[rc_w, c_gbc], writes=[c_wo[k]])
        wU = [carve(RW, s * 2048, [128, 8, 2, 128]) for s in range(2)]
        wU_flat = [carve(RW, s * 2048, [128, 2048]) for s in range(2)]
        c_wU = cells(2)
        def load_wU(j, extra=()):
            s = j % 2
            ex = list(extra)
            if sg.latent:
                for b in range(2):
                    c0 = b * DFF + j * 128
                    dma("pool", wU[s][:, :, b, :], w_up[l, :, c0:c0 + 128].rearrange("(k p) n -> p k n", p=128),
                        f"w{s}", writes=[c_wU[s]] + ex)
                    ex = []
            else:
                dma("sp", wU_flat[s], scU[l, j], f"w{s}", writes=[c_wU[s]] + ex)

        def d_s1(b):
            for i0 in (4 * b, 4 * b + 2):
                for i in (i0, i0 + 1):
                    n = i // 4
                    for half in range(2):
                        pb, pc = bank()
                        for k in range(8):
                            op("pe", lambda e: e.matmul(out=pb[:], lhsT=mT[k][:, i * 128:(i + 1) * 128],
                                                        rhs=wo[:, k, half * 512:(half + 1) * 512], start=(k == 0), stop=(k == 7)),
                               reads=[c_m[k][n], c_wo[k]], writes=[pc], signal=(k == 7))
                        xs_ = X[:, i, half * 512:(half + 1) * 512]
                        op("dve", lambda e: e.scalar_tensor_tensor(out=xs_, in0=xs_, scalar=ALPHA, in1=pb[:], op0=ALU.mult, op1=ALU.add),
                           reads=[pc], writes=[c_x[i]])
                stats_block([i0, i0 + 1], 0)
            stats_finish(0, 4 * b, 4 * b + 4)
            if b == NT // 4 - 1:
                load_wU(0, extra=c_wo)

        def d_s2(b):
            blk = list(range(4 * b, 4 * b + 4))
            affine_block(blk, lng_, lnb_, c_lnt)
            stats_block(blk, 1)
            stats_finish(1, 4 * b, 4 * b + 4)

        ln_pipeline(NT, d_s1, d_s2, lambda b: ht_block(sg, l, 3, xnb, c_xnb, list(range(4 * b, 4 * b + 4))))
        kb.barrier()
        dump(f"x1_{sg.name}{l}", X[:, 0:NT, :])
        dump(f"h2_{sg.name}{l}", HT[:, :, 0:T])
        chk(f"D_{sg.name}{l}")

        gT = carve(RA, 0, [128, 6, 2048])
        c_gT = cells(6, 4)
        UW = 2080
        ub_ = [[carve(RA, 12288 + (s * 2 + b) * UW, [128, UW]) for b in range(2)] for s in range(2)]
        c_ub = cells(2, 2, 4)
        ct = [carve(RB, s * 1024, [128, 512], F32) for s in range(2)]
        c_ct = cells(2)
        dgu = [carve(RB, 2048 + s * 768, [128, 6, 128]) for s in range(2)]
        tv = [carve(RB, 10240 + s * 1024, [128, 512], F32) for s in range(2)]
        c_tv = cells(2)
        c_dgu = cells(2, 6)
        stg3 = [carve(RB, 4096 + s * 2048, [128, 1024], F32) for s in range(2)]
        c_stg3 = cells(2)
        gbc2 = carve(RB, 8192, [128, 1024], F32)
        xnb2 = [carve(RA, 12288 + s * 1024, [128, 1024]) for s in range(4)]
        c_xnb2 = cells(4)
        wD = carve(RW, 4096, [128, 6, 1024])
        c_wD = cells(6)
        lng2 = carve(RW, 10240, [128, 1024], F32); lnb2 = carve(RW, 12288, [128, 1024], F32)
        c_lnt2 = Cell()
        gbc = gbc2
        for s in range(2):
            for b in range(2):
                op("pool", lambda e: e.memset(ub_[s][b], 0.0), writes=c_ub[s][b])
        c_gbc = Cell()

        def build_gbc2():
            for half in range(2):
                pb, pc = bank()
                for jj in range(4):
                    j = half * 4 + jj
                    op("dve", lambda e: e.tensor_scalar(out=dgf[:], in0=identf[:], scalar1=modc(l, 5, j, g), scalar2=None,
                                                        op0=ALU.mult), reads=[c_mod, c_const], writes=[c_dgf])
                    op("pe", lambda e: e.matmul(out=pb[:, jj * 128:(jj + 1) * 128], lhsT=onesf[:], rhs=dgf[:], start=True, stop=True),
                       reads=[c_dgf, c_const], writes=[pc])
                op("act", lambda e: e.copy(out=gbc2[:, half * 512:(half + 1) * 512], in_=pb[:]), reads=[pc], writes=[c_gbc])

        dma("sp", lng2, ln2_g[l].partition_broadcast(128), "w2", writes=[c_lnt2])
        dma("sp", lnb2, ln2_b[l].partition_broadcast(128), "w2", writes=[c_lnt2])
        wui = 0
        cti = 0
        sdi = 0
        upad = sg.upad

        def uview(buf, n, shift):
            if L >= 512:
                return buf[:, n * 512 + shift:n * 512 + shift + 512]
            nsq = 512 // L
            return buf[:, 0:nseq * upad].rearrange("p (s w) -> p s w", w=upad)[:, n * nsq:(n + 1) * nsq, shift:shift + L]

        def tview(t):
            return t if L >= 512 else t.rearrange("p (s w) -> p s w", w=L)

        for gi, (j0, nj) in enumerate(FFN_GROUPS):
            for jj in range(nj):
                j = j0 + jj
                s = j % 2
                if j + 1 < NFF:
                    load_wU(j + 1)
                for n in range(NN):
                    for b in range(2):
                        pb, pc = bank()
                        for k in range(8):
                            op("pe", lambda e: e.matmul(out=pb[:], lhsT=wU[s][:, k, b, :], rhs=HT[:, k, n * 512:(n + 1) * 512],
                                                        start=(k == 0), stop=(k == 7)),
                               reads=[c_wU[s]] + c_ht[k][n * 4:n * 4 + 4], writes=[pc], signal=(k == 7))
                        op("act", lambda e: e.copy(out=uview(ub_[s][b], n, 1), in_=tview(pb[:])), reads=[pc],
                           writes=[c_ub[s][b][n]])
                if j == 0:
                    build_gbc2()
                for k3 in range(3):
                    col = k3 * 44 + j
                    op("dve", lambda e: e.tensor_scalar(out=dgu[s][:, k3, :], in0=identb[:], scalar1=cvc(l, "fcw", col),
                                                        scalar2=None, op0=ALU.mult), reads=[c_cv, c_const], writes=[c_dgu[s][k3]])
                for n in range(NN):
                    rdn = [m_ for m_ in (n - 1, n, n + 1) if 0 <= m_ < NN]
                    pb, pc = bank()
                    for k3 in range(3):
                        op("pe", lambda e: e.matmul(out=tview(pb[:]), lhsT=dgu[s][:, k3, :], rhs=uview(ub_[s][0], n, k3),
                                                    start=(k3 == 0), stop=(k3 == 2)),
                           reads=[c_dgu[s][k3]] + [c_ub[s][0][m_] for m_ in rdn], writes=[pc], signal=(k3 == 2))
                    r = cti % 2
                    cti += 1
                    op("act", lambda e: e.activation(out=ct[r], in_=pb[:], func=AF.Silu, bias=cvc(l, "fcb", j)),
                       reads=[pc, c_cv], writes=[c_ct[r]])
                    colv = NFF + j
                    rdv = [c_ub[s][1][m_] for m_ in rdn]
                    op("act", lambda e: e.activation(out=tview(tv[r]), in_=uview(ub_[s][1], n, 1), func=AF.Identity,
                                                     scale=cvc(l, "fcw", 44 + colv), bias=cvc(l, "fcb", colv)),
                       reads=rdv + [c_cv], writes=[c_tv[r]])
                    for sh, wrow in ((0, 0), (2, 2)):
                        op("dve", lambda e: e.scalar_tensor_tensor(out=tview(tv[r]), in0=uview(ub_[s][1], n, sh),
                                                                   scalar=cvc(l, "fcw", wrow * 44 + colv), in1=tview(tv[r]),
                                                                   op0=ALU.mult, op1=ALU.add), reads=rdv + [c_cv], writes=[c_tv[r]])
                    op("dve", lambda e: e.tensor_tensor(out=gT[:, jj, n * 512:(n + 1) * 512], in0=tv[r], in1=ct[r], op=ALU.mult),
                       reads=[c_tv[r], c_ct[r]], writes=[c_gT[jj][n]])
                sd = sdi % 2
                sdi += 1
                dma("sp", stg3[sd], w_down[l, j * 128:(j + 1) * 128, :], f"g{sd}", writes=[c_stg3[sd]])
                if sg.latent:
                    dma("sp", scU[l, j], wU_flat[s], f"x{s}", reads=[c_wU[s]])
                op("pool", lambda e: e.tensor_tensor(out=wD[:, jj, :], in0=stg3[sd], in1=gbc2, op=ALU.mult),
                   reads=[c_stg3[sd], c_gbc], writes=[c_wD[jj]])
            lastg = gi == len(FFN_GROUPS) - 1

            def e_s1(b, gi=gi, nj=nj, lastg=lastg):
                for i0 in (4 * b, 4 * b + 2):
                    for i in (i0, i0 + 1):
                        n = i // 4
                        for half in range(2):
                            pb, pc = bank()
                            for jj in range(nj):
                                op("pe", lambda e: e.matmul(out=pb[:], lhsT=gT[:, jj, i * 128:(i + 1) * 128],
                                                            rhs=wD[:, jj, half * 512:(half + 1) * 512], start=(jj == 0), stop=(jj == nj - 1)),
                                   reads=[c_gT[jj][n], c_wD[jj]], writes=[pc], signal=(jj == nj - 1))
                            xs_ = X[:, i, half * 512:(half + 1) * 512]
                            if gi == 0:
                                op("dve", lambda e: e.scalar_tensor_tensor(out=xs_, in0=xs_, scalar=ALPHA, in1=pb[:], op0=ALU.mult,
                                                                           op1=ALU.add), reads=[pc], writes=[c_x[i]])
                            else:
                                op("dve", lambda e: e.tensor_tensor(out=xs_, in0=xs_, in1=pb[:], op=ALU.add), reads=[pc],
                                   writes=[c_x[i]])
                    if lastg:
                        stats_block([i0, i0 + 1], 0)
                if lastg:
                    stats_finish(0, 4 * b, 4 * b + 4)
                    if b == NT // 4 - 1 and nxt is not None:
                        pref[(nxt[0].name, nxt[1])] = load_wA(nxt[0], nxt[1], extra=c_wU + c_wD)

            def e_s2(b):
                blk = list(range(4 * b, 4 * b + 4))
                affine_block(blk, lng2, lnb2, c_lnt2)
                if last:
                    for i in blk:
                        st = dma("sp", sg.yout[i * 128:(i + 1) * 128, :], X[:, i, :], f"x{i % 8}", reads=[c_x[i]])
                        kb.out_stamps.append(st)
                else:
                    stats_block(blk, 1)
                    stats_finish(1, 4 * b, 4 * b + 4)

            if not lastg:
                ln_pipeline(NT, e_s1, None, None)
            else:
                ln_pipeline(NT, e_s1, e_s2,
                            (lambda b: ht_block(sg, l + 1, 0, xnb2, c_xnb2, list(range(4 * b, 4 * b + 4)))) if not last else None)
        kb.barrier()

    segS_ = Seg("S", 2048, 1, True, 1, xs, ys)
    pre0 = {}
    if stop is None:
        sg0 = [q for q in ("S", "P") if q in segs][0]
        nt0 = 16 if sg0 == "S" else 8
        xin0 = xs if sg0 == "S" else xp
        for i in range(nt0):
            dma("sp", X[:, i, :], xin0[i * 128:(i + 1) * 128, :], f"x{i % 8}", reads=([c_x[i - 8]] if i >= 8 else []), writes=[c_x[i]])
        for b_ in range(nt0 // 4):
            stats_block(list(range(4 * b_, 4 * b_ + 4)), 1)
            stats_finish(1, 4 * b_, 4 * b_ + 4)
        pre0[sg0] = True
    for l in range(DEPTH):
        items = [("bmod", b_mod[l].rearrange("(c p) -> c p", p=128), 48),
                 ("bconv", b_in[l, 768:1280].rearrange("(c p) -> c p", p=128), 4),
                 ("bgate", b_in[l, 1536:4608].rearrange("(c p) -> c p", p=128), 24),
                 ("convb", conv_b[l].rearrange("(c p) -> c p", p=128), 2),
                 ("lng", conv_ln_g[l].rearrange("(c p) -> c p", p=128), 2),
                 ("lnb", conv_ln_b[l].rearrange("(c p) -> c p", p=128), 2),
                 ("bfnet", b_fnet[l].rearrange("(c p) -> c p", p=128), 8),
                 ("fcw", ffn_conv_w[l].rearrange("k (c p) -> (k c) p", p=128), 132),
                 ("fcb", ffn_conv_b[l].rearrange("(c p) -> c p", p=128), 44),
                 ("cw", conv_w[l].rearrange("k (c p) -> (k c) p", p=128), 62)]
        o = 0
        for name, ap, n in items:
            CVOFF[name] = o
            o += n
        assert o == NCV
        load_cols(CV[:, l, :], [(ap, n) for _, ap, n in items])
    if stop is None and "S" in segs:
        pref[("S", 0)] = load_wA(segS_, 0)
    for job in jobs:
        job.finish()
    modc = lambda l, s, k, g: MODT[:, l, s * 8 + k, g:g + 1]
    kb.barrier()
    if "modT" in dbg:
        kb.out_stamps.append(kb.dma("sp", dbg["modT"], MODT[:].rearrange("p l c g -> p (l c g)"), "dbg"))
        kb.out_stamps.append(kb.dma("sp", dbg["cv"], CV[:].rearrange("p l c -> p (l c)"), "dbg"))

    segP = Seg("P", 1024, 4, False, 0, xp, yp)
    segS = Seg("S", 2048, 1, True, 1, xs, ys)
    try:
        chk("mod")
        for sg in (segS, segP):
            if sg.name not in segs:
                continue
            for l in range(DEPTH):
                order = [q for q in (segS, segP) if q.name in segs]
                if l < DEPTH - 1:
                    nxt = (sg, l + 1)
                else:
                    qi = order.index(sg)
                    nxt = (order[qi + 1], 0) if qi + 1 < len(order) else None
                if stop is not None:
                    nxt = None
                run_layer(sg, l, l == 0, l == DEPTH - 1, nxt)
                chk(f"E_{sg.name}{l}")
    except StopBuild:
        pass
    kb.barrier()
    return nc


def host_consts():
    bf = ml_dtypes.bfloat16
    c = {}
    c["c_identb"] = np.eye(128, dtype=np.float32).astype(bf)
    c["c_identf"] = np.eye(128, dtype=np.float32)
    j = np.arange(128)[:, None]; i = np.arange(128)[None, :]
    mL = np.where(j >= i, 0.0, NEG); mR = np.where(j <= i, 0.0, NEG)
    c["c_mask"] = np.stack([mL, mR]).astype(np.float32).astype(bf)
    c["c_mask01"] = np.concatenate([(j >= i), np.ones((128, 128), bool), (j <= i)], 1).astype(np.float32).astype(bf)
    pos = np.arange(2048)
    row = (pos // 64).astype(np.float32); col = (pos % 64).astype(np.float32)
    inv = (10000.0 ** (-np.arange(16, dtype=np.float32) / 16)).astype(np.float32)
    ang = np.concatenate([row[:, None] * inv, col[:, None] * inv], -1).astype(np.float32)
    cs = np.stack([np.cos(ang), np.sin(ang)]).astype(np.float32)
    c["c_rope"] = np.ascontiguousarray(cs.reshape(2, 16, 128, 32).transpose(0, 2, 1, 3))
    def dft(Ln):
        t = np.arange(Ln, dtype=np.int64)
        ph = (2.0 * np.pi / Ln) * ((t[:, None] * t[None, :]) % Ln).astype(np.float64)
        return np.stack([np.cos(ph), np.sin(ph)])
    c["c_dft256"] = dft(256).astype(np.float32).astype(bf)
    d2k = dft(2048).astype(np.float32).astype(bf)
    d2k = d2k.reshape(2, 4, 4, 128, 4, 512).transpose(4, 1, 3, 0, 2, 5)
    c["c_dft2k"] = np.ascontiguousarray(d2k).reshape(4, 4, 128, 2 * 4 * 512)
    d64 = dft(64)
    blocks = []
    for Ln in (256, 2048):
        nrm = 1.0 / np.sqrt(Ln * 64.0)
        bc = np.zeros((128, 128)); bs = np.zeros((128, 128))
        for gq in range(2):
            bc[gq * 64:(gq + 1) * 64, gq * 64:(gq + 1) * 64] = d64[0] * nrm
            bs[gq * 64:(gq + 1) * 64, gq * 64:(gq + 1) * 64] = -d64[1] * nrm
        blocks += [bc, bs]
    c["c_bcs"] = np.stack(blocks).astype(np.float32).astype(bf)
    return c


_NC_CACHE = {}


def kernel(x_prompt, x_sample, cache_k, cache_v, c, c_ctx, w_mod, b_mod, w_in, b_in, sink, w_attn_o, conv_w, conv_b,
           conv_ln_g, conv_ln_b, w_conv_o, w_fnet, b_fnet, w_o, ln1_g, ln1_b, w_up, ffn_conv_w, ffn_conv_b, w_down,
           ln2_g, ln2_b, _debug=None):
    f = lambda a: np.ascontiguousarray(np.asarray(a, dtype=np.float32))
    _stop = None
    _segs = "PS"
    _ncores = 8
    if _debug:
        _debug = dict(_debug)
        _stop = _debug.pop("_stop", None)
        _segs = _debug.pop("_segs", "PS")
        _ncores = _debug.pop("_ncores", 8)
    key = (str(sorted(_debug.items())) if _debug else None, _stop, _segs)
    if key not in _NC_CACHE:
        _NC_CACHE[key] = build(_debug, _stop, _segs)
    nc = _NC_CACHE[key]
    consts = host_consts()
    shared = dict(w_mod=f(w_mod), b_mod=f(b_mod), w_in=f(w_in), b_in=f(b_in), sink=f(sink), w_attn_o=f(w_attn_o),
                  conv_w=f(conv_w), conv_b=f(conv_b), conv_ln_g=f(conv_ln_g), conv_ln_b=f(conv_ln_b), w_conv_o=f(w_conv_o),
                  w_fnet=f(w_fnet), b_fnet=f(b_fnet), w_o=f(w_o), ln1_g=f(ln1_g), ln1_b=f(ln1_b), w_up=f(w_up),
                  ffn_conv_w=f(ffn_conv_w), ffn_conv_b=f(ffn_conv_b), w_down=f(w_down), ln2_g=f(ln2_g), ln2_b=f(ln2_b))
    shared.update(consts)
    xp = f(x_prompt); xs = f(x_sample); ck_ = f(cache_k); cv_ = f(cache_v); cc = f(c); cx = f(c_ctx)
    in_maps = []
    for b in range(_ncores):
        m = dict(shared)
        m["xp"] = np.ascontiguousarray(xp[4 * b:4 * b + 4].reshape(1024, D))
        m["xs"] = np.ascontiguousarray(xs[b])
        m["ck"] = np.ascontiguousarray(ck_[b].reshape(DEPTH, 256, 128))
        m["cv"] = np.ascontiguousarray(cv_[b].reshape(DEPTH, 256, 128))
        m["cvec"] = np.ascontiguousarray(np.stack([cx, cc[b]]))
        in_maps.append(m)
    res = run_bass_kernel_spmd(nc, in_maps, core_ids=list(range(_ncores)))
    R = res.results
    if _debug is not None:
        kernel._dbg = [{k: v for k, v in r.items()} for r in R]
        return None
    y_p = np.concatenate([r["yp"].reshape(4, 256, D) for r in R], 0)
    y_s = np.stack([r["ys"] for r in R], 0)
    nk_ = np.concatenate([r["nk"].reshape(4, DEPTH, 256, 2, 64) for r in R], 0)
    nv_ = np.concatenate([r["nv"].reshape(4, DEPTH, 256, 2, 64) for r in R], 0)
    if _debug:
        kernel._dbg = [{k: v for k, v in r.items() if k.startswith("dbg_")} for r in R]
    return (y_p.astype(np.float32), y_s.astype(np.float32), nk_.astype(np.float32), nv_.astype(np.float32))
```
